# Optimizing a Trainium2 kernel written in Bass

```python
import math
import jax, jax.numpy as jnp
from jax import lax
import numpy as np

D_MODEL = 4096
BATCH = 1
SEQ = 8192
DEPTH = 2

CTX_LEN = 256
GRID_W = 64
EPS = 1e-6
ROPE_BASE = 10000.0

DA_HEAD_DIM = 128
DA_HEADS = D_MODEL // 512
DA_V_DIM = 2 * DA_HEAD_DIM
DA_WIDTH = DA_HEADS * DA_V_DIM
Q_BLOCK = 128

FT_GROUPS = 4
FT_GROUP_DIM = D_MODEL // (4 * FT_GROUPS)
FT_WIDTH = FT_GROUPS * FT_GROUP_DIM

GLA_HEADS = 4
GLA_V_DIM = D_MODEL // (4 * GLA_HEADS)
GLA_K_DIM = GLA_V_DIM // 2
GLA_WIDTH = GLA_HEADS * GLA_V_DIM
GLA_GATE_RANK = 16
GLA_TAU = 16.0
GLA_CHUNK = 64

MIX_WIDTH = DA_WIDTH + FT_WIDTH + GLA_WIDTH
IN_SIZES = (
    DA_HEADS * 2 * DA_HEAD_DIM,
    DA_HEADS * 2 * DA_HEAD_DIM,
    DA_WIDTH,
    FT_WIDTH,
    GLA_HEADS * GLA_K_DIM,
    GLA_HEADS * GLA_K_DIM,
    GLA_WIDTH,
    2 * GLA_GATE_RANK,
    GLA_WIDTH,
)
IN_WIDTH = sum(IN_SIZES)
D_FF = 256 * ((8 * D_MODEL // 3 + 255) // 256)
N_MOD = 6

kernel_name = 'hymba_style_diffattn_fnet_gla_dit'


def rms_norm(x, g):
    xf = x.astype(jnp.float32)
    y = xf * lax.rsqrt(jnp.mean(xf * xf, axis=-1, keepdims=True) + EPS)
    return (y * g.astype(jnp.float32)).astype(x.dtype)


def modulate(x, g, shift, scale):
    return rms_norm(x, g) * (1 + scale) + shift


def split_columns(p):
    out, start = [], 0
    for size in IN_SIZES:
        out.append(p[..., start:start + size])
        start += size
    return out


def split_heads(z, n):
    b, t, _ = z.shape
    return z.reshape(b, t, n, -1).transpose(0, 2, 1, 3)


def merge_heads(z):
    b, h, t, d = z.shape
    return z.transpose(0, 2, 1, 3).reshape(b, t, h * d)


def da_heads_qk(z):
    b, t, _ = z.shape
    return z.reshape(b, t, DA_HEADS, 2, DA_HEAD_DIM).transpose(0, 2, 3, 1, 4)


def axial_rope_tables(row, col):
    half = DA_HEAD_DIM // 2
    freqs = ROPE_BASE ** (-jnp.arange(0, half, 2, dtype=jnp.float32) / half)
    ar = row.astype(jnp.float32)[:, None] * freqs
    ac = col.astype(jnp.float32)[:, None] * freqs
    ang = jnp.concatenate([ar, ar, ac, ac], axis=-1)
    return jnp.cos(ang), jnp.sin(ang)


def apply_axial_rope(x, cos, sin):
    x1, x2, x3, x4 = jnp.split(x, 4, axis=-1)
    rot = jnp.concatenate([-x2, x1, -x4, x3], axis=-1)
    return (x.astype(jnp.float32) * cos + rot.astype(jnp.float32) * sin).astype(x.dtype)


def diff_weighted_values(q, k, v, lam):
    s = jnp.einsum('bhmqd,bhmkd->bhmqk', q, k).astype(jnp.float32) * (DA_HEAD_DIM ** -0.5)
    p = jax.nn.softmax(s, axis=-1)
    w = p[:, :, 0] - lam * p[:, :, 1]
    return jnp.einsum('bhqk,bhke->bhqe', w.astype(v.dtype), v)


def diff_attention_latent(q, k_all, v_all, lam):
    b, h, m, t, d = q.shape
    nb = t // Q_BLOCK
    qb = q.reshape(b, h, m, nb, Q_BLOCK, d).transpose(3, 0, 1, 2, 4, 5)
    ob = lax.map(lambda blk: diff_weighted_values(blk, k_all, v_all, lam), qb)
    return ob.transpose(1, 2, 0, 3, 4).reshape(b, h, t, DA_V_DIM)


def da_post(o, g, lam_init):
    return merge_heads(rms_norm(o, g) * (1.0 - lam_init))


def fourier_mix(u):
    b, t, _ = u.shape
    z = u.astype(jnp.float32).reshape(b, t, FT_GROUPS, FT_GROUP_DIM)
    f = jnp.fft.fft2(z, axes=(1, 3), norm='ortho').real
    return f.reshape(b, t, FT_WIDTH).astype(u.dtype)


def gla_prep(q, k, v, gd, w2, b2):
    qh = split_heads(q, GLA_HEADS).astype(jnp.float32) * (GLA_K_DIM ** -0.5)
    kh = split_heads(k, GLA_HEADS).astype(jnp.float32)
    vh = split_heads(v, GLA_HEADS).astype(jnp.float32)
    log_a = []
    for i, g in enumerate(jnp.split(gd, 2, axis=-1)):
        logits = (g @ w2[i] + b2[i]).astype(jnp.float32)
        log_a.append(split_heads(jax.nn.log_sigmoid(logits) / GLA_TAU, GLA_HEADS))
    return qh, kh, vh, log_a


def gla_chunked(q, k, v, log_a, s0):
    b, h, t, dk = q.shape
    n = t // GLA_CHUNK
    to_chunks = lambda z: z.reshape(b, h, n, GLA_CHUNK, z.shape[-1])
    q, k, v, log_a = to_chunks(q), to_chunks(k), to_chunks(v), to_chunks(log_a)
    cum = jnp.cumsum(log_a, axis=3)
    cum_last = cum[:, :, :, -1:]
    q_in = q * jnp.exp(cum)
    k_in = k * jnp.exp(-cum)
    mask = jnp.tril(jnp.ones((GLA_CHUNK, GLA_CHUNK), dtype=bool))
    a = jnp.where(mask, jnp.einsum('bhnid,bhnjd->bhnij', q_in, k_in), 0.0)
    o_intra = jnp.einsum('bhnij,bhnje->bhnie', a, v)
    kv = jnp.einsum('bhncd,bhnce->bhnde', k * jnp.exp(cum_last - cum), v)
    decay = jnp.exp(cum_last[:, :, :, 0])

    def step(s, inp):
        dec, kv_n = inp
        return dec[..., None] * s + kv_n, s

    s_final, s_before = lax.scan(step, s0, (jnp.moveaxis(decay, 2, 0), jnp.moveaxis(kv, 2, 0)))
    s_before = jnp.moveaxis(s_before, 0, 2)
    o_inter = jnp.einsum('bhncd,bhnde->bhnce', q_in, s_before)
    return (o_intra + o_inter).reshape(b, h, t, -1), s_final


def gla_scan(q, k, v, log_a, s0, reverse):
    if reverse:
        q, k, v, log_a = (jnp.flip(z, axis=2) for z in (q, k, v, log_a))
    o, s = gla_chunked(q, k, v, log_a, s0)
    if reverse:
        o = jnp.flip(o, axis=2)
    return o, s


def gla_output(o, r, g):
    y = rms_norm(o, g) * jax.nn.silu(split_heads(r, GLA_HEADS).astype(jnp.float32))
    return merge_heads(y).astype(r.dtype)


def conv_ffn(h, w_up, conv_w, conv_b, w_down):
    u = h @ w_up
    up = jnp.pad(u, ((0, 0), (1, 1), (0, 0)))
    u = up[:, :-2] * conv_w[0] + up[:, 1:-1] * conv_w[1] + up[:, 2:] * conv_w[2] + conv_b
    gate, val = jnp.split(u, 2, axis=-1)
    return (jax.nn.silu(gate) * val) @ w_down


def setup_inputs(seed: int = 0) -> dict:
    key = jax.random.key(seed)
    ks = jax.random.split(key, 24)
    f32 = jnp.float32

    def nrm(k, shape, scale=1.0):
        return jax.random.normal(k, shape, f32) * scale

    def gain(k, shape):
        return 1.0 + 0.02 * jax.random.normal(k, shape, f32)

    return {
        'x': nrm(ks[0], (BATCH, SEQ, D_MODEL)),
        'c': nrm(ks[1], (BATCH, D_MODEL)),
        'ctx': nrm(ks[2], (BATCH, CTX_LEN, D_MODEL)),
        'c_ctx': nrm(ks[3], (D_MODEL,)),
        'w_ada': nrm(ks[4], (DEPTH, D_MODEL, N_MOD * D_MODEL), D_MODEL ** -0.5),
        'b_ada': nrm(ks[5], (DEPTH, N_MOD * D_MODEL), 0.01),
        'norm1_g': gain(ks[6], (DEPTH, D_MODEL)),
        'norm2_g': gain(ks[7], (DEPTH, D_MODEL)),
        'w_in': nrm(ks[8], (DEPTH, D_MODEL, IN_WIDTH), D_MODEL ** -0.5),
        'q_norm_g': gain(ks[9], (DEPTH, DA_HEAD_DIM)),
        'k_norm_g': gain(ks[10], (DEPTH, DA_HEAD_DIM)),
        'lambda_q1': nrm(ks[11], (DEPTH, DA_HEAD_DIM), 0.1),
        'lambda_k1': nrm(ks[12], (DEPTH, DA_HEAD_DIM), 0.1),
        'lambda_q2': nrm(ks[13], (DEPTH, DA_HEAD_DIM), 0.1),
        'lambda_k2': nrm(ks[14], (DEPTH, DA_HEAD_DIM), 0.1),
        'da_subln_g': gain(ks[15], (DEPTH, DA_V_DIM)),
        'gla_gate_w2': nrm(ks[16], (DEPTH, 2, GLA_GATE_RANK, GLA_HEADS * GLA_K_DIM), GLA_GATE_RANK ** -0.5),
        'gla_gate_b': nrm(ks[17], (DEPTH, 2, GLA_HEADS * GLA_K_DIM), 0.01),
        'gla_norm_g': gain(ks[18], (DEPTH, GLA_V_DIM)),
        'w_out': nrm(ks[19], (DEPTH, MIX_WIDTH, D_MODEL), MIX_WIDTH ** -0.5),
        'w_up': nrm(ks[20], (DEPTH, D_MODEL, 2 * D_FF), D_MODEL ** -0.5),
        'conv_w': nrm(ks[21], (DEPTH, 3, 2 * D_FF), 3 ** -0.5),
        'conv_b': nrm(ks[22], (DEPTH, 2 * D_FF), 0.01),
        'w_down': nrm(ks[23], (DEPTH, D_FF, D_MODEL), D_FF ** -0.5),
    }


def reference(x, c, ctx, c_ctx, w_ada, b_ada, norm1_g, norm2_g, w_in, q_norm_g, k_norm_g,
              lambda_q1, lambda_k1, lambda_q2, lambda_k2, da_subln_g, gla_gate_w2, gla_gate_b,
              gla_norm_g, w_out, w_up, conv_w, conv_b, w_down):
    rows = x.shape[1] // GRID_W
    row = jnp.repeat(jnp.arange(rows), GRID_W)
    col = jnp.tile(jnp.arange(GRID_W), rows)
    cos, sin = axial_rope_tables(row, col)
    bsz = x.shape[0]
    s_zero = jnp.zeros((bsz, GLA_HEADS, GLA_K_DIM, GLA_V_DIM), jnp.float32)
    xc = ctx

    for l in range(DEPTH):
        need_ctx = l < DEPTH - 1
        sh1_l, sc1_l, g1_l, sh2_l, sc2_l, g2_l = jnp.split(
            (jax.nn.silu(c) @ w_ada[l] + b_ada[l])[:, None, :], N_MOD, axis=-1)
        sh1_c, sc1_c, g1_c, sh2_c, sc2_c, g2_c = jnp.split(
            jax.nn.silu(c_ctx) @ w_ada[l] + b_ada[l], N_MOD, axis=-1)

        p_l = split_columns(modulate(x, norm1_g[l], sh1_l, sc1_l) @ w_in[l])
        p_c = split_columns(modulate(xc, norm1_g[l], sh1_c, sc1_c) @ w_in[l])

        lam_init = 0.8 - 0.6 * math.exp(-0.3 * l)
        lam = (jnp.exp(jnp.sum(lambda_q1[l] * lambda_k1[l])) - jnp.exp(jnp.sum(lambda_q2[l] * lambda_k2[l]))
               + lam_init).astype(jnp.float32)
        dq_l = apply_axial_rope(rms_norm(da_heads_qk(p_l[0]), q_norm_g[l]), cos, sin)
        dk_l = apply_axial_rope(rms_norm(da_heads_qk(p_l[1]), k_norm_g[l]), cos, sin)
        dv_l = split_heads(p_l[2], DA_HEADS)
        dq_c = rms_norm(da_heads_qk(p_c[0]), q_norm_g[l])
        dk_c = rms_norm(da_heads_qk(p_c[1]), k_norm_g[l])
        dv_c = split_heads(p_c[2], DA_HEADS)
        k_all = jnp.concatenate([dk_c, dk_l], axis=3)
        v_all = jnp.concatenate([dv_c, dv_l], axis=2)
        da_l = da_post(diff_attention_latent(dq_l, k_all, v_all, lam), da_subln_g[l], lam_init)

        ft_l = fourier_mix(p_l[3])

        gq_l, gk_l, gv_l, la_l = gla_prep(p_l[4], p_l[5], p_l[6], p_l[7], gla_gate_w2[l], gla_gate_b[l])
        gq_c, gk_c, gv_c, la_c = gla_prep(p_c[4], p_c[5], p_c[6], p_c[7], gla_gate_w2[l], gla_gate_b[l])
        oc_f, sc_f = gla_scan(gq_c, gk_c, gv_c, la_c[0], s_zero, False)
        oc_b, sc_b = gla_scan(gq_c, gk_c, gv_c, la_c[1], s_zero, True)
        ol_f, _ = gla_scan(gq_l, gk_l, gv_l, la_l[0], sc_f, False)
        ol_b, _ = gla_scan(gq_l, gk_l, gv_l, la_l[1], sc_b, True)
        gla_l = gla_output(ol_f + ol_b, p_l[8], gla_norm_g[l])

        y_l = jnp.concatenate([da_l, ft_l, gla_l], axis=-1) @ w_out[l]
        x_new = x + g1_l * y_l
        x_new = x_new + g2_l * conv_ffn(modulate(x_new, norm2_g[l], sh2_l, sc2_l),
                                        w_up[l], conv_w[l], conv_b[l], w_down[l])

        if need_ctx:
            da_c = da_post(diff_weighted_values(dq_c, dk_c, dv_c, lam), da_subln_g[l], lam_init)
            ft_c = fourier_mix(p_c[3])
            gla_c = gla_output(oc_f + oc_b, p_c[8], gla_norm_g[l])
            y_c = jnp.concatenate([da_c, ft_c, gla_c], axis=-1) @ w_out[l]
            xc = xc + g1_c * y_c
            xc = xc + g2_c * conv_ffn(modulate(xc, norm2_g[l], sh2_c, sc2_c),
                                      w_up[l], conv_w[l], conv_b[l], w_down[l])
        x = x_new

    return x
```

```python
import numpy as np
import ml_dtypes
import concourse.bass as bass
import concourse.mybir as mybir
from concourse.bass_utils import run_bass_kernel_spmd

F32 = mybir.dt.float32
BF16 = mybir.dt.bfloat16
ALU = mybir.AluOpType
AF = mybir.ActivationFunctionType

NCORES = 8
D = 4096
SEQ = 8192
CTX = 256
DEPTH = 2
KC = D // 128
TL = SEQ // NCORES
TC = CTX // NCORES
T1 = TC + TL
IN_W = 10272
DFF = 11008
EPS = 1e-6


class Buf:
    __slots__ = ("name", "w", "r")

    def __init__(self, name=""):
        self.name = name
        self.w = None
        self.r = {}


class Eng:
    def __init__(self, name):
        self.name = name
        self.prog = []
        self.count = 0
        self.seen = {}
        self.sem = None


class K:
    def __init__(self, nc, n_dma_sems=10):
        self.nc = nc
        self.pe = Eng("tensor")
        self.act = Eng("scalar")
        self.dve = Eng("vector")
        self.pool = Eng("gpsimd")
        self.sp = Eng("sync")
        self.engs = [self.pe, self.act, self.dve, self.pool, self.sp]
        for e in self.engs:
            e.sem = nc.alloc_semaphore(name="c_" + e.name)
        self.dma_pool = {}
        for e in (self.sp, self.pool, self.act):
            lst = []
            for i in range(n_dma_sems):
                lst.append([nc.alloc_semaphore(name=f"d_{e.name}{i}"), 0])
            self.dma_pool[e.name] = [lst, 0]
        self.out_events = []
        self._n = 0

    def sb(self, shape, dt, name=None):
        self._n += 1
        return self.nc.alloc_sbuf_tensor(name or f"sb{self._n}", list(shape), dt).ap()

    def ps(self, shape, dt=F32, name=None):
        self._n += 1
        return self.nc.alloc_psum_tensor(name or f"ps{self._n}", list(shape), dt).ap()

    def _wait(self, E, ev):
        if ev is None:
            return
        sem, val = ev
        k = id(sem)
        if E.seen.get(k, 0) >= val:
            return
        E.seen[k] = val
        E.prog.append(lambda eng, sem=sem, val=val: eng.wait_ge(sem, val))

    def _deps(self, E, reads, writes):
        for b in reads:
            self._wait(E, b.w)
        for b in writes:
            self._wait(E, b.w)
            for ev in list(b.r.values()):
                self._wait(E, ev)

    def _commit(self, ev, reads, writes):
        for b in reads:
            b.r[id(ev[0])] = ev
        for b in writes:
            b.w = ev
            b.r = {}

    def op(self, E, fn, reads=(), writes=()):
        self._deps(E, reads, writes)
        E.count += 1
        cnt = E.count
        sem = E.sem
        E.prog.append(lambda eng, fn=fn, sem=sem: fn(eng).then_inc(sem, 1))
        ev = (sem, cnt)
        if E is self.pe:
            E.seen[id(sem)] = cnt
        self._commit(ev, reads, writes)
        return ev

    def dma(self, E, out_ap, in_ap, reads=(), writes=(), is_output=False, **kw):
        self._deps(E, reads, writes)
        pool, idx = self.dma_pool[E.name]
        slot = pool[idx % len(pool)]
        self.dma_pool[E.name][1] = idx + 1
        sem, uses = slot
        if uses > 0:
            self._wait(E, (sem, 16 * uses))
        slot[1] = uses + 1
        val = 16 * (uses + 1)
        E.prog.append(lambda eng, o=out_ap, i=in_ap, sem=sem, kw=kw:
                      eng.dma_start(out=o, in_=i, **kw).then_inc(sem, 16))
        ev = (sem, val)
        self._commit(ev, reads, writes)
        if is_output:
            self.out_events.append(ev)
        return ev

    def finish(self):
        for ev in self.out_events:
            self._wait(self.sp, ev)
        with self.nc.Block() as block:
            for E in self.engs:
                if not E.prog:
                    continue

                def body(eng, E=E):
                    for c in E.prog:
                        c(eng)
                getattr(block, E.name)(body)


class Ring:
    def __init__(self, items):
        self.items = items
        self.i = 0

    def next(self):
        it = self.items[self.i % len(self.items)]
        self.i += 1
        return it


def blocks_of(total, nblk):
    base = -(-total // nblk)
    out, s = [], 0
    while s < total:
        w = min(base, total - s)
        out.append((s, w))
        s += w
    return out


def split_ranges(blk, ranges):
    s, w = blk
    res = []
    for (rs, re_, tag) in ranges:
        a, b = max(s, rs), min(s + w, re_)
        if a < b:
            res.append((a, b, tag))
    return res


MODW = 6 * D // NCORES


def build_p0():
    nc = bass.Bass("TRN2", target_bir_lowering=False)
    cs = nc.dram_tensor("cs", [128, KC, 2], F32, kind="ExternalInput").ap()
    wa = nc.dram_tensor("wa", [DEPTH, D, MODW], F32, kind="ExternalInput").ap()
    ba = nc.dram_tensor("ba", [DEPTH, MODW], F32, kind="ExternalInput").ap()
    out = nc.dram_tensor("mods", [DEPTH, 2, MODW], F32, kind="ExternalOutput").ap()
    k = K(nc)
    cs_sb = k.sb([128, KC, 2], F32)
    css = k.sb([128, KC, 2], F32)
    Bcs, Bcss = Buf(), Buf()
    k.dma(k.sp, cs_sb, cs, writes=[Bcs])
    k.op(k.act, lambda e: e.activation(out=css, in_=cs_sb, func=AF.Silu), reads=[Bcs], writes=[Bcss])
    wring = Ring([(k.sb([128, 4, 512], F32), Buf()) for _ in range(4)])
    pring = Ring([(k.ps([128, 512]), Buf()) for _ in range(2)])
    bring = Ring([(k.sb([2, 512], F32), Buf()) for _ in range(2)])
    oring = Ring([(k.sb([2, 512], F32), Buf()) for _ in range(2)])
    nq = 0
    for l in range(DEPTH):
        for cb in range(MODW // 512):
            c0 = cb * 512
            ps, Bp = pring.next()
            for kg in range(KC // 4):
                wt, Bw = wring.next()
                q = k.sp if nq % 2 == 0 else k.act
                nq += 1
                k.dma(q, wt, wa[l, kg * 512:(kg + 1) * 512, c0:c0 + 512].rearrange("(kc p) n -> p kc n", p=128),
                      writes=[Bw])
                for j in range(4):
                    kc = kg * 4 + j
                    k.op(k.pe, lambda e, ps=ps, wt=wt, j=j, kc=kc: e.matmul(
                        ps[0:2, :], lhsT=css[:, kc, :], rhs=wt[:, j, :], start=(kc == 0), stop=(kc == KC - 1)),
                        reads=[Bcss, Bw], writes=[Bp])
            bt, Bb = bring.next()
            for s in range(2):
                k.dma(k.sp, bt[s:s + 1, :], ba[l:l + 1, c0:c0 + 512], writes=[Bb])
            ot, Bo = oring.next()
            k.op(k.dve, lambda e, ot=ot, ps=ps, bt=bt: e.tensor_tensor(out=ot, in0=ps[0:2, :], in1=bt, op=ALU.add),
                 reads=[Bp, Bb], writes=[Bo])
            k.dma(k.sp, out[l, :, c0:c0 + 512], ot, reads=[Bo], is_output=True)
    k.finish()
    return nc


def run_p0(inp):
    cs = np.stack([inp["c"][0], inp["c_ctx"]], axis=-1)
    cs = np.ascontiguousarray(cs.reshape(KC, 128, 2).transpose(1, 0, 2))
    nc = build_p0()
    in_maps = []
    for i in range(NCORES):
        sl = slice(i * MODW, (i + 1) * MODW)
        in_maps.append({"cs": cs,
                        "wa": np.ascontiguousarray(inp["w_ada"][:, :, sl]),
                        "ba": np.ascontiguousarray(inp["b_ada"][:, sl])})
    res = run_bass_kernel_spmd(nc, in_maps, core_ids=list(range(NCORES)))
    mods = np.concatenate([r["mods"] for r in res.results], axis=-1)
    return mods


def mods_layout(mods_l):
    return np.ascontiguousarray(mods_l.reshape(2, 6, KC, 128).transpose(3, 0, 1, 2))


def vec_layout(v):
    return np.ascontiguousarray(v.reshape(-1, 128).T)


BLK1 = blocks_of(T1, 3)
RNG1 = [(0, TC, 1), (TC, T1, 0)]
FM_GROUPS = [(0, 4096, "qk"), (6144, 1024, "z"), (7168, 1024, "gqk"), (9216, 32, "gd")]
TM_GROUPS = [(4096, 2048, "v"), (7680, 512, "gk"), (8192, 1024, "gv"), (9248, 1024, "r")]


def build_p1():
    nc = bass.Bass("TRN2", target_bir_lowering=False)
    dt = nc.dram_tensor
    xT = dt("xT", [D, T1], F32, kind="ExternalInput").ap()
    w = dt("w", [D, IN_W], F32, kind="ExternalInput").ap()
    modsT = dt("modsT", [128, 2, 6, KC], F32, kind="ExternalInput").ap()
    g1 = dt("g1", [128, KC], F32, kind="ExternalInput").ap()
    qkg = dt("qkg", [128, 2], F32, kind="ExternalInput").ap()
    cosT = dt("cosT", [128, T1], F32, kind="ExternalInput").ap()
    sinT = dt("sinT", [128, T1], F32, kind="ExternalInput").ap()
    onesd = dt("ones", [128, 128], F32, kind="ExternalInput").ap()
    rotd = dt("rot", [128, 128], F32, kind="ExternalInput").ap()
    o_qk = dt("o_qk", [32, 128, T1], BF16, kind="ExternalOutput").ap()
    o_z = dt("o_z", [8, 128, T1], BF16, kind="ExternalOutput").ap()
    o_gqk = dt("o_gqk", [8, 128, T1], F32, kind="ExternalOutput").ap()
    o_gd = dt("o_gd", [32, T1], F32, kind="ExternalOutput").ap()
    o_v = dt("o_v", [T1, 2048], BF16, kind="ExternalOutput").ap()
    o_gk = dt("o_gk", [T1, 512], F32, kind="ExternalOutput").ap()
    o_gv = dt("o_gv", [T1, 1024], BF16, kind="ExternalOutput").ap()
    o_r = dt("o_r", [T1, 1024], F32, kind="ExternalOutput").ap()

    k = K(nc)
    mods = k.sb([128, 2, 6, KC], F32); Bmods = Buf()
    g1s = k.sb([128, KC], F32); Bg1 = Buf()
    qkgs = k.sb([128, 2], F32); Bqkg = Buf()
    cosS = k.sb([128, T1], F32); Bcos = Buf()
    sinS = k.sb([128, T1], F32); Bsin = Buf()
    ones = k.sb([128, 128], F32); Bones = Buf()
    rot = k.sb([128, 128], F32); Brot = Buf()
    for (d_, s_, b_) in [(mods, modsT, Bmods), (g1s, g1, Bg1), (qkgs, qkg, Bqkg), (cosS, cosT, Bcos),
                         (sinS, sinT, Bsin), (ones, onesd, Bones), (rot, rotd, Brot)]:
        k.dma(k.sp, d_, s_, writes=[b_])
    a1 = k.sb([128, 2, KC], F32); Ba1 = Buf()
    for s in range(2):
        k.op(k.dve, lambda e, s=s: e.scalar_tensor_tensor(out=a1[:, s, :], in0=mods[:, s, 1, :], scalar=1.0, in1=g1s,
                                                          op0=ALU.add, op1=ALU.mult),
             reads=[Bmods, Bg1], writes=[Ba1])

    psr = Ring([(k.ps([128, 512]), Buf()) for _ in range(6)])
    ps_ss = (k.ps([128, 512]), Buf())
    ps_rot = (k.ps([128, 512]), Buf())

    xr = Ring([(k.sb([128, T1], F32), Buf()) for _ in range(2)])
    sqr = Ring([(k.sb([128, T1], F32), Buf()) for _ in range(2)])
    ssb = [psr.next() for _ in range(3)]
    for t in range(KC):
        xt, Bx = xr.next()
        k.dma(k.sp, xt, xT[t * 128:(t + 1) * 128, :], writes=[Bx])
        sq, Bsq = sqr.next()
        k.op(k.act, lambda e, sq=sq, xt=xt: e.activation(out=sq, in_=xt, func=AF.Square), reads=[Bx], writes=[Bsq])
        for bi, (s0, wd) in enumerate(BLK1):
            k.op(k.pe, lambda e, bi=bi, s0=s0, wd=wd, sq=sq, t=t: e.matmul(
                ssb[bi][0][:, 0:wd], lhsT=ones, rhs=sq[:, s0:s0 + wd], start=(t == 0), stop=(t == KC - 1)),
                reads=[Bones, Bsq], writes=[ssb[bi][1]])
    rstd = k.sb([128, T1], F32); Brstd = Buf()
    epsb = k.sb([128, 1], F32); Beps = Buf()
    k.op(k.dve, lambda e: e.memset(epsb, EPS), writes=[Beps])
    for bi, (s0, wd) in enumerate(BLK1):
        k.op(k.act, lambda e, bi=bi, s0=s0, wd=wd: e.activation(out=rstd[:, s0:s0 + wd], in_=ssb[bi][0][:, 0:wd],
                                                                func=AF.Sqrt, bias=epsb, scale=1.0 / D),
             reads=[ssb[bi][1], Beps], writes=[Brstd])
    k.op(k.dve, lambda e: e.reciprocal(out=rstd, in_=rstd), reads=[Brstd], writes=[Brstd])

    hT = k.sb([128, KC, T1], BF16)
    Bh = [Buf() for _ in range(KC)]
    for t in range(KC):
        xt, Bx = xr.next()
        k.dma(k.sp, xt, xT[t * 128:(t + 1) * 128, :], writes=[Bx])
        tmp, Bt = sqr.next()
        k.op(k.dve, lambda e, tmp=tmp, xt=xt: e.tensor_tensor(out=tmp, in0=xt, in1=rstd, op=ALU.mult),
             reads=[Bx, Brstd], writes=[Bt])
        for (a, b, s) in RNG1:
            k.op(k.dve, lambda e, a=a, b=b, s=s, t=t, tmp=tmp: e.tensor_scalar(
                out=hT[:, t, a:b], in0=tmp[:, a:b], scalar1=a1[:, s, t:t + 1], scalar2=mods[:, s, 0, t:t + 1],
                op0=ALU.mult, op1=ALU.add), reads=[Bt, Ba1, Bmods], writes=[Bh[t]])

    wr = Ring([(k.sb([128, KC, 512], BF16), Buf()) for _ in range(2)])

    panel_list = []
    for (g0, gw, kind) in FM_GROUPS + TM_GROUPS:
        for c0 in range(g0, g0 + gw, 512):
            panel_list.append((c0, min(512, g0 + gw - c0)))
    loaded = {}
    nxt = [0]

    def prefetch():
        if nxt[0] < len(panel_list):
            c0, pw = panel_list[nxt[0]]
            nxt[0] += 1
            wt, Bw = wr.next()
            k.dma(k.pool, wt[:, :, 0:pw], w[:, c0:c0 + pw].rearrange("(kc p) n -> p kc n", p=128), writes=[Bw])
            loaded[c0] = (wt, Bw)

    def load_panel(c0, pw):
        if c0 not in loaded:
            prefetch()
        res = loaded.pop(c0)
        prefetch()
        return res

    t_sq = Ring([(k.sb([128, 352], F32), Buf()) for _ in range(3)])
    t_sd = Ring([(k.sb([128, 352], F32), Buf()) for _ in range(2)])
    t_qn = Ring([(k.sb([128, 352], F32), Buf()) for _ in range(3)])
    t_a = Ring([(k.sb([128, 352], F32), Buf()) for _ in range(2)])
    t_b = Ring([(k.sb([128, 352], F32), Buf()) for _ in range(2)])
    o16 = Ring([(k.sb([128, T1], BF16), Buf()) for _ in range(2)])
    o32 = Ring([(k.sb([128, T1], F32), Buf()) for _ in range(2)])

    tile_idx = {"qk": 0, "z": 0, "gqk": 0}
    pending = {}

    def add_hook(kc, fn):
        pending.setdefault(kc, []).append(fn)

    def qk_epilogue(ti, banks):
        gi = 0 if ti < 16 else 1
        ot, Bo = o16.next()
        st = {}
        for bi, (s0, wd) in enumerate(BLK1):
            pb, Bp = banks[bi]
            sq, Bsq = t_sq.next()
            k.op(k.act, lambda e, sq=sq, pb=pb, wd=wd: e.activation(out=sq[:, 0:wd], in_=pb[:, 0:wd], func=AF.Square),
                 reads=[Bp], writes=[Bsq])
            st[bi] = (sq, Bsq)

            def stage1(bi=bi, s0=s0, wd=wd, pb=pb, Bp=Bp):
                sq, Bsq = st[bi]
                k.op(k.pe, lambda e, sq=sq, wd=wd: e.matmul(ps_ss[0][:, 0:wd], lhsT=ones, rhs=sq[:, 0:wd], start=True, stop=True),
                     reads=[Bones, Bsq], writes=[ps_ss[1]])
                sd, Bsd = t_sd.next()
                k.op(k.act, lambda e, sd=sd, wd=wd: e.activation(out=sd[:, 0:wd], in_=ps_ss[0][:, 0:wd], func=AF.Sqrt,
                                                                 bias=epsb, scale=1.0 / 128),
                     reads=[ps_ss[1], Beps], writes=[Bsd])
                k.op(k.dve, lambda e, sd=sd, wd=wd: e.reciprocal(out=sd[:, 0:wd], in_=sd[:, 0:wd]), reads=[Bsd], writes=[Bsd])
                qn, Bqn = t_qn.next()
                k.op(k.dve, lambda e, qn=qn, pb=pb, sd=sd, wd=wd, gi=gi: e.scalar_tensor_tensor(
                    out=qn[:, 0:wd], in0=pb[:, 0:wd], scalar=qkgs[:, gi:gi + 1], in1=sd[:, 0:wd], op0=ALU.mult, op1=ALU.mult),
                    reads=[Bp, Bsd, Bqkg], writes=[Bqn])
                st[("qn", bi)] = (qn, Bqn)

            def stage2(bi=bi, s0=s0, wd=wd):
                qn, Bqn = st[("qn", bi)]
                k.op(k.pe, lambda e, qn=qn, wd=wd: e.matmul(ps_rot[0][:, 0:wd], lhsT=rot, rhs=qn[:, 0:wd], start=True, stop=True),
                     reads=[Brot, Bqn], writes=[ps_rot[1]])
                ta, Bta = t_a.next()
                k.op(k.pool, lambda e, ta=ta, qn=qn, s0=s0, wd=wd: e.tensor_tensor(out=ta[:, 0:wd], in0=qn[:, 0:wd], in1=cosS[:, s0:s0 + wd], op=ALU.mult),
                     reads=[Bqn, Bcos], writes=[Bta])
                tb, Btb = t_b.next()
                k.op(k.dve, lambda e, tb=tb, s0=s0, wd=wd: e.tensor_tensor(out=tb[:, 0:wd], in0=ps_rot[0][:, 0:wd], in1=sinS[:, s0:s0 + wd], op=ALU.mult),
                     reads=[ps_rot[1], Bsin], writes=[Btb])
                k.op(k.pool, lambda e, ta=ta, tb=tb, ot=ot, s0=s0, wd=wd: e.tensor_tensor(out=ot[:, s0:s0 + wd], in0=ta[:, 0:wd], in1=tb[:, 0:wd], op=ALU.add),
                     reads=[Bta, Btb], writes=[Bo])
                if bi == len(BLK1) - 1:
                    k.dma(k.sp, o_qk[ti], ot, reads=[Bo], is_output=True)
            add_hook(6 + 3 * bi, stage1)
            add_hook(16 + 4 * bi, stage2)

    def flush_hooks(hooks, upto=None):
        for kc in sorted(hooks):
            if upto is not None and kc != upto:
                continue
            for fn in hooks[kc]:
                fn()

    for (g0, gw, kind) in FM_GROUPS:
        for c0 in range(g0, g0 + gw, 512):
            pw = min(512, g0 + gw - c0)
            wt, Bw = load_panel(c0, pw)
            for m0 in range(0, pw, 128):
                mw = min(128, pw - m0)
                banks = [psr.next() for _ in BLK1]
                cur = pending
                pending = {}
                for kc in range(KC):
                    for bi, (s0, wd) in enumerate(BLK1):
                        k.op(k.pe, lambda e, bi=bi, s0=s0, wd=wd, kc=kc, m0=m0, mw=mw, wt=wt, banks=banks: e.matmul(
                            banks[bi][0][0:mw, 0:wd], lhsT=wt[:, kc, m0:m0 + mw], rhs=hT[:, kc, s0:s0 + wd],
                            start=(kc == 0), stop=(kc == KC - 1)), reads=[Bw, Bh[kc]], writes=[banks[bi][1]])
                    if kc in cur:
                        flush_hooks(cur, upto=kc)
                if kind == "qk":
                    ti = tile_idx["qk"]; tile_idx["qk"] += 1
                    qk_epilogue(ti, banks)
                else:
                    use16 = (kind == "z")
                    ot, Bo = (o16 if use16 else o32).next()
                    for bi, (s0, wd) in enumerate(BLK1):
                        pb, Bp = banks[bi]
                        eng = k.act if bi % 2 == 0 else k.dve
                        if eng is k.act:
                            k.op(eng, lambda e, ot=ot, pb=pb, s0=s0, wd=wd, mw=mw: e.activation(out=ot[0:mw, s0:s0 + wd], in_=pb[0:mw, 0:wd], func=AF.Copy),
                                 reads=[Bp], writes=[Bo])
                        else:
                            k.op(eng, lambda e, ot=ot, pb=pb, s0=s0, wd=wd, mw=mw: e.tensor_copy(out=ot[0:mw, s0:s0 + wd], in_=pb[0:mw, 0:wd]),
                                 reads=[Bp], writes=[Bo])
                    if kind == "gd":
                        k.dma(k.sp, o_gd, ot[0:32, :], reads=[Bo], is_output=True)
                    else:
                        ti = tile_idx[kind]; tile_idx[kind] += 1
                        k.dma(k.sp, (o_z if kind == "z" else o_gqk)[ti], ot, reads=[Bo], is_output=True)
    flush_hooks(pending)
    pending = {}

    e16 = Ring([(k.sb([128, 512], BF16), Buf()) for _ in range(3)])
    e32 = Ring([(k.sb([128, 512], F32), Buf()) for _ in range(3)])
    ttiles = [(s, min(128, T1 - s)) for s in range(0, T1, 128)]
    outs = {"v": (o_v, True), "gk": (o_gk, False), "gv": (o_gv, True), "r": (o_r, False)}
    ne = 0
    for (g0, gw, kind) in TM_GROUPS:
        od, is16 = outs[kind]
        for c0 in range(g0, g0 + gw, 512):
            wt, Bw = load_panel(c0, 512)
            for (s0, mt) in ttiles:
                pb, Bp = psr.next()
                for kc in range(KC):
                    k.op(k.pe, lambda e, pb=pb, kc=kc, s0=s0, mt=mt, wt=wt: e.matmul(
                        pb[0:mt, :], lhsT=hT[:, kc, s0:s0 + mt], rhs=wt[:, kc, :], start=(kc == 0), stop=(kc == KC - 1)),
                        reads=[Bw, Bh[kc]], writes=[Bp])
                et, Be = (e16 if is16 else e32).next()
                ne += 1
                if ne % 2 == 0:
                    k.op(k.act, lambda e, et=et, pb=pb, mt=mt: e.activation(out=et[0:mt, :], in_=pb[0:mt, :], func=AF.Copy), reads=[Bp], writes=[Be])
                else:
                    k.op(k.dve, lambda e, et=et, pb=pb, mt=mt: e.tensor_copy(out=et[0:mt, :], in_=pb[0:mt, :]), reads=[Bp], writes=[Be])
                k.dma(k.sp, od[s0:s0 + mt, c0 - g0:c0 - g0 + 512], et[0:mt, :], reads=[Be], is_output=True)
    k.finish()
    return nc


def rope_tables():
    half = 64
    freqs = (10000.0 ** (-np.arange(0, half, 2, dtype=np.float32) / half)).astype(np.float32)
    t = np.arange(SEQ)
    ar = (t // 64).astype(np.float32)[:, None] * freqs
    ac = (t % 64).astype(np.float32)[:, None] * freqs
    ang = np.concatenate([ar, ar, ac, ac], axis=-1)
    return np.cos(ang).astype(np.float32), np.sin(ang).astype(np.float32)


def rot_matrix():
    R = np.zeros((128, 128), np.float32)
    for j in range(32):
        R[32 + j, j] = -1.0
        R[j, 32 + j] = 1.0
        R[96 + j, 64 + j] = -1.0
        R[64 + j, 96 + j] = 1.0
    return R


def run_p1(xT_cores, w_in_l, modsT, g1n, qg, kg):
    cos, sin = rope_tables()
    nc = build_p1()
    ones = np.ones((128, 128), np.float32)
    rot = rot_matrix()
    g1 = vec_layout(g1n)
    qkg = np.ascontiguousarray(np.stack([qg, kg], axis=-1))
    in_maps = []
    for i in range(NCORES):
        cT = np.concatenate([np.ones((128, TC), np.float32), cos[i * TL:(i + 1) * TL].T], axis=1)
        sT = np.concatenate([np.zeros((128, TC), np.float32), sin[i * TL:(i + 1) * TL].T], axis=1)
        in_maps.append({"xT": xT_cores[i], "w": w_in_l, "modsT": modsT, "g1": g1, "qkg": qkg,
                        "cosT": np.ascontiguousarray(cT), "sinT": np.ascontiguousarray(sT), "ones": ones, "rot": rot})
    res = run_bass_kernel_spmd(nc, in_maps, core_ids=list(range(NCORES)))
    return res.results


NTOK = CTX + SEQ
NKT = NTOK // 128


def build_da():
    nc = bass.Bass("TRN2", target_bir_lowering=False)
    dt = nc.dram_tensor
    qTd = dt("qT", [2, 128, NTOK], BF16, kind="ExternalInput").ap()
    kTd = dt("kT", [2, 128, NTOK], BF16, kind="ExternalInput").ap()
    vd = dt("vaug", [NTOK, 257], BF16, kind="ExternalInput").ap()
    lamv = dt("lamv", [128, 4], F32, kind="ExternalInput").ap()
    lamc = dt("lamc", [128, 2], F32, kind="ExternalInput").ap()
    gsd = dt("gs", [128, 256], F32, kind="ExternalInput").ap()
    onesd = dt("ones", [128, 128], F32, kind="ExternalInput").ap()
    identd = dt("ident", [128, 128], F32, kind="ExternalInput").ap()
    outd = dt("daT", [2, 128, NTOK], BF16, kind="ExternalOutput").ap()
    k = K(nc)
    qT = k.sb([128, 2, NTOK], BF16); Bq = Buf()
    kT = k.sb([128, 2, NTOK], BF16); Bk = Buf()
    va = k.sb([128, NKT, 257], BF16); Bv = Buf()
    for m in range(2):
        k.dma(k.sp, qT[:, m, :], qTd[m], writes=[Bq])
        k.dma(k.sp, kT[:, m, :], kTd[m], writes=[Bk])
    for g in range(0, NKT, 11):
        k.dma(k.sp, va[:, g:g + 11, :], vd[g * 128:(g + 11) * 128, :].rearrange("(kt p) c -> p kt c", p=128), writes=[Bv])
    lv = k.sb([128, 4], F32); Blv = Buf()
    lc = k.sb([128, 2], F32); Blc = Buf()
    gs = k.sb([128, 256], F32); Bgs = Buf()
    ones = k.sb([128, 128], F32); Bones = Buf()
    ident = k.sb([128, 128], F32); Bid = Buf()
    for (d_, s_, b_) in [(lv, lamv, Blv), (lc, lamc, Blc), (gs, gsd, Bgs), (ones, onesd, Bones), (ident, identd, Bid)]:
        k.dma(k.sp, d_, s_, writes=[b_])
    epsb = k.sb([128, 1], F32); Beps = Buf()
    k.op(k.dve, lambda e: e.memset(epsb, EPS), writes=[Beps])
    pr = k.sb([128, 2], F32); Bpr = Buf()
    k.op(k.dve, lambda e: e.tensor_tensor(out=pr[:, 0:1], in0=lv[:, 0:1], in1=lv[:, 1:2], op=ALU.mult), reads=[Blv], writes=[Bpr])
    k.op(k.dve, lambda e: e.tensor_tensor(out=pr[:, 1:2], in0=lv[:, 2:3], in1=lv[:, 3:4], op=ALU.mult), reads=[Blv, Bpr], writes=[Bpr])
    ps_t = (k.ps([128, 512]), Buf())
    k.op(k.pe, lambda e: e.matmul(ps_t[0][:, 0:2], lhsT=ones, rhs=pr, start=True, stop=True), reads=[Bones, Bpr], writes=[ps_t[1]])
    ex = k.sb([128, 2], F32); Bex = Buf()
    k.op(k.act, lambda e: e.activation(out=ex, in_=ps_t[0][:, 0:2], func=AF.Exp), reads=[ps_t[1]], writes=[Bex])
    neglam = k.sb([128, 1], F32); Bnl = Buf()
    k.op(k.dve, lambda e: e.tensor_tensor(out=neglam, in0=ex[:, 1:2], in1=ex[:, 0:1], op=ALU.subtract), reads=[Bex], writes=[Bnl])
    k.op(k.dve, lambda e: e.tensor_tensor(out=neglam, in0=neglam, in1=lc[:, 0:1], op=ALU.subtract), reads=[Blc, Bnl], writes=[Bnl])
    gss = k.sb([128, 256], F32); Bgss = Buf()
    k.op(k.dve, lambda e: e.tensor_scalar(out=gss, in0=gs, scalar1=lc[:, 1:2], scalar2=None, op0=ALU.mult), reads=[Bgs, Blc], writes=[Bgss])

    ps_s = Ring([(k.ps([128, 512]), Buf()) for _ in range(3)])
    acc = [[(k.ps([128, 512]), Buf()) for _ in range(2)] for _ in range(2)]
    pT = Ring([(k.sb([128, 2, 256], BF16), Buf()) for _ in range(4)])
    sc = 128.0 ** -0.5
    r12 = Ring([(k.sb([128, 2], F32), Buf()) for _ in range(4)])
    o1r = Ring([(k.sb([128, 256], F32), Buf()) for _ in range(2)])
    orr = Ring([(k.sb([128, 256], F32), Buf()) for _ in range(2)])
    sqr = Ring([(k.sb([128, 256], F32), Buf()) for _ in range(2)])
    ssr = Ring([(k.sb([128, 1], F32), Buf()) for _ in range(4)])
    sdr = Ring([(k.sb([128, 2], F32), Buf()) for _ in range(4)])
    yr = Ring([(k.sb([128, 256], F32), Buf()) for _ in range(4)])
    otr = Ring([(k.sb([128, 2, 256], BF16), Buf()) for _ in range(2)])

    qblocks = [(0, list(range(2)))] + [(CTX + 256 * b, list(range(NKT))) for b in range(SEQ // 256)]
    steps = []
    for bi, (q0, kts) in enumerate(qblocks):
        for ki, kt in enumerate(kts):
            steps.append((bi, q0, ki, kt, len(kts)))
    LOOK = 2
    sbuf_of = {}

    def issue_S(si):
        bi, q0, ki, kt, n = steps[si]
        sb_, Bs = ps_s.next()
        pt, Bpt = pT.next()
        for m in range(2):
            k.op(k.pe, lambda e, sb_=sb_, m=m, kt=kt, q0=q0: e.matmul(
                sb_[:, m * 256:(m + 1) * 256], lhsT=kT[:, m, kt * 128:(kt + 1) * 128], rhs=qT[:, m, q0:q0 + 256],
                start=True, stop=True), reads=[Bk, Bq], writes=[Bs])
        k.op(k.act, lambda e, pt=pt, sb_=sb_: e.activation(out=pt.rearrange("p a b -> p (a b)"), in_=sb_, func=AF.Exp, scale=sc),
             reads=[Bs], writes=[Bpt])
        sbuf_of[si] = (pt, Bpt)

    def epilogue_front(q0):
        ys = []
        for qs in range(2):
            a1_, Ba1 = acc[0][qs]
            a2_, Ba2 = acc[1][qs]
            rr, Brr = r12.next()
            k.op(k.dve, lambda e, rr=rr, a1_=a1_: e.reciprocal(out=rr[:, 0:1], in_=a1_[:, 256:257]), reads=[Ba1], writes=[Brr])
            k.op(k.dve, lambda e, rr=rr, a2_=a2_: e.reciprocal(out=rr[:, 1:2], in_=a2_[:, 256:257]), reads=[Ba2, Brr], writes=[Brr])
            k.op(k.dve, lambda e, rr=rr: e.tensor_tensor(out=rr[:, 1:2], in0=rr[:, 1:2], in1=neglam, op=ALU.mult), reads=[Brr, Bnl], writes=[Brr])
            o1, Bo1 = o1r.next()
            k.op(k.act, lambda e, o1=o1, a1_=a1_, rr=rr: e.activation(out=o1, in_=a1_[:, 0:256], func=AF.Copy, scale=rr[:, 0:1]),
                 reads=[Ba1, Brr], writes=[Bo1])
            oo, Boo = orr.next()
            k.op(k.dve, lambda e, oo=oo, a2_=a2_, rr=rr, o1=o1: e.scalar_tensor_tensor(
                out=oo, in0=a2_[:, 0:256], scalar=rr[:, 1:2], in1=o1, op0=ALU.mult, op1=ALU.add), reads=[Ba2, Brr, Bo1], writes=[Boo])
            sq, Bsq = sqr.next()
            ss, Bss = ssr.next()
            k.op(k.act, lambda e, sq=sq, oo=oo, ss=ss: e.activation(out=sq, in_=oo, func=AF.Square, accum_out=ss), reads=[Boo], writes=[Bsq, Bss])
            sd, Bsd = sdr.next()
            k.op(k.act, lambda e, ss=ss, sd=sd: e.activation(out=sd[:, 0:1], in_=ss, func=AF.Sqrt, bias=epsb, scale=1.0 / 256), reads=[Bss, Beps], writes=[Bsd])
            k.op(k.dve, lambda e, sd=sd: e.reciprocal(out=sd[:, 1:2], in_=sd[:, 0:1]), reads=[Bsd], writes=[Bsd])
            y, By = yr.next()
            k.op(k.dve, lambda e, y=y, oo=oo, sd=sd: e.scalar_tensor_tensor(out=y, in0=oo, scalar=sd[:, 1:2], in1=gss, op0=ALU.mult, op1=ALU.mult),
                 reads=[Boo, Bsd, Bgss], writes=[By])
            ys.append((y, By))

        def back():
            ot, Bot = otr.next()
            for qs in range(2):
                y, By = ys[qs]
                for f in range(2):
                    k.op(k.pe, lambda e, y=y, f=f: e.transpose(out=ps_t[0][:, f * 128:(f + 1) * 128], in_=y[:, f * 128:(f + 1) * 128], identity=ident),
                         reads=[By, Bid], writes=[ps_t[1]])
                k.op(k.dve, lambda e, ot=ot, qs=qs: e.tensor_copy(out=ot[:, :, qs * 128:(qs + 1) * 128],
                                                               in_=ps_t[0][:, 0:256].rearrange("p (f q) -> p f q", f=2)),
                     reads=[ps_t[1]], writes=[Bot])
            for f in range(2):
                k.dma(k.sp, outd[f, :, q0:q0 + 256], ot[:, f, :], reads=[Bot], is_output=True)
        return back

    for si in range(min(LOOK, len(steps))):
        issue_S(si)
    pending = None
    for si, (bi, q0, ki, kt, n) in enumerate(steps):
        if si + LOOK < len(steps):
            issue_S(si + LOOK)
        pt, Bpt = sbuf_of.pop(si)
        for m in range(2):
            for qs in range(2):
                a_, Ba = acc[m][qs]
                k.op(k.pe, lambda e, a_=a_, pt=pt, m=m, qs=qs, kt=kt, ki=ki, n=n: e.matmul(
                    a_[:, 0:257], lhsT=pt[:, m, qs * 128:(qs + 1) * 128], rhs=va[:, kt, :],
                    start=(ki == 0), stop=(ki == n - 1)), reads=[Bpt, Bv], writes=[Ba])
        if pending is not None and (ki == 6 or ki == n - 1):
            pending()
            pending = None
        if ki == n - 1:
            pending = epilogue_front(q0)
    if pending is not None:
        pending()
    k.finish()
    return nc


def to_global_fm(arrs):
    return np.concatenate([a[..., :TC] for a in arrs] + [a[..., TC:] for a in arrs], axis=-1)


def to_global_tm(arrs):
    return np.concatenate([a[:TC] for a in arrs] + [a[TC:] for a in arrs], axis=0)


def lam_init_of(l):
    return float(0.8 - 0.6 * np.exp(-0.3 * l))


def run_da(p1, inp, l):
    qk = to_global_fm([np.asarray(r["o_qk"]) for r in p1])
    v = to_global_tm([np.asarray(r["o_v"]) for r in p1])
    nc = build_da()
    li = lam_init_of(l)
    lamv = np.ascontiguousarray(np.stack([inp["lambda_q1"][l], inp["lambda_k1"][l], inp["lambda_q2"][l], inp["lambda_k2"][l]], -1))
    lamc = np.ascontiguousarray(np.broadcast_to(np.array([li, 1.0 - li], np.float32), (128, 2)))
    gs = np.ascontiguousarray(np.broadcast_to(inp["da_subln_g"][l], (128, 256)))
    ones = np.ones((128, 128), np.float32)
    ident = np.eye(128, dtype=np.float32)
    onecol = np.ones((NTOK, 1), dtype=v.dtype)
    in_maps = []
    for h in range(NCORES):
        in_maps.append({"qT": np.ascontiguousarray(qk[2 * h:2 * h + 2]), "kT": np.ascontiguousarray(qk[16 + 2 * h:16 + 2 * h + 2]),
                        "vaug": np.ascontiguousarray(np.concatenate([v[:, 256 * h:256 * (h + 1)], onecol], axis=1)),
                        "lamv": lamv, "lamc": lamc, "gs": gs, "ones": ones, "ident": ident})
    res = run_bass_kernel_spmd(nc, in_maps, core_ids=list(range(NCORES)))
    return np.concatenate([np.asarray(r["daT"]).reshape(256, NTOK) for r in res.results], axis=0)


def build_ft():
    nc = bass.Bass("TRN2", target_bir_lowering=False)
    dt = nc.dram_tensor
    zTd = dt("zT", [8, 128, NTOK], BF16, kind="ExternalInput").ap()
    csd = dt("cs", [2, 128, 512], BF16, kind="ExternalInput").ap()
    tabd = dt("tab", [2, SEQ, TL], BF16, kind="ExternalInput").ap()
    ctabd = dt("ctab", [2, CTX, TC], BF16, kind="ExternalInput").ap()
    outd = dt("ftT", [8, 128, T1], BF16, kind="ExternalOutput").ap()
    k = K(nc)
    cs = k.sb([128, 2, 512], BF16); Bcs = Buf()
    for kc in range(2):
        k.dma(k.sp, cs[:, kc, :], csd[kc], writes=[Bcs])
    ctab = k.sb([128, 2, 2, TC], BF16); Bct = Buf()
    for c in range(2):
        k.dma(k.sp, ctab[:, c, :, :], ctabd[c].rearrange("(st p) t -> p st t", p=128), writes=[Bct])
    zr = Ring([(k.sb([128, 2, NTOK], BF16), Buf()) for _ in range(2)])
    AB = k.sb([128, NKT, 512], BF16); BAB = Buf()
    psa = Ring([(k.ps([128, 512]), Buf()) for _ in range(3)])
    acc = [[(k.ps([128, 512]), Buf()) for _ in range(2)] for _ in range(2)]
    psc = (k.ps([128, 512]), Buf())
    tr = Ring([(k.sb([128, 2, TL], BF16), Buf()) for _ in range(4)])
    otr = Ring([(k.sb([128, T1], BF16), Buf()) for _ in range(3)])
    ne = 0
    for g in range(4):
        zt, Bz = zr.next()
        for kc in range(2):
            k.dma(k.sp, zt[:, kc, :], zTd[2 * g + kc], writes=[Bz])
        for tt in range(NKT):
            pa, Bpa = psa.next()
            for kc in range(2):
                k.op(k.pe, lambda e, pa=pa, zt=zt, kc=kc, tt=tt: e.matmul(pa, lhsT=zt[:, kc, tt * 128:(tt + 1) * 128], rhs=cs[:, kc, :],
                                                                          start=(kc == 0), stop=(kc == 1)), reads=[Bz, Bcs], writes=[Bpa])
            ne += 1
            if ne % 2:
                k.op(k.act, lambda e, pa=pa, tt=tt: e.activation(out=AB[:, tt, :], in_=pa, func=AF.Copy), reads=[Bpa], writes=[BAB])
            else:
                k.op(k.dve, lambda e, pa=pa, tt=tt: e.tensor_copy(out=AB[:, tt, :], in_=pa), reads=[Bpa], writes=[BAB])
        for st in range(SEQ // 128):
            tb, Btb = tr.next()
            for c in range(2):
                k.dma(k.sp if c == 0 else k.act, tb[:, c, :], tabd[c, st * 128:(st + 1) * 128, :], writes=[Btb])
            for c in range(2):
                for ct in range(2):
                    for nb in range(2):
                        a_, Ba = acc[ct][nb]
                        k.op(k.pe, lambda e, a_=a_, st=st, c=c, ct=ct, nb=nb, tb=tb: e.matmul(
                            a_, lhsT=AB[:, 2 + st, c * 256 + ct * 128:c * 256 + (ct + 1) * 128], rhs=tb[:, c, nb * 512:(nb + 1) * 512],
                            start=(st == 0 and c == 0), stop=(st == SEQ // 128 - 1 and c == 1)), reads=[BAB, Btb], writes=[Ba])
        for ct in range(2):
            ot, Bo = otr.next()
            n = 0
            for st in range(2):
                for c in range(2):
                    k.op(k.pe, lambda e, st=st, c=c, ct=ct, n=n: e.matmul(
                        psc[0][:, 0:TC], lhsT=AB[:, st, c * 256 + ct * 128:c * 256 + (ct + 1) * 128], rhs=ctab[:, c, st, :],
                        start=(n == 0), stop=(n == 3)), reads=[BAB, Bct], writes=[psc[1]])
                    n += 1
            k.op(k.dve, lambda e, ot=ot: e.tensor_copy(out=ot[:, 0:TC], in_=psc[0][:, 0:TC]), reads=[psc[1]], writes=[Bo])
            for nb in range(2):
                a_, Ba = acc[ct][nb]
                if nb == 0:
                    k.op(k.act, lambda e, ot=ot, a_=a_, nb=nb: e.activation(out=ot[:, TC + nb * 512:TC + (nb + 1) * 512], in_=a_, func=AF.Copy),
                         reads=[Ba], writes=[Bo])
                else:
                    k.op(k.dve, lambda e, ot=ot, a_=a_, nb=nb: e.tensor_copy(out=ot[:, TC + nb * 512:TC + (nb + 1) * 512], in_=a_),
                         reads=[Ba], writes=[Bo])
            k.dma(k.sp, outd[2 * g + ct], ot, reads=[Bo], is_output=True)
    k.finish()
    return nc


_FT_TABLES = {}


def ft_tables():
    if not _FT_TABLES:
        bf = ml_dtypes.bfloat16
        c = np.arange(256)
        ang = 2 * np.pi * ((c[:, None] * c[None, :]) % 256) / 256.0
        cs = np.concatenate([np.cos(ang) / 16.0, -np.sin(ang) / 16.0], axis=1)
        _FT_TABLES["cs"] = np.ascontiguousarray(cs.reshape(2, 128, 512)).astype(bf)
        _FT_TABLES["ctab"] = np.stack([np.cos(ang) / 16.0, np.sin(ang) / 16.0]).astype(np.float32)
        s = np.arange(SEQ, dtype=np.int64)
        tabs = []
        for i in range(NCORES):
            t = np.arange(i * TL, (i + 1) * TL, dtype=np.int64)
            a = (2 * np.pi / SEQ) * ((s[:, None] * t[None, :]) % SEQ).astype(np.float64)
            sc = 1.0 / np.sqrt(SEQ)
            tabs.append(np.stack([(np.cos(a) * sc).astype(np.float32).astype(bf), (np.sin(a) * sc).astype(np.float32).astype(bf)]))
        _FT_TABLES["tab"] = tabs
    return _FT_TABLES


def run_ft(p1):
    zT = to_global_fm([np.asarray(r["o_z"]) for r in p1])
    T = ft_tables()
    nc = build_ft()
    bf = ml_dtypes.bfloat16
    in_maps = []
    for i in range(NCORES):
        in_maps.append({"zT": zT, "cs": T["cs"], "tab": T["tab"][i],
                        "ctab": np.ascontiguousarray(T["ctab"][:, :, i * TC:(i + 1) * TC]).astype(bf)})
    res = run_bass_kernel_spmd(nc, in_maps, core_ids=list(range(NCORES)))
    return [np.asarray(r["ftT"]).reshape(1024, T1) for r in res.results]


def build_gla():
    nc = bass.Bass("TRN2", target_bir_lowering=False)
    dt = nc.dram_tensor
    qTd = dt("qT", [128, NTOK], F32, kind="ExternalInput").ap()
    kTd = dt("kT", [128, NTOK], F32, kind="ExternalInput").ap()
    ktd = dt("ktok", [NTOK, 128], F32, kind="ExternalInput").ap()
    vd = dt("v", [NTOK, 256], BF16, kind="ExternalInput").ap()
    gdd = dt("gda", [17, NTOK], F32, kind="ExternalInput").ap()
    w2d = dt("w2a", [17, 128], F32, kind="ExternalInput").ap()
    triId = dt("triI", [128, 128], F32, kind="ExternalInput").ap()
    triAd = dt("triA", [128, 128], F32, kind="ExternalInput").ap()
    maskd = dt("maskT", [128, 128], F32, kind="ExternalInput").ap()
    outd = dt("o", [NTOK, 256], F32, kind="ExternalOutput").ap()
    k = K(nc)
    qT = k.sb([128, NTOK], F32); Bq = Buf()
    kT = k.sb([128, NTOK], F32); Bk = Buf()
    kt = k.sb([128, NKT, 128], F32); Bkt = Buf()
    v = k.sb([128, NKT, 256], BF16); Bv = Buf()
    gda = k.sb([17, NTOK], F32); Bgd = Buf()
    w2a = k.sb([17, 128], F32); Bw2 = Buf()
    triI = k.sb([128, 128], F32); BtI = Buf()
    triA = k.sb([128, 128], F32); BtA = Buf()
    mask = k.sb([128, 128], F32); Bm = Buf()
    k.dma(k.sp, gda, gdd, writes=[Bgd])
    for (d_, s_, b_) in [(w2a, w2d, Bw2), (triI, triId, BtI), (triA, triAd, BtA), (mask, maskd, Bm)]:
        k.dma(k.sp, d_, s_, writes=[b_])
    k.dma(k.sp, qT, qTd, writes=[Bq])
    k.dma(k.act, kT, kTd, writes=[Bk])
    for g in range(0, NKT, 22):
        k.dma(k.sp, kt[:, g:g + 22, :], ktd[g * 128:(g + 22) * 128, :].rearrange("(c p) d -> p c d", p=128), writes=[Bkt])
        k.dma(k.act, v[:, g:g + 22, :], vd[g * 128:(g + 22) * 128, :].rearrange("(c p) d -> p c d", p=128), writes=[Bv])
    oneb = k.sb([128, 1], F32); B1 = Buf()
    k.op(k.dve, lambda e: e.memset(oneb, 1.0), writes=[B1])
    S = [(k.sb([128, 256], F32), Buf()) for _ in range(2)]
    Sb = [(k.sb([128, 256], BF16), Buf()) for _ in range(2)]
    k.op(k.dve, lambda e: e.memset(S[0][0], 0.0), writes=[S[0][1]])
    k.op(k.dve, lambda e: e.memset(Sb[0][0], 0.0), writes=[Sb[0][1]])
    p_lg = (k.ps([128, 512]), Buf()); p_cT = (k.ps([128, 512]), Buf()); p_rs = (k.ps([128, 512]), Buf())
    p_AT = (k.ps([128, 512]), Buf()); p_o = Ring([(k.ps([128, 512]), Buf()) for _ in range(2)]); p_kv = (k.ps([128, 512]), Buf())

    def R2(shape, dt_):
        return Ring([(k.sb(shape, dt_), Buf()) for _ in range(2)])
    r_e, r_la, r_E1, r_E2, r_E3 = (R2([128, 128], F32) for _ in range(5))
    r_qi, r_ki, r_ko, r_Am = (R2([128, 128], BF16) for _ in range(4))
    r_o = R2([128, 256], F32)
    sc = 128.0 ** -0.5
    for c in range(NKT):
        cs_ = slice(c * 128, (c + 1) * 128)
        k.op(k.pe, lambda e, cs_=cs_: e.matmul(p_lg[0][:, 0:128], lhsT=gda[:, cs_], rhs=w2a, start=True, stop=True),
             reads=[Bgd, Bw2], writes=[p_lg[1]])
        e_, Be = r_e.next()
        k.op(k.act, lambda e, e_=e_: e.activation(out=e_, in_=p_lg[0][:, 0:128], func=AF.Exp, scale=-1.0), reads=[p_lg[1]], writes=[Be])
        la, Bla = r_la.next()
        k.op(k.act, lambda e, e_=e_, la=la: e.activation(out=la, in_=e_, func=AF.Ln, bias=oneb, scale=1.0), reads=[Be, B1], writes=[Bla])
        k.op(k.pe, lambda e, la=la: e.matmul(p_cT[0][:, 0:128], lhsT=la, rhs=triI, start=True, stop=True), reads=[Bla, BtI], writes=[p_cT[1]])
        k.op(k.pe, lambda e, la=la: e.matmul(p_rs[0][:, 0:128], lhsT=triA, rhs=la, start=True, stop=True), reads=[Bla, BtA], writes=[p_rs[1]])
        E1, BE1 = r_E1.next(); E2, BE2 = r_E2.next(); E3, BE3 = r_E3.next()
        k.op(k.act, lambda e, E1=E1: e.activation(out=E1, in_=p_cT[0][:, 0:128], func=AF.Exp), reads=[p_cT[1]], writes=[BE1])
        k.op(k.act, lambda e, E2=E2: e.activation(out=E2, in_=p_cT[0][:, 0:128], func=AF.Exp, scale=-1.0), reads=[p_cT[1]], writes=[BE2])
        k.op(k.act, lambda e, E3=E3: e.activation(out=E3, in_=p_rs[0][:, 0:128], func=AF.Exp), reads=[p_rs[1]], writes=[BE3])
        qi, Bqi = r_qi.next(); ki, Bki = r_ki.next(); ko, Bko = r_ko.next()
        k.op(k.dve, lambda e, qi=qi, E1=E1, cs_=cs_: e.scalar_tensor_tensor(out=qi, in0=qT[:, cs_], scalar=sc, in1=E1, op0=ALU.mult, op1=ALU.mult),
             reads=[Bq, BE1], writes=[Bqi])
        k.op(k.pool, lambda e, ki=ki, E2=E2, cs_=cs_: e.tensor_tensor(out=ki, in0=kT[:, cs_], in1=E2, op=ALU.mult), reads=[Bk, BE2], writes=[Bki])
        k.op(k.pool, lambda e, ko=ko, E3=E3, c=c: e.tensor_tensor(out=ko, in0=kt[:, c, :], in1=E3, op=ALU.mult), reads=[Bkt, BE3], writes=[Bko])
        k.op(k.pe, lambda e, ki=ki, qi=qi: e.matmul(p_AT[0][:, 0:128], lhsT=ki, rhs=qi, start=True, stop=True), reads=[Bki, Bqi], writes=[p_AT[1]])
        Am, BAm = r_Am.next()
        k.op(k.dve, lambda e, Am=Am: e.tensor_tensor(out=Am, in0=p_AT[0][:, 0:128], in1=mask, op=ALU.mult), reads=[p_AT[1], Bm], writes=[BAm])
        po, Bpo = p_o.next()
        Scur, BScur = S[c % 2]; Snew, BSnew = S[(c + 1) % 2]
        Sbcur, BSbcur = Sb[c % 2]; Sbnew, BSbnew = Sb[(c + 1) % 2]
        k.op(k.pe, lambda e, po=po, Am=Am, c=c: e.matmul(po[:, 0:256], lhsT=Am, rhs=v[:, c, :], start=True, stop=False), reads=[BAm, Bv], writes=[Bpo])
        k.op(k.pe, lambda e, po=po, qi=qi, Sbcur=Sbcur: e.matmul(po[:, 0:256], lhsT=qi, rhs=Sbcur, start=False, stop=True), reads=[Bqi, BSbcur], writes=[Bpo])
        k.op(k.pe, lambda e, ko=ko, c=c: e.matmul(p_kv[0][:, 0:256], lhsT=ko, rhs=v[:, c, :], start=True, stop=True), reads=[Bko, Bv], writes=[p_kv[1]])
        k.op(k.dve, lambda e, Snew=Snew, Scur=Scur, E1=E1: e.scalar_tensor_tensor(out=Snew, in0=Scur, scalar=E1[:, 127:128], in1=p_kv[0][:, 0:256],
                                                                                op0=ALU.mult, op1=ALU.add), reads=[BScur, BE1, p_kv[1]], writes=[BSnew])
        k.op(k.act, lambda e, Sbnew=Sbnew, Snew=Snew: e.activation(out=Sbnew, in_=Snew, func=AF.Copy), reads=[BSnew], writes=[BSbnew])
        ot, Bot = r_o.next()
        k.op(k.act, lambda e, ot=ot, po=po: e.activation(out=ot, in_=po[:, 0:256], func=AF.Copy), reads=[Bpo], writes=[Bot])
        k.dma(k.sp, outd[cs_, :], ot, reads=[Bot], is_output=True)
    k.finish()
    return nc


def gla_perm(direction):
    if direction == 0:
        return np.arange(NTOK)
    return np.concatenate([np.arange(CTX)[::-1], CTX + np.arange(SEQ)[::-1]])


def run_gla(p1, inp, l):
    gqk = to_global_fm([np.asarray(r["o_gqk"]) for r in p1])
    gk = to_global_tm([np.asarray(r["o_gk"]) for r in p1])
    gv = to_global_tm([np.asarray(r["o_gv"]) for r in p1])
    gd = to_global_fm([np.asarray(r["o_gd"]) for r in p1])
    nc = build_gla()
    j = np.arange(128)
    triI = np.where(j[:, None] <= j[None, :], -1.0 / 16, 0.0).astype(np.float32)
    triA = np.where(j[:, None] > j[None, :], -1.0 / 16, 0.0).astype(np.float32)
    maskT = (j[:, None] <= j[None, :]).astype(np.float32)
    onesrow = np.ones((1, NTOK), np.float32)
    in_maps = []
    perms = []
    for i in range(NCORES):
        hh, d = i // 2, i % 2
        pm = gla_perm(d)
        perms.append(pm)
        w2a = np.concatenate([inp["gla_gate_w2"][l, d][:, hh * 128:(hh + 1) * 128], inp["gla_gate_b"][l, d][None, hh * 128:(hh + 1) * 128]], 0)
        in_maps.append({"qT": np.ascontiguousarray(gqk[hh][:, pm]), "kT": np.ascontiguousarray(gqk[4 + hh][:, pm]),
                        "ktok": np.ascontiguousarray(gk[pm, hh * 128:(hh + 1) * 128]),
                        "v": np.ascontiguousarray(gv[pm, hh * 256:(hh + 1) * 256]),
                        "gda": np.ascontiguousarray(np.concatenate([gd[16 * d:16 * (d + 1)][:, pm], onesrow], 0)),
                        "w2a": np.ascontiguousarray(w2a.astype(np.float32)), "triI": triI, "triA": triA, "maskT": maskT})
    res = run_bass_kernel_spmd(nc, in_maps, core_ids=list(range(NCORES)))
    of = np.zeros((NTOK, 1024), np.float32)
    ob = np.zeros((NTOK, 1024), np.float32)
    for i in range(NCORES):
        hh, d = i // 2, i % 2
        o = np.asarray(res.results[i]["o"])
        tgt = of if d == 0 else ob
        tgt[perms[i], hh * 256:(hh + 1) * 256] = o
    return of, ob


T3 = T1 + 4
BLK3 = blocks_of(T3, 3)
RNG3 = [(0, TC + 2, 1), (TC + 2, T3, 0)]
NFT = 2 * DFF // 128
NJ = DFF // 128


def k_barrier(k):
    evs = [(E.sem, E.count) for E in k.engs if E.count > 0]
    for name, (lst, _) in k.dma_pool.items():
        for sem, uses in lst:
            if uses > 0:
                evs.append((sem, 16 * uses))
    for E in k.engs:
        for ev in evs:
            if ev[0] is E.sem:
                continue
            k._wait(E, ev)


def build_p3():
    nc = bass.Bass("TRN2", target_bir_lowering=False)
    dt = nc.dram_tensor
    xTd = dt("xT", [D, T3], F32, kind="ExternalInput").ap()
    mixd = dt("mixT", [3072, T3], BF16, kind="ExternalInput").ap()
    gofd = dt("gof", [T3, 1024], F32, kind="ExternalInput").ap()
    gobd = dt("gob", [T3, 1024], F32, kind="ExternalInput").ap()
    rd = dt("r", [T3, 1024], F32, kind="ExternalInput").ap()
    modsd = dt("modsT", [128, 2, 6, KC], F32, kind="ExternalInput").ap()
    g2d = dt("g2", [128, KC], F32, kind="ExternalInput").ap()
    gngd = dt("gng", [128, 1024], F32, kind="ExternalInput").ap()
    onesd = dt("ones", [128, 128], F32, kind="ExternalInput").ap()
    identd = dt("ident", [128, 128], F32, kind="ExternalInput").ap()
    cmd = dt("cm", [128, T3], F32, kind="ExternalInput").ap()
    cwd = dt("cw", [128, 3, NFT], F32, kind="ExternalInput").ap()
    cbd = dt("cb", [128, NFT], F32, kind="ExternalInput").ap()
    wod = dt("w_out", [D, D], F32, kind="ExternalInput").ap()
    wud = dt("w_up", [D, 2 * DFF], F32, kind="ExternalInput").ap()
    wdd = dt("w_down", [DFF, D], F32, kind="ExternalInput").ap()
    xod = dt("xoT", [D, T3], F32, kind="ExternalOutput").ap()
    xnd = dt("xnT", [D, T3], F32, kind="Internal").ap()
    aTd = dt("aT", [NJ, 128, T3], BF16, kind="Internal").ap()
    Bxn_d = [Buf() for _ in range(KC)]
    BaT_d = [Buf() for _ in range(NJ)]
    k = K(nc)

    Bmix = [Buf() for _ in range(KC)]
    wflat = k.sb([128, 4 * KC * 256], BF16)
    wr = Ring([(wflat[:, i * KC * 256:(i + 1) * KC * 256].rearrange("p (k n) -> p k n", n=256), Buf()) for i in range(4)])
    mods = k.sb([128, 2, 6, KC], F32); Bmods = Buf()
    g2s = k.sb([128, KC], F32); Bg2 = Buf()
    ones = k.sb([128, 128], F32); Bones = Buf()
    ident = k.sb([128, 128], F32); Bid = Buf()
    cm = k.sb([128, T3], F32); Bcm = Buf()
    cw = k.sb([128, 3, NFT], F32); Bcw = Buf()
    cb = k.sb([128, NFT], F32); Bcb = Buf()
    for (d_, s_, b_) in [(mods, modsd, Bmods), (g2s, g2d, Bg2), (ones, onesd, Bones), (ident, identd, Bid),
                         (cm, cmd, Bcm), (cw, cwd, Bcw), (cb, cbd, Bcb)]:
        k.dma(k.sp, d_, s_, writes=[b_])
    epsb = k.sb([128, 1], F32); Beps = Buf()
    k.op(k.dve, lambda e: e.memset(epsb, EPS), writes=[Beps])
    a2 = k.sb([128, 2, KC], F32); Ba2 = Buf()
    for s in range(2):
        k.op(k.dve, lambda e, s=s: e.scalar_tensor_tensor(out=a2[:, s, :], in0=mods[:, s, 4, :], scalar=1.0, in1=g2s, op0=ALU.add, op1=ALU.mult),
             reads=[Bmods, Bg2], writes=[Ba2])
    psr = Ring([(k.ps([128, 512]), Buf()) for _ in range(8)])
    ss4 = k.sb([128, 4], F32); Bss4 = Buf()
    sd4 = k.sb([128, 4], F32); Bsd4 = Buf()
    rs4 = k.sb([128, 4], F32); Brs4 = Buf()
    a16r = Ring([(k.sb([128, T3], BF16), Buf()) for _ in range(2)])
    big_cm = nc.sbuf_tensor("bigmix", [128, KC * T3], BF16)
    big = big_cm.__enter__().ap()
    mixT = big.rearrange("p (k t) -> p k t", t=T3)
    for g in range(0, 24, 6):
        k.dma(k.sp, mixT[:, g:g + 6, :], mixd[g * 128:(g + 6) * 128, :].rearrange("(kc p) t -> p kc t", p=128), writes=Bmix[g:g + 6])
    with nc.sbuf_tensor("gA", [128, 8, 1024], F32) as gA_t:
        gA = gA_t.ap()
        gng = gA[:, 0, :]; Bgng = Buf()
        k.dma(k.sp, gng, gngd, writes=[Bgng])
        t_of, t_ob, t_r, t_o, t_sr, t_y, t_junk = (gA[:, i, :] for i in range(1, 8))
        Bof, Bob, Br_, Bo_, Bsr, By, Bj = (Buf() for _ in range(7))
        for s0 in range(0, T3, 128):
            mt = min(128, T3 - s0)
            k.dma(k.sp, t_of[0:mt], gofd[s0:s0 + mt, :], writes=[Bof])
            k.dma(k.act, t_ob[0:mt], gobd[s0:s0 + mt, :], writes=[Bob])
            k.dma(k.sp, t_r[0:mt], rd[s0:s0 + mt, :], writes=[Br_])
            k.op(k.pool, lambda e, mt=mt: e.tensor_tensor(out=t_o[0:mt], in0=t_of[0:mt], in1=t_ob[0:mt], op=ALU.add), reads=[Bof, Bob], writes=[Bo_])
            k.op(k.act, lambda e, mt=mt: e.activation(out=t_sr[0:mt], in_=t_r[0:mt], func=AF.Silu), reads=[Br_], writes=[Bsr])
            k.op(k.pool, lambda e, mt=mt: e.tensor_tensor(out=t_sr[0:mt], in0=t_sr[0:mt], in1=gng[0:mt], op=ALU.mult), reads=[Bsr, Bgng], writes=[Bsr])
            for hh in range(4):
                k.op(k.act, lambda e, mt=mt, hh=hh: e.activation(out=t_junk[0:mt, hh * 256:(hh + 1) * 256], in_=t_o[0:mt, hh * 256:(hh + 1) * 256],
                                                                 func=AF.Square, accum_out=ss4[0:mt, hh:hh + 1]), reads=[Bo_], writes=[Bj, Bss4])
            k.op(k.act, lambda e, mt=mt: e.activation(out=sd4[0:mt], in_=ss4[0:mt], func=AF.Sqrt, bias=epsb[0:mt], scale=1.0 / 256), reads=[Bss4, Beps], writes=[Bsd4])
            k.op(k.dve, lambda e, mt=mt: e.reciprocal(out=rs4[0:mt], in_=sd4[0:mt]), reads=[Bsd4], writes=[Brs4])
            for hh in range(4):
                k.op(k.dve, lambda e, mt=mt, hh=hh: e.scalar_tensor_tensor(
                    out=t_y[0:mt, hh * 256:(hh + 1) * 256], in0=t_o[0:mt, hh * 256:(hh + 1) * 256], scalar=rs4[0:mt, hh:hh + 1],
                    in1=t_sr[0:mt, hh * 256:(hh + 1) * 256], op0=ALU.mult, op1=ALU.mult), reads=[Bo_, Brs4, Bsr], writes=[By])
            for b in range(2):
                pb, Bp = psr.next()
                for f in range(4):
                    k.op(k.pe, lambda e, pb=pb, b=b, f=f, mt=mt: e.transpose(out=pb[:, f * 128:f * 128 + mt], in_=t_y[0:mt, (4 * b + f) * 128:(4 * b + f + 1) * 128],
                                                                           identity=ident[0:mt, 0:mt]), reads=[By, Bid], writes=[Bp])
                k.op(k.act if b == 0 else k.dve, (lambda e, pb=pb, b=b, s0=s0, mt=mt: e.activation(
                    out=mixT[:, 24 + 4 * b:28 + 4 * b, s0:s0 + mt], in_=pb.rearrange("p (f q) -> p f q", f=4)[:, :, 0:mt], func=AF.Copy)) if b == 0 else
                    (lambda e, pb=pb, b=b, s0=s0, mt=mt: e.tensor_copy(
                        out=mixT[:, 24 + 4 * b:28 + 4 * b, s0:s0 + mt], in_=pb.rearrange("p (f q) -> p f q", f=4)[:, :, 0:mt])),
                    reads=[Bp], writes=Bmix[24 + 4 * b:28 + 4 * b])
        k_barrier(k)

    plist = [(wod, c0) for c0 in range(0, D, 256)]
    for pp in range(NJ // 2):
        plist += [(wud, pp * 256), (wud, DFF + pp * 256)]
    ploaded = {}
    pnx = [0]

    def prefetch():
        if pnx[0] < len(plist):
            wsrc, c0 = plist[pnx[0]]
            wt, Bw = wr.next()
            k.dma(k.pool, wt, wsrc[:, c0:c0 + 256].rearrange("(kc p) n -> p kc n", p=128), writes=[Bw])
            ploaded[pnx[0]] = (wt, Bw)
            pnx[0] += 1
    pcur = [0]

    def load_panel(wsrc, c0, pw, nk):
        while pcur[0] >= pnx[0]:
            prefetch()
        res = ploaded.pop(pcur[0])
        pcur[0] += 1
        while pnx[0] < min(len(plist), pcur[0] + 2):
            prefetch()
        return res

    with nc.sbuf_tensor("tB", [128, 12, T3], F32) as tB_t:
        tB = tB_t.ap()
        xr = Ring([(tB[:, i, :], Buf()) for i in range(2)])
        xnr = Ring([(tB[:, 2 + i, :], Buf()) for i in range(2)])
        sqr = Ring([(tB[:, 4 + i, :], Buf()) for i in range(2)])
        acc = tB[:, 6, :]; Bacc = Buf()
        rstd = tB[:, 7, :]; Brstd = Buf()
        for c0 in range(0, D, 256):
            wt, Bw = load_panel(wod, c0, 256, KC)
            for mi in range(2):
                m = c0 // 128 + mi
                banks = [psr.next() for _ in BLK3]
                for kc in range(KC):
                    for bi, (s0, wd) in enumerate(BLK3):
                        k.op(k.pe, lambda e, bi=bi, s0=s0, wd=wd, kc=kc, mi=mi, wt=wt, banks=banks: e.matmul(
                            banks[bi][0][:, 0:wd], lhsT=wt[:, kc, mi * 128:(mi + 1) * 128], rhs=mixT[:, kc, s0:s0 + wd],
                            start=(kc == 0), stop=(kc == KC - 1)), reads=[Bw, Bmix[kc]], writes=[banks[bi][1]])
                xt, Bx = xr.next()
                k.dma(k.sp, xt, xTd[m * 128:(m + 1) * 128, :], writes=[Bx])
                xn, Bxn = xnr.next()
                for bi, blk in enumerate(BLK3):
                    for (a, b, s) in split_ranges(blk, RNG3):
                        k.op(k.dve, lambda e, a=a, b=b, s=s, m=m, xn=xn, xt=xt, pb=banks[bi][0], s0=blk[0]: e.scalar_tensor_tensor(
                            out=xn[:, a:b], in0=pb[:, a - s0:b - s0], scalar=mods[:, s, 2, m:m + 1], in1=xt[:, a:b], op0=ALU.mult, op1=ALU.add),
                            reads=[banks[bi][1], Bx, Bmods], writes=[Bxn])
                k.dma(k.sp, xnd[m * 128:(m + 1) * 128, :], xn, reads=[Bxn], writes=[Bxn_d[m]])
                if m == 0:
                    k.op(k.act, lambda e, xn=xn: e.activation(out=acc, in_=xn, func=AF.Square), reads=[Bxn], writes=[Bacc])
                else:
                    sq, Bsq = sqr.next()
                    k.op(k.act, lambda e, xn=xn, sq=sq: e.activation(out=sq, in_=xn, func=AF.Square), reads=[Bxn], writes=[Bsq])
                    k.op(k.dve, lambda e, sq=sq: e.tensor_tensor(out=acc, in0=acc, in1=sq, op=ALU.add), reads=[Bsq, Bacc], writes=[Bacc])
        banks = [psr.next() for _ in BLK3]
        for bi, (s0, wd) in enumerate(BLK3):
            k.op(k.pe, lambda e, bi=bi, s0=s0, wd=wd: e.matmul(banks[bi][0][:, 0:wd], lhsT=ones, rhs=acc[:, s0:s0 + wd], start=True, stop=True),
                 reads=[Bones, Bacc], writes=[banks[bi][1]])
            k.op(k.act, lambda e, bi=bi, s0=s0, wd=wd: e.activation(out=rstd[:, s0:s0 + wd], in_=banks[bi][0][:, 0:wd], func=AF.Sqrt, bias=epsb, scale=1.0 / D),
                 reads=[banks[bi][1], Beps], writes=[Brstd])
        rst2 = tB[:, 8, :]; Brst2 = Buf()
        k.op(k.dve, lambda e: e.reciprocal(out=rst2, in_=rstd), reads=[Brstd], writes=[Brst2])
        for m in range(KC):
            xn, Bxn = xnr.next()
            k.dma(k.sp, xn, xnd[m * 128:(m + 1) * 128, :], reads=[Bxn_d[m]], writes=[Bxn])
            tmp, Bt = sqr.next()
            k.op(k.dve, lambda e, tmp=tmp, xn=xn: e.tensor_tensor(out=tmp, in0=xn, in1=rst2, op=ALU.mult), reads=[Bxn, Brst2], writes=[Bt])
            for (a, b, s) in RNG3:
                k.op(k.dve, lambda e, a=a, b=b, s=s, m=m, tmp=tmp: e.tensor_scalar(
                    out=tmp[:, a:b], in0=tmp[:, a:b], scalar1=a2[:, s, m:m + 1], scalar2=mods[:, s, 3, m:m + 1], op0=ALU.mult, op1=ALU.add),
                    reads=[Bt, Ba2, Bmods], writes=[Bt])
            k.op(k.pool, lambda e, m=m, tmp=tmp: e.tensor_tensor(out=mixT[:, m, :], in0=tmp, in1=cm, op=ALU.mult), reads=[Bt, Bcm], writes=[Bmix[m]])
        k_barrier(k)
        ugr = Ring([(tB[:, i, :], Buf()) for i in (0, 1)])
        uvr = Ring([(tB[:, i, :], Buf()) for i in (2, 3)])
        cgr = Ring([(tB[:, i, :], Buf()) for i in (4, 5)])
        cvr = Ring([(tB[:, i, :], Buf()) for i in (6, 7)])
        sgr = Ring([(tB[:, i, :], Buf()) for i in (8, 9)])
        for rg_ in (cgr, cvr):
            for (c_, Bc) in rg_.items:
                k.op(k.pool, lambda e, c_=c_: e.memset(c_, 0.0), writes=[Bc])
        for pp in range(NJ // 2):
            wg, Bwg = load_panel(wud, pp * 256, 256, KC)
            wv, Bwv = load_panel(wud, DFF + pp * 256, 256, KC)
            for mi in range(2):
                j = 2 * pp + mi
                cs_ = []
                for (wt, Bw, ur, cr, ti) in [(wg, Bwg, ugr, cgr, j), (wv, Bwv, uvr, cvr, NJ + j)]:
                    banks = [psr.next() for _ in BLK3]
                    for kc in range(KC):
                        for bi, (s0, wd) in enumerate(BLK3):
                            k.op(k.pe, lambda e, bi=bi, s0=s0, wd=wd, kc=kc, mi=mi, wt=wt, banks=banks: e.matmul(
                                banks[bi][0][:, 0:wd], lhsT=wt[:, kc, mi * 128:(mi + 1) * 128], rhs=mixT[:, kc, s0:s0 + wd],
                                start=(kc == 0), stop=(kc == KC - 1)), reads=[Bw, Bmix[kc]], writes=[banks[bi][1]])
                    u, Bu = ur.next()
                    for bi, (s0, wd) in enumerate(BLK3):
                        k.op(k.act, lambda e, u=u, pb=banks[bi][0], s0=s0, wd=wd: e.activation(out=u[:, s0:s0 + wd], in_=pb[:, 0:wd], func=AF.Copy),
                             reads=[banks[bi][1]], writes=[Bu])
                    c_, Bc = cr.next()
                    for (a, b, s) in RNG3:
                        k.op(k.act, lambda e, u=u, c_=c_, a=a, b=b, ti=ti: e.activation(
                            out=c_[:, a + 1:b - 1], in_=u[:, a + 1:b - 1], func=AF.Identity, scale=cw[:, 1, ti:ti + 1], bias=cb[:, ti:ti + 1]),
                            reads=[Bu, Bcw, Bcb], writes=[Bc])
                        k.op(k.dve, lambda e, u=u, c_=c_, a=a, b=b, ti=ti: e.scalar_tensor_tensor(
                            out=c_[:, a + 1:b - 1], in0=u[:, a:b - 2], scalar=cw[:, 0, ti:ti + 1], in1=c_[:, a + 1:b - 1], op0=ALU.mult, op1=ALU.add),
                            reads=[Bu, Bcw, Bc], writes=[Bc])
                        k.op(k.dve, lambda e, u=u, c_=c_, a=a, b=b, ti=ti: e.scalar_tensor_tensor(
                            out=c_[:, a + 1:b - 1], in0=u[:, a + 2:b], scalar=cw[:, 2, ti:ti + 1], in1=c_[:, a + 1:b - 1], op0=ALU.mult, op1=ALU.add),
                            reads=[Bu, Bcw, Bc], writes=[Bc])
                    cs_.append((c_, Bc))
                (cg, Bcg), (cv, Bcv) = cs_
                sg, Bsg = sgr.next()
                k.op(k.act, lambda e, sg=sg, cg=cg: e.activation(out=sg, in_=cg, func=AF.Silu), reads=[Bcg], writes=[Bsg])
                a16, Ba16 = a16r.next()
                k.op(k.dve, lambda e, a16=a16, sg=sg, cv=cv: e.tensor_tensor(out=a16, in0=sg, in1=cv, op=ALU.mult), reads=[Bsg, Bcv], writes=[Ba16])
                k.dma(k.sp, aTd[j], a16, reads=[Ba16], writes=[BaT_d[j]])
        k_barrier(k)

    big_cm.__exit__(None, None, None)
    BLK2 = blocks_of(T3, 2)
    PW = BLK2[0][1]
    aS = k.sb([128, NJ, PW], BF16)
    BaS = Buf()
    wx = k.sb([128, NJ, 128], BF16)
    wr2 = Ring([(wflat[:, i * NJ * 128:(i + 1) * NJ * 128].rearrange("p (k n) -> p k n", n=128), Buf()) for i in range(2)] + [(wx, Buf())])
    xpr = Ring([(k.sb([128, PW], F32), Buf()) for _ in range(2)])
    xor_ = Ring([(k.sb([128, PW], F32), Buf()) for _ in range(2)])
    for (s0, wd) in BLK2:
        for g in range(0, NJ, 43):
            k.dma(k.sp, aS[:, g:g + 43, 0:wd], aTd[g:g + 43, :, s0:s0 + wd].rearrange("k p t -> p k t"), reads=BaT_d[g:g + 43], writes=[BaS])
        hw = wd // 2
        for m in range(KC):
            wt, Bw = wr2.next()
            k.dma(k.pool, wt, wdd[:, m * 128:(m + 1) * 128].rearrange("(kc p) n -> p kc n", p=128), writes=[Bw])
            pbs = [psr.next(), psr.next()]
            for kc in range(NJ):
                for hi in range(2):
                    k.op(k.pe, lambda e, pb=pbs[hi][0], kc=kc, wt=wt, hi=hi, hw=hw: e.matmul(
                        pb[:, 0:hw], lhsT=wt[:, kc, :], rhs=aS[:, kc, hi * hw:(hi + 1) * hw],
                        start=(kc == 0), stop=(kc == NJ - 1)), reads=[Bw, BaS], writes=[pbs[hi][1]])
            xp, Bxp = xpr.next()
            k.dma(k.sp, xp[:, 0:wd], xnd[m * 128:(m + 1) * 128, s0:s0 + wd], reads=[Bxn_d[m]], writes=[Bxp])
            xo, Bxo = xor_.next()
            for hi in range(2):
                for (a, b, s) in split_ranges((s0 + hi * hw, hw), RNG3):
                    k.op(k.dve, lambda e, a=a, b=b, s=s, m=m, xo=xo, xp=xp, pb=pbs[hi][0], s0=s0, o=s0 + hi * hw: e.scalar_tensor_tensor(
                        out=xo[:, a - s0:b - s0], in0=pb[:, a - o:b - o], scalar=mods[:, s, 5, m:m + 1], in1=xp[:, a - s0:b - s0], op0=ALU.mult, op1=ALU.add),
                        reads=[pbs[hi][1], Bxp, Bmods], writes=[Bxo])
            k.dma(k.sp, xod[m * 128:(m + 1) * 128, s0:s0 + wd], xo[:, 0:wd], reads=[Bxo], is_output=True)
        k_barrier(k)
    k.finish()
    return nc


def idx1_of(i):
    return np.concatenate([np.arange(TC * i, TC * (i + 1)), CTX + np.arange(TL * i, TL * (i + 1))])


def idx3_of(i):
    c = np.arange(TC * i - 1, TC * (i + 1) + 1)
    t = np.arange(TL * i - 1, TL * (i + 1) + 1)
    valid = np.concatenate([(c >= 0) & (c < CTX), (t >= 0) & (t < SEQ)])
    idx = np.concatenate([np.clip(c, 0, CTX - 1), CTX + np.clip(t, 0, SEQ - 1)])
    return idx, valid


def run_p3(xT_glob, daT, ft_cores, gof, gob, p1, modsT, inp, l):
    ftT = to_global_fm(ft_cores)
    mix_glob = np.concatenate([daT, ftT], axis=0)
    r_glob = to_global_tm([np.asarray(r["o_r"]) for r in p1])
    nc = build_p3()
    ones = np.ones((128, 128), np.float32)
    ident = np.eye(128, dtype=np.float32)
    g2 = vec_layout(inp["norm2_g"][l])
    gng = np.ascontiguousarray(np.broadcast_to(np.tile(inp["gla_norm_g"][l], 4), (128, 1024)))
    cw = np.ascontiguousarray(inp["conv_w"][l].reshape(3, NFT, 128).transpose(2, 0, 1))
    cb = vec_layout(inp["conv_b"][l])
    w_out, w_up, w_down = inp["w_out"][l], inp["w_up"][l], inp["w_down"][l]
    in_maps = []
    for i in range(NCORES):
        idx, valid = idx3_of(i)
        vm = valid.astype(np.float32)
        xT = np.ascontiguousarray(xT_glob[:, idx] * vm[None, :])
        mixT = mix_glob[:, idx].copy()
        mixT[:, ~valid] = 0
        gf = gof[idx].copy(); gf[~valid] = 0
        gb = gob[idx].copy(); gb[~valid] = 0
        rr = r_glob[idx].copy(); rr[~valid] = 0
        in_maps.append({"xT": xT, "mixT": np.ascontiguousarray(mixT), "gof": gf, "gob": gb, "r": rr, "modsT": modsT, "g2": g2,
                        "gng": gng, "ones": ones, "ident": ident, "cm": np.ascontiguousarray(np.broadcast_to(vm, (128, T3))),
                        "cw": cw, "cb": cb, "w_out": w_out, "w_up": w_up, "w_down": w_down})
    res = run_bass_kernel_spmd(nc, in_maps, core_ids=list(range(NCORES)))
    outs = [np.asarray(r["xoT"]) for r in res.results]
    ctx_part = [o[:, 1:1 + TC] for o in outs]
    lat_part = [o[:, TC + 3:TC + 3 + TL] for o in outs]
    return np.concatenate(ctx_part + lat_part, axis=1)


def kernel(**inp):
    inp = {k_: np.asarray(v_) for k_, v_ in inp.items()}
    mods = run_p0(inp)
    xT_glob = np.ascontiguousarray(np.concatenate([inp["ctx"][0], inp["x"][0]], axis=0).T)
    for l in range(DEPTH):
        modsT = mods_layout(mods[l])
        xT_cores = [np.ascontiguousarray(xT_glob[:, idx1_of(i)]) for i in range(NCORES)]
        p1 = run_p1(xT_cores, inp["w_in"][l], modsT, inp["norm1_g"][l], inp["q_norm_g"][l], inp["k_norm_g"][l])
        p1 = [{k_: np.asarray(v_) for k_, v_ in r.items()} for r in p1]
        daT = run_da(p1, inp, l)
        ft = run_ft(p1)
        gof, gob = run_gla(p1, inp, l)
        xT_glob = run_p3(xT_glob, daT, ft, gof, gob, p1, modsT, inp, l)
    out = np.ascontiguousarray(xT_glob[:, CTX:].T)[None].astype(np.float32)
    return out
```

```python
import numpy as np
import ml_dtypes
import concourse.bass as bass
import concourse.mybir as mybir
from concourse.bass_utils import run_bass_kernel_spmd

F32 = mybir.dt.float32
BF16 = mybir.dt.bfloat16
ALU = mybir.AluOpType
AF = mybir.ActivationFunctionType

NCORES = 8
D = 4096
SEQ = 8192
CTX = 256
DEPTH = 2
KC = D // 128
TL = SEQ // NCORES
TC = CTX // NCORES
T1 = TC + TL
IN_W = 10272
DFF = 11008
EPS = 1e-6


class Buf:
    __slots__ = ("name", "w", "r")

    def __init__(self, name=""):
        self.name = name
        self.w = None
        self.r = {}


class Eng:
    def __init__(self, name):
        self.name = name
        self.prog = []
        self.count = 0
        self.seen = {}
        self.sem = None


class K:
    def __init__(self, nc, n_dma_sems=10):
        self.nc = nc
        self.pe = Eng("tensor")
        self.act = Eng("scalar")
        self.dve = Eng("vector")
        self.pool = Eng("gpsimd")
        self.sp = Eng("sync")
        self.engs = [self.pe, self.act, self.dve, self.pool, self.sp]
        for e in self.engs:
            e.sem = nc.alloc_semaphore(name="c_" + e.name)
        self.dma_pool = {}
        for e in (self.sp, self.pool, self.act):
            lst = []
            for i in range(n_dma_sems):
                lst.append([nc.alloc_semaphore(name=f"d_{e.name}{i}"), 0])
            self.dma_pool[e.name] = [lst, 0]
        self.out_events = []
        self._n = 0

    def sb(self, shape, dt, name=None):
        self._n += 1
        return self.nc.alloc_sbuf_tensor(name or f"sb{self._n}", list(shape), dt).ap()

    def ps(self, shape, dt=F32, name=None):
        self._n += 1
        return self.nc.alloc_psum_tensor(name or f"ps{self._n}", list(shape), dt).ap()

    def _wait(self, E, ev):
        if ev is None:
            return
        sem, val = ev
        k = id(sem)
        if E.seen.get(k, 0) >= val:
            return
        E.seen[k] = val
        E.prog.append(lambda eng, sem=sem, val=val: eng.wait_ge(sem, val))

    def _deps(self, E, reads, writes):
        for b in reads:
            self._wait(E, b.w)
        for b in writes:
            self._wait(E, b.w)
            for ev in list(b.r.values()):
                self._wait(E, ev)

    def _commit(self, ev, reads, writes):
        for b in reads:
            b.r[id(ev[0])] = ev
        for b in writes:
            b.w = ev
            b.r = {}

    def op(self, E, fn, reads=(), writes=()):
        self._deps(E, reads, writes)
        E.count += 1
        cnt = E.count
        sem = E.sem
        E.prog.append(lambda eng, fn=fn, sem=sem: fn(eng).then_inc(sem, 1))
        ev = (sem, cnt)
        if E is self.pe:
            E.seen[id(sem)] = cnt
        self._commit(ev, reads, writes)
        return ev

    def dma(self, E, out_ap, in_ap, reads=(), writes=(), is_output=False, **kw):
        self._deps(E, reads, writes)
        pool, idx = self.dma_pool[E.name]
        slot = pool[idx % len(pool)]
        self.dma_pool[E.name][1] = idx + 1
        sem, uses = slot
        if uses > 0:
            self._wait(E, (sem, 16 * uses))
        slot[1] = uses + 1
        val = 16 * (uses + 1)
        E.prog.append(lambda eng, o=out_ap, i=in_ap, sem=sem, kw=kw:
                      eng.dma_start(out=o, in_=i, **kw).then_inc(sem, 16))
        ev = (sem, val)
        self._commit(ev, reads, writes)
        if is_output:
            self.out_events.append(ev)
        return ev

    def finish(self):
        for ev in self.out_events:
            self._wait(self.sp, ev)
        with self.nc.Block() as block:
            for E in self.engs:
                if not E.prog:
                    continue

                def body(eng, E=E):
                    for c in E.prog:
                        c(eng)
                getattr(block, E.name)(body)


class Ring:
    def __init__(self, items):
        self.items = items
        self.i = 0

    def next(self):
        it = self.items[self.i % len(self.items)]
        self.i += 1
        return it


def blocks_of(total, nblk):
    base = -(-total // nblk)
    out, s = [], 0
    while s < total:
        w = min(base, total - s)
        out.append((s, w))
        s += w
    return out


def split_ranges(blk, ranges):
    s, w = blk
    res = []
    for (rs, re_, tag) in ranges:
        a, b = max(s, rs), min(s + w, re_)
        if a < b:
            res.append((a, b, tag))
    return res


MODW = 6 * D // NCORES


def build_p0():
    nc = bass.Bass("TRN2", target_bir_lowering=False)
    cs = nc.dram_tensor("cs", [128, KC, 2], F32, kind="ExternalInput").ap()
    wa = nc.dram_tensor("wa", [DEPTH, D, MODW], F32, kind="ExternalInput").ap()
    ba = nc.dram_tensor("ba", [DEPTH, MODW], F32, kind="ExternalInput").ap()
    out = nc.dram_tensor("mods", [DEPTH, 2, MODW], F32, kind="ExternalOutput").ap()
    k = K(nc)
    cs_sb = k.sb([128, KC, 2], F32)
    css = k.sb([128, KC, 2], F32)
    Bcs, Bcss = Buf(), Buf()
    k.dma(k.sp, cs_sb, cs, writes=[Bcs])
    k.op(k.act, lambda e: e.activation(out=css, in_=cs_sb, func=AF.Silu), reads=[Bcs], writes=[Bcss])
    wring = Ring([(k.sb([128, 4, 512], F32), Buf()) for _ in range(4)])
    pring = Ring([(k.ps([128, 512]), Buf()) for _ in range(2)])
    bring = Ring([(k.sb([2, 512], F32), Buf()) for _ in range(2)])
    oring = Ring([(k.sb([2, 512], F32), Buf()) for _ in range(2)])
    nq = 0
    for l in range(DEPTH):
        for cb in range(MODW // 512):
            c0 = cb * 512
            ps, Bp = pring.next()
            for kg in range(KC // 4):
                wt, Bw = wring.next()
                q = k.sp if nq % 2 == 0 else k.act
                nq += 1
                k.dma(q, wt, wa[l, kg * 512:(kg + 1) * 512, c0:c0 + 512].rearrange("(kc p) n -> p kc n", p=128),
                      writes=[Bw])
                for j in range(4):
                    kc = kg * 4 + j
                    k.op(k.pe, lambda e, ps=ps, wt=wt, j=j, kc=kc: e.matmul(
                        ps[0:2, :], lhsT=css[:, kc, :], rhs=wt[:, j, :], start=(kc == 0), stop=(kc == KC - 1)),
                        reads=[Bcss, Bw], writes=[Bp])
            bt, Bb = bring.next()
            for s in range(2):
                k.dma(k.sp, bt[s:s + 1, :], ba[l:l + 1, c0:c0 + 512], writes=[Bb])
            ot, Bo = oring.next()
            k.op(k.dve, lambda e, ot=ot, ps=ps, bt=bt: e.tensor_tensor(out=ot, in0=ps[0:2, :], in1=bt, op=ALU.add),
                 reads=[Bp, Bb], writes=[Bo])
            k.dma(k.sp, out[l, :, c0:c0 + 512], ot, reads=[Bo], is_output=True)
    k.finish()
    return nc


def run_p0(inp):
    cs = np.stack([inp["c"][0], inp["c_ctx"]], axis=-1)
    cs = np.ascontiguousarray(cs.reshape(KC, 128, 2).transpose(1, 0, 2))
    nc = build_p0()
    in_maps = []
    for i in range(NCORES):
        sl = slice(i * MODW, (i + 1) * MODW)
        in_maps.append({"cs": cs,
                        "wa": np.ascontiguousarray(inp["w_ada"][:, :, sl]),
                        "ba": np.ascontiguousarray(inp["b_ada"][:, sl])})
    res = run_bass_kernel_spmd(nc, in_maps, core_ids=list(range(NCORES)))
    mods = np.concatenate([r["mods"] for r in res.results], axis=-1)
    return mods


def mods_layout(mods_l):
    return np.ascontiguousarray(mods_l.reshape(2, 6, KC, 128).transpose(3, 0, 1, 2))


def vec_layout(v):
    return np.ascontiguousarray(v.reshape(-1, 128).T)


BLK1 = blocks_of(T1, 3)
RNG1 = [(0, TC, 1), (TC, T1, 0)]
FM_GROUPS = [(0, 4096, "qk"), (6144, 1024, "z"), (7168, 1024, "gqk"), (9216, 32, "gd")]
TM_GROUPS = [(4096, 2048, "v"), (7680, 512, "gk"), (8192, 1024, "gv"), (9248, 1024, "r")]


def build_p1():
    nc = bass.Bass("TRN2", target_bir_lowering=False)
    dt = nc.dram_tensor
    xT = dt("xT", [D, T1], F32, kind="ExternalInput").ap()
    w = dt("w", [D, IN_W], F32, kind="ExternalInput").ap()
    modsT = dt("modsT", [128, 2, 6, KC], F32, kind="ExternalInput").ap()
    g1 = dt("g1", [128, KC], F32, kind="ExternalInput").ap()
    qkg = dt("qkg", [128, 2], F32, kind="ExternalInput").ap()
    cosT = dt("cosT", [128, T1], F32, kind="ExternalInput").ap()
    sinT = dt("sinT", [128, T1], F32, kind="ExternalInput").ap()
    onesd = dt("ones", [128, 128], F32, kind="ExternalInput").ap()
    rotd = dt("rot", [128, 128], F32, kind="ExternalInput").ap()
    o_qk = dt("o_qk", [32, 128, T1], BF16, kind="ExternalOutput").ap()
    o_z = dt("o_z", [8, 128, T1], BF16, kind="ExternalOutput").ap()
    o_gqk = dt("o_gqk", [8, 128, T1], F32, kind="ExternalOutput").ap()
    o_gd = dt("o_gd", [32, T1], F32, kind="ExternalOutput").ap()
    o_v = dt("o_v", [T1, 2048], BF16, kind="ExternalOutput").ap()
    o_gk = dt("o_gk", [T1, 512], F32, kind="ExternalOutput").ap()
    o_gv = dt("o_gv", [T1, 1024], BF16, kind="ExternalOutput").ap()
    o_r = dt("o_r", [T1, 1024], F32, kind="ExternalOutput").ap()

    k = K(nc)
    mods = k.sb([128, 2, 6, KC], F32); Bmods = Buf()
    g1s = k.sb([128, KC], F32); Bg1 = Buf()
    qkgs = k.sb([128, 2], F32); Bqkg = Buf()
    cosS = k.sb([128, T1], F32); Bcos = Buf()
    sinS = k.sb([128, T1], F32); Bsin = Buf()
    ones = k.sb([128, 128], F32); Bones = Buf()
    rot = k.sb([128, 128], F32); Brot = Buf()
    for (d_, s_, b_) in [(mods, modsT, Bmods), (g1s, g1, Bg1), (qkgs, qkg, Bqkg), (cosS, cosT, Bcos),
                         (sinS, sinT, Bsin), (ones, onesd, Bones), (rot, rotd, Brot)]:
        k.dma(k.sp, d_, s_, writes=[b_])
    a1 = k.sb([128, 2, KC], F32); Ba1 = Buf()
    for s in range(2):
        k.op(k.dve, lambda e, s=s: e.scalar_tensor_tensor(out=a1[:, s, :], in0=mods[:, s, 1, :], scalar=1.0, in1=g1s,
                                                          op0=ALU.add, op1=ALU.mult),
             reads=[Bmods, Bg1], writes=[Ba1])

    psr = Ring([(k.ps([128, 512]), Buf()) for _ in range(6)])
    ps_ss = (k.ps([128, 512]), Buf())
    ps_rot = (k.ps([128, 512]), Buf())

    xr = Ring([(k.sb([128, T1], F32), Buf()) for _ in range(2)])
    sqr = Ring([(k.sb([128, T1], F32), Buf()) for _ in range(2)])
    ssb = [psr.next() for _ in range(3)]
    for t in range(KC):
        xt, Bx = xr.next()
        k.dma(k.sp, xt, xT[t * 128:(t + 1) * 128, :], writes=[Bx])
        sq, Bsq = sqr.next()
        k.op(k.act, lambda e, sq=sq, xt=xt: e.activation(out=sq, in_=xt, func=AF.Square), reads=[Bx], writes=[Bsq])
        for bi, (s0, wd) in enumerate(BLK1):
            k.op(k.pe, lambda e, bi=bi, s0=s0, wd=wd, sq=sq, t=t: e.matmul(
                ssb[bi][0][:, 0:wd], lhsT=ones, rhs=sq[:, s0:s0 + wd], start=(t == 0), stop=(t == KC - 1)),
                reads=[Bones, Bsq], writes=[ssb[bi][1]])
    rstd = k.sb([128, T1], F32); Brstd = Buf()
    epsb = k.sb([128, 1], F32); Beps = Buf()
    k.op(k.dve, lambda e: e.memset(epsb, EPS), writes=[Beps])
    for bi, (s0, wd) in enumerate(BLK1):
        k.op(k.act, lambda e, bi=bi, s0=s0, wd=wd: e.activation(out=rstd[:, s0:s0 + wd], in_=ssb[bi][0][:, 0:wd],
                                                                func=AF.Sqrt, bias=epsb, scale=1.0 / D),
             reads=[ssb[bi][1], Beps], writes=[Brstd])
    k.op(k.dve, lambda e: e.reciprocal(out=rstd, in_=rstd), reads=[Brstd], writes=[Brstd])

    hT = k.sb([128, KC, T1], BF16)
    Bh = [Buf() for _ in range(KC)]
    for t in range(KC):
        xt, Bx = xr.next()
        k.dma(k.sp, xt, xT[t * 128:(t + 1) * 128, :], writes=[Bx])
        tmp, Bt = sqr.next()
        k.op(k.dve, lambda e, tmp=tmp, xt=xt: e.tensor_tensor(out=tmp, in0=xt, in1=rstd, op=ALU.mult),
             reads=[Bx, Brstd], writes=[Bt])
        for (a, b, s) in RNG1:
            k.op(k.dve, lambda e, a=a, b=b, s=s, t=t, tmp=tmp: e.tensor_scalar(
                out=hT[:, t, a:b], in0=tmp[:, a:b], scalar1=a1[:, s, t:t + 1], scalar2=mods[:, s, 0, t:t + 1],
                op0=ALU.mult, op1=ALU.add), reads=[Bt, Ba1, Bmods], writes=[Bh[t]])

    wr = Ring([(k.sb([128, KC, 512], BF16), Buf()) for _ in range(2)])

    panel_list = []
    for (g0, gw, kind) in FM_GROUPS + TM_GROUPS:
        for c0 in range(g0, g0 + gw, 512):
            panel_list.append((c0, min(512, g0 + gw - c0)))
    loaded = {}
    nxt = [0]

    def prefetch():
        if nxt[0] < len(panel_list):
            c0, pw = panel_list[nxt[0]]
            nxt[0] += 1
            wt, Bw = wr.next()
            k.dma(k.pool, wt[:, :, 0:pw], w[:, c0:c0 + pw].rearrange("(kc p) n -> p kc n", p=128), writes=[Bw])
            loaded[c0] = (wt, Bw)

    def load_panel(c0, pw):
        if c0 not in loaded:
            prefetch()
        res = loaded.pop(c0)
        prefetch()
        return res

    t_sq = Ring([(k.sb([128, 352], F32), Buf()) for _ in range(3)])
    t_sd = Ring([(k.sb([128, 352], F32), Buf()) for _ in range(2)])
    t_qn = Ring([(k.sb([128, 352], F32), Buf()) for _ in range(3)])
    t_a = Ring([(k.sb([128, 352], F32), Buf()) for _ in range(2)])
    t_b = Ring([(k.sb([128, 352], F32), Buf()) for _ in range(2)])
    o16 = Ring([(k.sb([128, T1], BF16), Buf()) for _ in range(2)])
    o32 = Ring([(k.sb([128, T1], F32), Buf()) for _ in range(2)])

    tile_idx = {"qk": 0, "z": 0, "gqk": 0}
    pending = {}

    def add_hook(kc, fn):
        pending.setdefault(kc, []).append(fn)

    def qk_epilogue(ti, banks):
        gi = 0 if ti < 16 else 1
        ot, Bo = o16.next()
        st = {}
        for bi, (s0, wd) in enumerate(BLK1):
            pb, Bp = banks[bi]
            sq, Bsq = t_sq.next()
            k.op(k.act, lambda e, sq=sq, pb=pb, wd=wd: e.activation(out=sq[:, 0:wd], in_=pb[:, 0:wd], func=AF.Square),
                 reads=[Bp], writes=[Bsq])
            st[bi] = (sq, Bsq)

            def stage1(bi=bi, s0=s0, wd=wd, pb=pb, Bp=Bp):
                sq, Bsq = st[bi]
                k.op(k.pe, lambda e, sq=sq, wd=wd: e.matmul(ps_ss[0][:, 0:wd], lhsT=ones, rhs=sq[:, 0:wd], start=True, stop=True),
                     reads=[Bones, Bsq], writes=[ps_ss[1]])
                sd, Bsd = t_sd.next()
                k.op(k.act, lambda e, sd=sd, wd=wd: e.activation(out=sd[:, 0:wd], in_=ps_ss[0][:, 0:wd], func=AF.Sqrt,
                                                                 bias=epsb, scale=1.0 / 128),
                     reads=[ps_ss[1], Beps], writes=[Bsd])
                k.op(k.dve, lambda e, sd=sd, wd=wd: e.reciprocal(out=sd[:, 0:wd], in_=sd[:, 0:wd]), reads=[Bsd], writes=[Bsd])
                qn, Bqn = t_qn.next()
                k.op(k.dve, lambda e, qn=qn, pb=pb, sd=sd, wd=wd, gi=gi: e.scalar_tensor_tensor(
                    out=qn[:, 0:wd], in0=pb[:, 0:wd], scalar=qkgs[:, gi:gi + 1], in1=sd[:, 0:wd], op0=ALU.mult, op1=ALU.mult),
                    reads=[Bp, Bsd, Bqkg], writes=[Bqn])
                st[("qn", bi)] = (qn, Bqn)

            def stage2(bi=bi, s0=s0, wd=wd):
                qn, Bqn = st[("qn", bi)]
                k.op(k.pe, lambda e, qn=qn, wd=wd: e.matmul(ps_rot[0][:, 0:wd], lhsT=rot, rhs=qn[:, 0:wd], start=True, stop=True),
                     reads=[Brot, Bqn], writes=[ps_rot[1]])
                ta, Bta = t_a.next()
                k.op(k.pool, lambda e, ta=ta, qn=qn, s0=s0, wd=wd: e.tensor_tensor(out=ta[:, 0:wd], in0=qn[:, 0:wd], in1=cosS[:, s0:s0 + wd], op=ALU.mult),
                     reads=[Bqn, Bcos], writes=[Bta])
                tb, Btb = t_b.next()
                k.op(k.dve, lambda e, tb=tb, s0=s0, wd=wd: e.tensor_tensor(out=tb[:, 0:wd], in0=ps_rot[0][:, 0:wd], in1=sinS[:, s0:s0 + wd], op=ALU.mult),
                     reads=[ps_rot[1], Bsin], writes=[Btb])
                k.op(k.pool, lambda e, ta=ta, tb=tb, ot=ot, s0=s0, wd=wd: e.tensor_tensor(out=ot[:, s0:s0 + wd], in0=ta[:, 0:wd], in1=tb[:, 0:wd], op=ALU.add),
                     reads=[Bta, Btb], writes=[Bo])
                if bi == len(BLK1) - 1:
                    k.dma(k.sp, o_qk[ti], ot, reads=[Bo], is_output=True)
            add_hook(6 + 3 * bi, stage1)
            add_hook(16 + 4 * bi, stage2)

    def flush_hooks(hooks, upto=None):
        for kc in sorted(hooks):
            if upto is not None and kc != upto:
                continue
            for fn in hooks[kc]:
                fn()

    for (g0, gw, kind) in FM_GROUPS:
        for c0 in range(g0, g0 + gw, 512):
            pw = min(512, g0 + gw - c0)
            wt, Bw = load_panel(c0, pw)
            for m0 in range(0, pw, 128):
                mw = min(128, pw - m0)
                banks = [psr.next() for _ in BLK1]
                cur = pending
                pending = {}
                for kc in range(KC):
                    for bi, (s0, wd) in enumerate(BLK1):
                        k.op(k.pe, lambda e, bi=bi, s0=s0, wd=wd, kc=kc, m0=m0, mw=mw, wt=wt, banks=banks: e.matmul(
                            banks[bi][0][0:mw, 0:wd], lhsT=wt[:, kc, m0:m0 + mw], rhs=hT[:, kc, s0:s0 + wd],
                            start=(kc == 0), stop=(kc == KC - 1)), reads=[Bw, Bh[kc]], writes=[banks[bi][1]])
                    if kc in cur:
                        flush_hooks(cur, upto=kc)
                if kind == "qk":
                    ti = tile_idx["qk"]; tile_idx["qk"] += 1
                    qk_epilogue(ti, banks)
                else:
                    use16 = (kind == "z")
                    ot, Bo = (o16 if use16 else o32).next()
                    for bi, (s0, wd) in enumerate(BLK1):
                        pb, Bp = banks[bi]
                        eng = k.act if bi % 2 == 0 else k.dve
                        if eng is k.act:
                            k.op(eng, lambda e, ot=ot, pb=pb, s0=s0, wd=wd, mw=mw: e.activation(out=ot[0:mw, s0:s0 + wd], in_=pb[0:mw, 0:wd], func=AF.Copy),
                                 reads=[Bp], writes=[Bo])
                        else:
                            k.op(eng, lambda e, ot=ot, pb=pb, s0=s0, wd=wd, mw=mw: e.tensor_copy(out=ot[0:mw, s0:s0 + wd], in_=pb[0:mw, 0:wd]),
                                 reads=[Bp], writes=[Bo])
                    if kind == "gd":
                        k.dma(k.sp, o_gd, ot[0:32, :], reads=[Bo], is_output=True)
                    else:
                        ti = tile_idx[kind]; tile_idx[kind] += 1
                        k.dma(k.sp, (o_z if kind == "z" else o_gqk)[ti], ot, reads=[Bo], is_output=True)
    flush_hooks(pending)
    pending = {}

    e16 = Ring([(k.sb([128, 512], BF16), Buf()) for _ in range(3)])
    e32 = Ring([(k.sb([128, 512], F32), Buf()) for _ in range(3)])
    ttiles = [(s, min(128, T1 - s)) for s in range(0, T1, 128)]
    outs = {"v": (o_v, True), "gk": (o_gk, False), "gv": (o_gv, True), "r": (o_r, False)}
    ne = 0
    for (g0, gw, kind) in TM_GROUPS:
        od, is16 = outs[kind]
        for c0 in range(g0, g0 + gw, 512):
            wt, Bw = load_panel(c0, 512)
            for (s0, mt) in ttiles:
                pb, Bp = psr.next()
                for kc in range(KC):
                    k.op(k.pe, lambda e, pb=pb, kc=kc, s0=s0, mt=mt, wt=wt: e.matmul(
                        pb[0:mt, :], lhsT=hT[:, kc, s0:s0 + mt], rhs=wt[:, kc, :], start=(kc == 0), stop=(kc == KC - 1)),
                        reads=[Bw, Bh[kc]], writes=[Bp])
                et, Be = (e16 if is16 else e32).next()
                ne += 1
                if ne % 2 == 0:
                    k.op(k.act, lambda e, et=et, pb=pb, mt=mt: e.activation(out=et[0:mt, :], in_=pb[0:mt, :], func=AF.Copy), reads=[Bp], writes=[Be])
                else:
                    k.op(k.dve, lambda e, et=et, pb=pb, mt=mt: e.tensor_copy(out=et[0:mt, :], in_=pb[0:mt, :]), reads=[Bp], writes=[Be])
                k.dma(k.sp, od[s0:s0 + mt, c0 - g0:c0 - g0 + 512], et[0:mt, :], reads=[Be], is_output=True)
    k.finish()
    return nc


def rope_tables():
    half = 64
    freqs = (10000.0 ** (-np.arange(0, half, 2, dtype=np.float32) / half)).astype(np.float32)
    t = np.arange(SEQ)
    ar = (t // 64).astype(np.float32)[:, None] * freqs
    ac = (t % 64).astype(np.float32)[:, None] * freqs
    ang = np.concatenate([ar, ar, ac, ac], axis=-1)
    return np.cos(ang).astype(np.float32), np.sin(ang).astype(np.float32)


def rot_matrix():
    R = np.zeros((128, 128), np.float32)
    for j in range(32):
        R[32 + j, j] = -1.0
        R[j, 32 + j] = 1.0
        R[96 + j, 64 + j] = -1.0
        R[64 + j, 96 + j] = 1.0
    return R


def run_p1(xT_cores, w_in_l, modsT, g1n, qg, kg):
    cos, sin = rope_tables()
    nc = build_p1()
    ones = np.ones((128, 128), np.float32)
    rot = rot_matrix()
    g1 = vec_layout(g1n)
    qkg = np.ascontiguousarray(np.stack([qg, kg], axis=-1))
    in_maps = []
    for i in range(NCORES):
        cT = np.concatenate([np.ones((128, TC), np.float32), cos[i * TL:(i + 1) * TL].T], axis=1)
        sT = np.concatenate([np.zeros((128, TC), np.float32), sin[i * TL:(i + 1) * TL].T], axis=1)
        in_maps.append({"xT": xT_cores[i], "w": w_in_l, "modsT": modsT, "g1": g1, "qkg": qkg,
                        "cosT": np.ascontiguousarray(cT), "sinT": np.ascontiguousarray(sT), "ones": ones, "rot": rot})
    res = run_bass_kernel_spmd(nc, in_maps, core_ids=list(range(NCORES)))
    return res.results


NTOK = CTX + SEQ
NKT = NTOK // 128


def build_da():
    nc = bass.Bass("TRN2", target_bir_lowering=False)
    dt = nc.dram_tensor
    qTd = dt("qT", [2, 128, NTOK], BF16, kind="ExternalInput").ap()
    kTd = dt("kT", [2, 128, NTOK], BF16, kind="ExternalInput").ap()
    vd = dt("vaug", [NTOK, 257], BF16, kind="ExternalInput").ap()
    lamv = dt("lamv", [128, 4], F32, kind="ExternalInput").ap()
    lamc = dt("lamc", [128, 2], F32, kind="ExternalInput").ap()
    gsd = dt("gs", [128, 256], F32, kind="ExternalInput").ap()
    onesd = dt("ones", [128, 128], F32, kind="ExternalInput").ap()
    identd = dt("ident", [128, 128], F32, kind="ExternalInput").ap()
    outd = dt("daT", [2, 128, NTOK], BF16, kind="ExternalOutput").ap()
    k = K(nc)
    qT = k.sb([128, 2, NTOK], BF16); Bq = Buf()
    kT = k.sb([128, 2, NTOK], BF16); Bk = Buf()
    va = k.sb([128, NKT, 257], BF16); Bv = Buf()
    for m in range(2):
        k.dma(k.sp, qT[:, m, :], qTd[m], writes=[Bq])
        k.dma(k.sp, kT[:, m, :], kTd[m], writes=[Bk])
    for g in range(0, NKT, 11):
        k.dma(k.sp, va[:, g:g + 11, :], vd[g * 128:(g + 11) * 128, :].rearrange("(kt p) c -> p kt c", p=128), writes=[Bv])
    lv = k.sb([128, 4], F32); Blv = Buf()
    lc = k.sb([128, 2], F32); Blc = Buf()
    gs = k.sb([128, 256], F32); Bgs = Buf()
    ones = k.sb([128, 128], F32); Bones = Buf()
    ident = k.sb([128, 128], F32); Bid = Buf()
    for (d_, s_, b_) in [(lv, lamv, Blv), (lc, lamc, Blc), (gs, gsd, Bgs), (ones, onesd, Bones), (ident, identd, Bid)]:
        k.dma(k.sp, d_, s_, writes=[b_])
    epsb = k.sb([128, 1], F32); Beps = Buf()
    k.op(k.dve, lambda e: e.memset(epsb, EPS), writes=[Beps])
    pr = k.sb([128, 2], F32); Bpr = Buf()
    k.op(k.dve, lambda e: e.tensor_tensor(out=pr[:, 0:1], in0=lv[:, 0:1], in1=lv[:, 1:2], op=ALU.mult), reads=[Blv], writes=[Bpr])
    k.op(k.dve, lambda e: e.tensor_tensor(out=pr[:, 1:2], in0=lv[:, 2:3], in1=lv[:, 3:4], op=ALU.mult), reads=[Blv, Bpr], writes=[Bpr])
    ps_t = (k.ps([128, 512]), Buf())
    k.op(k.pe, lambda e: e.matmul(ps_t[0][:, 0:2], lhsT=ones, rhs=pr, start=True, stop=True), reads=[Bones, Bpr], writes=[ps_t[1]])
    ex = k.sb([128, 2], F32); Bex = Buf()
    k.op(k.act, lambda e: e.activation(out=ex, in_=ps_t[0][:, 0:2], func=AF.Exp), reads=[ps_t[1]], writes=[Bex])
    neglam = k.sb([128, 1], F32); Bnl = Buf()
    k.op(k.dve, lambda e: e.tensor_tensor(out=neglam, in0=ex[:, 1:2], in1=ex[:, 0:1], op=ALU.subtract), reads=[Bex], writes=[Bnl])
    k.op(k.dve, lambda e: e.tensor_tensor(out=neglam, in0=neglam, in1=lc[:, 0:1], op=ALU.subtract), reads=[Blc, Bnl], writes=[Bnl])
    gss = k.sb([128, 256], F32); Bgss = Buf()
    k.op(k.dve, lambda e: e.tensor_scalar(out=gss, in0=gs, scalar1=lc[:, 1:2], scalar2=None, op0=ALU.mult), reads=[Bgs, Blc], writes=[Bgss])

    ps_s = Ring([(k.ps([128, 512]), Buf()) for _ in range(3)])
    acc = [[(k.ps([128, 512]), Buf()) for _ in range(2)] for _ in range(2)]
    pT = Ring([(k.sb([128, 2, 256], BF16), Buf()) for _ in range(4)])
    sc = 128.0 ** -0.5
    r12 = Ring([(k.sb([128, 2], F32), Buf()) for _ in range(4)])
    o1r = Ring([(k.sb([128, 256], F32), Buf()) for _ in range(2)])
    orr = Ring([(k.sb([128, 256], F32), Buf()) for _ in range(2)])
    sqr = Ring([(k.sb([128, 256], F32), Buf()) for _ in range(2)])
    ssr = Ring([(k.sb([128, 1], F32), Buf()) for _ in range(4)])
    sdr = Ring([(k.sb([128, 2], F32), Buf()) for _ in range(4)])
    yr = Ring([(k.sb([128, 256], F32), Buf()) for _ in range(4)])
    otr = Ring([(k.sb([128, 2, 256], BF16), Buf()) for _ in range(2)])

    qblocks = [(0, list(range(2)))] + [(CTX + 256 * b, list(range(NKT))) for b in range(SEQ // 256)]
    steps = []
    for bi, (q0, kts) in enumerate(qblocks):
        for ki, kt in enumerate(kts):
            steps.append((bi, q0, ki, kt, len(kts)))
    LOOK = 2
    sbuf_of = {}

    def issue_S(si):
        bi, q0, ki, kt, n = steps[si]
        sb_, Bs = ps_s.next()
        pt, Bpt = pT.next()
        for m in range(2):
            k.op(k.pe, lambda e, sb_=sb_, m=m, kt=kt, q0=q0: e.matmul(
                sb_[:, m * 256:(m + 1) * 256], lhsT=kT[:, m, kt * 128:(kt + 1) * 128], rhs=qT[:, m, q0:q0 + 256],
                start=True, stop=True), reads=[Bk, Bq], writes=[Bs])
        k.op(k.act, lambda e, pt=pt, sb_=sb_: e.activation(out=pt.rearrange("p a b -> p (a b)"), in_=sb_, func=AF.Exp, scale=sc),
             reads=[Bs], writes=[Bpt])
        sbuf_of[si] = (pt, Bpt)

    def epilogue_front(q0):
        ys = []
        for qs in range(2):
            a1_, Ba1 = acc[0][qs]
            a2_, Ba2 = acc[1][qs]
            rr, Brr = r12.next()
            k.op(k.dve, lambda e, rr=rr, a1_=a1_: e.reciprocal(out=rr[:, 0:1], in_=a1_[:, 256:257]), reads=[Ba1], writes=[Brr])
            k.op(k.dve, lambda e, rr=rr, a2_=a2_: e.reciprocal(out=rr[:, 1:2], in_=a2_[:, 256:257]), reads=[Ba2, Brr], writes=[Brr])
            k.op(k.dve, lambda e, rr=rr: e.tensor_tensor(out=rr[:, 1:2], in0=rr[:, 1:2], in1=neglam, op=ALU.mult), reads=[Brr, Bnl], writes=[Brr])
            o1, Bo1 = o1r.next()
            k.op(k.act, lambda e, o1=o1, a1_=a1_, rr=rr: e.activation(out=o1, in_=a1_[:, 0:256], func=AF.Copy, scale=rr[:, 0:1]),
                 reads=[Ba1, Brr], writes=[Bo1])
            oo, Boo = orr.next()
            k.op(k.dve, lambda e, oo=oo, a2_=a2_, rr=rr, o1=o1: e.scalar_tensor_tensor(
                out=oo, in0=a2_[:, 0:256], scalar=rr[:, 1:2], in1=o1, op0=ALU.mult, op1=ALU.add), reads=[Ba2, Brr, Bo1], writes=[Boo])
            sq, Bsq = sqr.next()
            ss, Bss = ssr.next()
            k.op(k.act, lambda e, sq=sq, oo=oo, ss=ss: e.activation(out=sq, in_=oo, func=AF.Square, accum_out=ss), reads=[Boo], writes=[Bsq, Bss])
            sd, Bsd = sdr.next()
            k.op(k.act, lambda e, ss=ss, sd=sd: e.activation(out=sd[:, 0:1], in_=ss, func=AF.Sqrt, bias=epsb, scale=1.0 / 256), reads=[Bss, Beps], writes=[Bsd])
            k.op(k.dve, lambda e, sd=sd: e.reciprocal(out=sd[:, 1:2], in_=sd[:, 0:1]), reads=[Bsd], writes=[Bsd])
            y, By = yr.next()
            k.op(k.dve, lambda e, y=y, oo=oo, sd=sd: e.scalar_tensor_tensor(out=y, in0=oo, scalar=sd[:, 1:2], in1=gss, op0=ALU.mult, op1=ALU.mult),
                 reads=[Boo, Bsd, Bgss], writes=[By])
            ys.append((y, By))

        def back():
            ot, Bot = otr.next()
            for qs in range(2):
                y, By = ys[qs]
                for f in range(2):
                    k.op(k.pe, lambda e, y=y, f=f: e.transpose(out=ps_t[0][:, f * 128:(f + 1) * 128], in_=y[:, f * 128:(f + 1) * 128], identity=ident),
                         reads=[By, Bid], writes=[ps_t[1]])
                k.op(k.dve, lambda e, ot=ot, qs=qs: e.tensor_copy(out=ot[:, :, qs * 128:(qs + 1) * 128],
                                                               in_=ps_t[0][:, 0:256].rearrange("p (f q) -> p f q", f=2)),
                     reads=[ps_t[1]], writes=[Bot])
            for f in range(2):
                k.dma(k.sp, outd[f, :, q0:q0 + 256], ot[:, f, :], reads=[Bot], is_output=True)
        return back

    for si in range(min(LOOK, len(steps))):
        issue_S(si)
    pending = None
    for si, (bi, q0, ki, kt, n) in enumerate(steps):
        if si + LOOK < len(steps):
            issue_S(si + LOOK)
        pt, Bpt = sbuf_of.pop(si)
        for m in range(2):
            for qs in range(2):
                a_, Ba = acc[m][qs]
                k.op(k.pe, lambda e, a_=a_, pt=pt, m=m, qs=qs, kt=kt, ki=ki, n=n: e.matmul(
                    a_[:, 0:257], lhsT=pt[:, m, qs * 128:(qs + 1) * 128], rhs=va[:, kt, :],
                    start=(ki == 0), stop=(ki == n - 1)), reads=[Bpt, Bv], writes=[Ba])
        if pending is not None and (ki == 6 or ki == n - 1):
            pending()
            pending = None
        if ki == n - 1:
            pending = epilogue_front(q0)
    if pending is not None:
        pending()
    k.finish()
    return nc


def to_global_fm(arrs):
    return np.concatenate([a[..., :TC] for a in arrs] + [a[..., TC:] for a in arrs], axis=-1)


def to_global_tm(arrs):
    return np.concatenate([a[:TC] for a in arrs] + [a[TC:] for a in arrs], axis=0)


def lam_init_of(l):
    return float(0.8 - 0.6 * np.exp(-0.3 * l))


def run_da(p1, inp, l):
    qk = to_global_fm([np.asarray(r["o_qk"]) for r in p1])
    v = to_global_tm([np.asarray(r["o_v"]) for r in p1])
    nc = build_da()
    li = lam_init_of(l)
    lamv = np.ascontiguousarray(np.stack([inp["lambda_q1"][l], inp["lambda_k1"][l], inp["lambda_q2"][l], inp["lambda_k2"][l]], -1))
    lamc = np.ascontiguousarray(np.broadcast_to(np.array([li, 1.0 - li], np.float32), (128, 2)))
    gs = np.ascontiguousarray(np.broadcast_to(inp["da_subln_g"][l], (128, 256)))
    ones = np.ones((128, 128), np.float32)
    ident = np.eye(128, dtype=np.float32)
    onecol = np.ones((NTOK, 1), dtype=v.dtype)
    in_maps = []
    for h in range(NCORES):
        in_maps.append({"qT": np.ascontiguousarray(qk[2 * h:2 * h + 2]), "kT": np.ascontiguousarray(qk[16 + 2 * h:16 + 2 * h + 2]),
                        "vaug": np.ascontiguousarray(np.concatenate([v[:, 256 * h:256 * (h + 1)], onecol], axis=1)),
                        "lamv": lamv, "lamc": lamc, "gs": gs, "ones": ones, "ident": ident})
    res = run_bass_kernel_spmd(nc, in_maps, core_ids=list(range(NCORES)))
    return np.concatenate([np.asarray(r["daT"]).reshape(256, NTOK) for r in res.results], axis=0)


def build_ft():
    nc = bass.Bass("TRN2", target_bir_lowering=False)
    dt = nc.dram_tensor
    zTd = dt("zT", [8, 128, NTOK], BF16, kind="ExternalInput").ap()
    csd = dt("cs", [2, 128, 512], BF16, kind="ExternalInput").ap()
    tabd = dt("tab", [2, SEQ, TL], BF16, kind="ExternalInput").ap()
    ctabd = dt("ctab", [2, CTX, TC], BF16, kind="ExternalInput").ap()
    outd = dt("ftT", [8, 128, T1], BF16, kind="ExternalOutput").ap()
    k = K(nc)
    cs = k.sb([128, 2, 512], BF16); Bcs = Buf()
    for kc in range(2):
        k.dma(k.sp, cs[:, kc, :], csd[kc], writes=[Bcs])
    ctab = k.sb([128, 2, 2, TC], BF16); Bct = Buf()
    for c in range(2):
        k.dma(k.sp, ctab[:, c, :, :], ctabd[c].rearrange("(st p) t -> p st t", p=128), writes=[Bct])
    zr = Ring([(k.sb([128, 2, NTOK], BF16), Buf()) for _ in range(2)])
    AB = k.sb([128, NKT, 512], BF16); BAB = Buf()
    psa = Ring([(k.ps([128, 512]), Buf()) for _ in range(3)])
    acc = [[(k.ps([128, 512]), Buf()) for _ in range(2)] for _ in range(2)]
    psc = (k.ps([128, 512]), Buf())
    tr = Ring([(k.sb([128, 2, TL], BF16), Buf()) for _ in range(4)])
    otr = Ring([(k.sb([128, T1], BF16), Buf()) for _ in range(3)])
    ne = 0
    for g in range(4):
        zt, Bz = zr.next()
        for kc in range(2):
            k.dma(k.sp, zt[:, kc, :], zTd[2 * g + kc], writes=[Bz])
        for tt in range(NKT):
            pa, Bpa = psa.next()
            for kc in range(2):
                k.op(k.pe, lambda e, pa=pa, zt=zt, kc=kc, tt=tt: e.matmul(pa, lhsT=zt[:, kc, tt * 128:(tt + 1) * 128], rhs=cs[:, kc, :],
                                                                          start=(kc == 0), stop=(kc == 1)), reads=[Bz, Bcs], writes=[Bpa])
            ne += 1
            if ne % 2:
                k.op(k.act, lambda e, pa=pa, tt=tt: e.activation(out=AB[:, tt, :], in_=pa, func=AF.Copy), reads=[Bpa], writes=[BAB])
            else:
                k.op(k.dve, lambda e, pa=pa, tt=tt: e.tensor_copy(out=AB[:, tt, :], in_=pa), reads=[Bpa], writes=[BAB])
        for st in range(SEQ // 128):
            tb, Btb = tr.next()
            for c in range(2):
                k.dma(k.sp, tb[:, c, :], tabd[c, st * 128:(st + 1) * 128, :], writes=[Btb])
            for c in range(2):
                for ct in range(2):
                    for nb in range(2):
                        a_, Ba = acc[ct][nb]
                        k.op(k.pe, lambda e, a_=a_, st=st, c=c, ct=ct, nb=nb, tb=tb: e.matmul(
                            a_, lhsT=AB[:, 2 + st, c * 256 + ct * 128:c * 256 + (ct + 1) * 128], rhs=tb[:, c, nb * 512:(nb + 1) * 512],
                            start=(st == 0 and c == 0), stop=(st == SEQ // 128 - 1 and c == 1)), reads=[BAB, Btb], writes=[Ba])
        for ct in range(2):
            ot, Bo = otr.next()
            n = 0
            for st in range(2):
                for c in range(2):
                    k.op(k.pe, lambda e, st=st, c=c, ct=ct, n=n: e.matmul(
                        psc[0][:, 0:TC], lhsT=AB[:, st, c * 256 + ct * 128:c * 256 + (ct + 1) * 128], rhs=ctab[:, c, st, :],
                        start=(n == 0), stop=(n == 3)), reads=[BAB, Bct], writes=[psc[1]])
                    n += 1
            k.op(k.dve, lambda e, ot=ot: e.tensor_copy(out=ot[:, 0:TC], in_=psc[0][:, 0:TC]), reads=[psc[1]], writes=[Bo])
            for nb in range(2):
                a_, Ba = acc[ct][nb]
                if nb == 0:
                    k.op(k.act, lambda e, ot=ot, a_=a_, nb=nb: e.activation(out=ot[:, TC + nb * 512:TC + (nb + 1) * 512], in_=a_, func=AF.Copy),
                         reads=[Ba], writes=[Bo])
                else:
                    k.op(k.dve, lambda e, ot=ot, a_=a_, nb=nb: e.tensor_copy(out=ot[:, TC + nb * 512:TC + (nb + 1) * 512], in_=a_),
                         reads=[Ba], writes=[Bo])
            k.dma(k.sp, outd[2 * g + ct], ot, reads=[Bo], is_output=True)
    k.finish()
    return nc


_FT_TABLES = {}


def ft_tables():
    if not _FT_TABLES:
        bf = ml_dtypes.bfloat16
        c = np.arange(256)
        ang = 2 * np.pi * ((c[:, None] * c[None, :]) % 256) / 256.0
        cs = np.concatenate([np.cos(ang) / 16.0, -np.sin(ang) / 16.0], axis=1)
        _FT_TABLES["cs"] = np.ascontiguousarray(cs.reshape(2, 128, 512)).astype(bf)
        _FT_TABLES["ctab"] = np.stack([np.cos(ang) / 16.0, np.sin(ang) / 16.0]).astype(np.float32)
        s = np.arange(SEQ, dtype=np.int64)
        tabs = []
        for i in range(NCORES):
            t = np.arange(i * TL, (i + 1) * TL, dtype=np.int64)
            a = (2 * np.pi / SEQ) * ((s[:, None] * t[None, :]) % SEQ).astype(np.float64)
            sc = 1.0 / np.sqrt(SEQ)
            tabs.append(np.stack([(np.cos(a) * sc).astype(np.float32).astype(bf), (np.sin(a) * sc).astype(np.float32).astype(bf)]))
        _FT_TABLES["tab"] = tabs
    return _FT_TABLES


def run_ft(p1):
    zT = to_global_fm([np.asarray(r["o_z"]) for r in p1])
    T = ft_tables()
    nc = build_ft()
    bf = ml_dtypes.bfloat16
    in_maps = []
    for i in range(NCORES):
        in_maps.append({"zT": zT, "cs": T["cs"], "tab": T["tab"][i],
                        "ctab": np.ascontiguousarray(T["ctab"][:, :, i * TC:(i + 1) * TC]).astype(bf)})
    res = run_bass_kernel_spmd(nc, in_maps, core_ids=list(range(NCORES)))
    return [np.asarray(r["ftT"]).reshape(1024, T1) for r in res.results]


def build_gla():
    nc = bass.Bass("TRN2", target_bir_lowering=False)
    dt = nc.dram_tensor
    qTd = dt("qT", [128, NTOK], F32, kind="ExternalInput").ap()
    kTd = dt("kT", [128, NTOK], F32, kind="ExternalInput").ap()
    ktd = dt("ktok", [NTOK, 128], F32, kind="ExternalInput").ap()
    vd = dt("v", [NTOK, 256], BF16, kind="ExternalInput").ap()
    gdd = dt("gda", [17, NTOK], F32, kind="ExternalInput").ap()
    w2d = dt("w2a", [17, 128], F32, kind="ExternalInput").ap()
    triId = dt("triI", [128, 128], F32, kind="ExternalInput").ap()
    triAd = dt("triA", [128, 128], F32, kind="ExternalInput").ap()
    maskd = dt("maskT", [128, 128], F32, kind="ExternalInput").ap()
    outd = dt("o", [NTOK, 256], F32, kind="ExternalOutput").ap()
    k = K(nc)
    qT = k.sb([128, NTOK], F32); Bq = Buf()
    kT = k.sb([128, NTOK], F32); Bk = Buf()
    kt = k.sb([128, NKT, 128], F32); Bkt = Buf()
    v = k.sb([128, NKT, 256], BF16); Bv = Buf()
    gda = k.sb([17, NTOK], F32); Bgd = Buf()
    w2a = k.sb([17, 128], F32); Bw2 = Buf()
    triI = k.sb([128, 128], F32); BtI = Buf()
    triA = k.sb([128, 128], F32); BtA = Buf()
    mask = k.sb([128, 128], F32); Bm = Buf()
    k.dma(k.sp, gda, gdd, writes=[Bgd])
    for (d_, s_, b_) in [(w2a, w2d, Bw2), (triI, triId, BtI), (triA, triAd, BtA), (mask, maskd, Bm)]:
        k.dma(k.sp, d_, s_, writes=[b_])
    k.dma(k.sp, qT, qTd, writes=[Bq])
    k.dma(k.act, kT, kTd, writes=[Bk])
    for g in range(0, NKT, 22):
        k.dma(k.sp, kt[:, g:g + 22, :], ktd[g * 128:(g + 22) * 128, :].rearrange("(c p) d -> p c d", p=128), writes=[Bkt])
        k.dma(k.act, v[:, g:g + 22, :], vd[g * 128:(g + 22) * 128, :].rearrange("(c p) d -> p c d", p=128), writes=[Bv])
    oneb = k.sb([128, 1], F32); B1 = Buf()
    k.op(k.dve, lambda e: e.memset(oneb, 1.0), writes=[B1])
    S = [(k.sb([128, 256], F32), Buf()) for _ in range(2)]
    Sb = [(k.sb([128, 256], BF16), Buf()) for _ in range(2)]
    k.op(k.dve, lambda e: e.memset(S[0][0], 0.0), writes=[S[0][1]])
    k.op(k.dve, lambda e: e.memset(Sb[0][0], 0.0), writes=[Sb[0][1]])
    p_lg = (k.ps([128, 512]), Buf()); p_cT = (k.ps([128, 512]), Buf()); p_rs = (k.ps([128, 512]), Buf())
    p_AT = (k.ps([128, 512]), Buf()); p_o = Ring([(k.ps([128, 512]), Buf()) for _ in range(2)]); p_kv = (k.ps([128, 512]), Buf())

    def R2(shape, dt_):
        return Ring([(k.sb(shape, dt_), Buf()) for _ in range(2)])
    r_e, r_la, r_E1, r_E2, r_E3 = (R2([128, 128], F32) for _ in range(5))
    r_qi, r_ki, r_ko, r_Am = (R2([128, 128], BF16) for _ in range(4))
    r_o = R2([128, 256], F32)
    sc = 128.0 ** -0.5
    for c in range(NKT):
        cs_ = slice(c * 128, (c + 1) * 128)
        k.op(k.pe, lambda e, cs_=cs_: e.matmul(p_lg[0][:, 0:128], lhsT=gda[:, cs_], rhs=w2a, start=True, stop=True),
             reads=[Bgd, Bw2], writes=[p_lg[1]])
        e_, Be = r_e.next()
        k.op(k.act, lambda e, e_=e_: e.activation(out=e_, in_=p_lg[0][:, 0:128], func=AF.Exp, scale=-1.0), reads=[p_lg[1]], writes=[Be])
        la, Bla = r_la.next()
        k.op(k.act, lambda e, e_=e_, la=la: e.activation(out=la, in_=e_, func=AF.Ln, bias=oneb, scale=1.0), reads=[Be, B1], writes=[Bla])
        k.op(k.pe, lambda e, la=la: e.matmul(p_cT[0][:, 0:128], lhsT=la, rhs=triI, start=True, stop=True), reads=[Bla, BtI], writes=[p_cT[1]])
        k.op(k.pe, lambda e, la=la: e.matmul(p_rs[0][:, 0:128], lhsT=triA, rhs=la, start=True, stop=True), reads=[Bla, BtA], writes=[p_rs[1]])
        E1, BE1 = r_E1.next(); E2, BE2 = r_E2.next(); E3, BE3 = r_E3.next()
        k.op(k.act, lambda e, E1=E1: e.activation(out=E1, in_=p_cT[0][:, 0:128], func=AF.Exp), reads=[p_cT[1]], writes=[BE1])
        k.op(k.act, lambda e, E2=E2: e.activation(out=E2, in_=p_cT[0][:, 0:128], func=AF.Exp, scale=-1.0), reads=[p_cT[1]], writes=[BE2])
        k.op(k.act, lambda e, E3=E3: e.activation(out=E3, in_=p_rs[0][:, 0:128], func=AF.Exp), reads=[p_rs[1]], writes=[BE3])
        qi, Bqi = r_qi.next(); ki, Bki = r_ki.next(); ko, Bko = r_ko.next()
        k.op(k.dve, lambda e, qi=qi, E1=E1, cs_=cs_: e.scalar_tensor_tensor(out=qi, in0=qT[:, cs_], scalar=sc, in1=E1, op0=ALU.mult, op1=ALU.mult),
             reads=[Bq, BE1], writes=[Bqi])
        k.op(k.pool, lambda e, ki=ki, E2=E2, cs_=cs_: e.tensor_tensor(out=ki, in0=kT[:, cs_], in1=E2, op=ALU.mult), reads=[Bk, BE2], writes=[Bki])
        k.op(k.pool, lambda e, ko=ko, E3=E3, c=c: e.tensor_tensor(out=ko, in0=kt[:, c, :], in1=E3, op=ALU.mult), reads=[Bkt, BE3], writes=[Bko])
        k.op(k.pe, lambda e, ki=ki, qi=qi: e.matmul(p_AT[0][:, 0:128], lhsT=ki, rhs=qi, start=True, stop=True), reads=[Bki, Bqi], writes=[p_AT[1]])
        Am, BAm = r_Am.next()
        k.op(k.dve, lambda e, Am=Am: e.tensor_tensor(out=Am, in0=p_AT[0][:, 0:128], in1=mask, op=ALU.mult), reads=[p_AT[1], Bm], writes=[BAm])
        po, Bpo = p_o.next()
        Scur, BScur = S[c % 2]; Snew, BSnew = S[(c + 1) % 2]
        Sbcur, BSbcur = Sb[c % 2]; Sbnew, BSbnew = Sb[(c + 1) % 2]
        k.op(k.pe, lambda e, po=po, Am=Am, c=c: e.matmul(po[:, 0:256], lhsT=Am, rhs=v[:, c, :], start=True, stop=False), reads=[BAm, Bv], writes=[Bpo])
        k.op(k.pe, lambda e, po=po, qi=qi, Sbcur=Sbcur: e.matmul(po[:, 0:256], lhsT=qi, rhs=Sbcur, start=False, stop=True), reads=[Bqi, BSbcur], writes=[Bpo])
        k.op(k.pe, lambda e, ko=ko, c=c: e.matmul(p_kv[0][:, 0:256], lhsT=ko, rhs=v[:, c, :], start=True, stop=True), reads=[Bko, Bv], writes=[p_kv[1]])
        k.op(k.dve, lambda e, Snew=Snew, Scur=Scur, E1=E1: e.scalar_tensor_tensor(out=Snew, in0=Scur, scalar=E1[:, 127:128], in1=p_kv[0][:, 0:256],
                                                                                op0=ALU.mult, op1=ALU.add), reads=[BScur, BE1, p_kv[1]], writes=[BSnew])
        k.op(k.act, lambda e, Sbnew=Sbnew, Snew=Snew: e.activation(out=Sbnew, in_=Snew, func=AF.Copy), reads=[BSnew], writes=[BSbnew])
        ot, Bot = r_o.next()
        k.op(k.act, lambda e, ot=ot, po=po: e.activation(out=ot, in_=po[:, 0:256], func=AF.Copy), reads=[Bpo], writes=[Bot])
        k.dma(k.sp, outd[cs_, :], ot, reads=[Bot], is_output=True)
    k.finish()
    return nc


def gla_perm(direction):
    if direction == 0:
        return np.arange(NTOK)
    return np.concatenate([np.arange(CTX)[::-1], CTX + np.arange(SEQ)[::-1]])


def run_gla(p1, inp, l):
    gqk = to_global_fm([np.asarray(r["o_gqk"]) for r in p1])
    gk = to_global_tm([np.asarray(r["o_gk"]) for r in p1])
    gv = to_global_tm([np.asarray(r["o_gv"]) for r in p1])
    gd = to_global_fm([np.asarray(r["o_gd"]) for r in p1])
    nc = build_gla()
    j = np.arange(128)
    triI = np.where(j[:, None] <= j[None, :], -1.0 / 16, 0.0).astype(np.float32)
    triA = np.where(j[:, None] > j[None, :], -1.0 / 16, 0.0).astype(np.float32)
    maskT = (j[:, None] <= j[None, :]).astype(np.float32)
    onesrow = np.ones((1, NTOK), np.float32)
    in_maps = []
    perms = []
    for i in range(NCORES):
        hh, d = i // 2, i % 2
        pm = gla_perm(d)
        perms.append(pm)
        w2a = np.concatenate([inp["gla_gate_w2"][l, d][:, hh * 128:(hh + 1) * 128], inp["gla_gate_b"][l, d][None, hh * 128:(hh + 1) * 128]], 0)
        in_maps.append({"qT": np.ascontiguousarray(gqk[hh][:, pm]), "kT": np.ascontiguousarray(gqk[4 + hh][:, pm]),
                        "ktok": np.ascontiguousarray(gk[pm, hh * 128:(hh + 1) * 128]),
                        "v": np.ascontiguousarray(gv[pm, hh * 256:(hh + 1) * 256]),
                        "gda": np.ascontiguousarray(np.concatenate([gd[16 * d:16 * (d + 1)][:, pm], onesrow], 0)),
                        "w2a": np.ascontiguousarray(w2a.astype(np.float32)), "triI": triI, "triA": triA, "maskT": maskT})
    res = run_bass_kernel_spmd(nc, in_maps, core_ids=list(range(NCORES)))
    of = np.zeros((NTOK, 1024), np.float32)
    ob = np.zeros((NTOK, 1024), np.float32)
    for i in range(NCORES):
        hh, d = i // 2, i % 2
        o = np.asarray(res.results[i]["o"])
        tgt = of if d == 0 else ob
        tgt[perms[i], hh * 256:(hh + 1) * 256] = o
    return of, ob


T3 = T1 + 4
BLK3 = blocks_of(T3, 3)
RNG3 = [(0, TC + 2, 1), (TC + 2, T3, 0)]
NFT = 2 * DFF // 128
NJ = DFF // 128


def k_barrier(k):
    evs = [(E.sem, E.count) for E in k.engs if E.count > 0]
    for name, (lst, _) in k.dma_pool.items():
        for sem, uses in lst:
            if uses > 0:
                evs.append((sem, 16 * uses))
    for E in k.engs:
        for ev in evs:
            if ev[0] is E.sem:
                continue
            k._wait(E, ev)


def build_p3():
    nc = bass.Bass("TRN2", target_bir_lowering=False)
    dt = nc.dram_tensor
    xTd = dt("xT", [D, T3], F32, kind="ExternalInput").ap()
    mixd = dt("mixT", [3072, T3], BF16, kind="ExternalInput").ap()
    gofd = dt("gof", [T3, 1024], F32, kind="ExternalInput").ap()
    gobd = dt("gob", [T3, 1024], F32, kind="ExternalInput").ap()
    rd = dt("r", [T3, 1024], F32, kind="ExternalInput").ap()
    modsd = dt("modsT", [128, 2, 6, KC], F32, kind="ExternalInput").ap()
    g2d = dt("g2", [128, KC], F32, kind="ExternalInput").ap()
    gngd = dt("gng", [128, 1024], F32, kind="ExternalInput").ap()
    onesd = dt("ones", [128, 128], F32, kind="ExternalInput").ap()
    identd = dt("ident", [128, 128], F32, kind="ExternalInput").ap()
    cmd = dt("cm", [128, T3], F32, kind="ExternalInput").ap()
    cwd = dt("cw", [128, 3, NFT], F32, kind="ExternalInput").ap()
    cbd = dt("cb", [128, NFT], F32, kind="ExternalInput").ap()
    wod = dt("w_out", [D, D], F32, kind="ExternalInput").ap()
    wud = dt("w_up", [D, 2 * DFF], F32, kind="ExternalInput").ap()
    wdd = dt("w_down", [DFF, D], F32, kind="ExternalInput").ap()
    xod = dt("xoT", [D, T3], F32, kind="ExternalOutput").ap()
    xnd = dt("xnT", [D, T3], F32, kind="Internal").ap()
    aTd = dt("aT", [NJ, 128, T3], BF16, kind="Internal").ap()
    xpd = dt("xpart", [D, T3], F32, kind="Internal").ap()
    Bxn_d = [Buf() for _ in range(KC)]
    BaT_d = [Buf() for _ in range(NJ)]
    k = K(nc)

    Bmix = [Buf() for _ in range(KC)]
    wflat = k.sb([128, 4 * KC * 256], BF16)
    wr = Ring([(wflat[:, i * KC * 256:(i + 1) * KC * 256].rearrange("p (k n) -> p k n", n=256), Buf()) for i in range(4)])
    mods = k.sb([128, 2, 6, KC], F32); Bmods = Buf()
    g2s = k.sb([128, KC], F32); Bg2 = Buf()
    ones = k.sb([128, 128], F32); Bones = Buf()
    ident = k.sb([128, 128], F32); Bid = Buf()
    cm = k.sb([128, T3], F32); Bcm = Buf()
    cw = k.sb([128, 3, NFT], F32); Bcw = Buf()
    cb = k.sb([128, NFT], F32); Bcb = Buf()
    for (d_, s_, b_) in [(mods, modsd, Bmods), (g2s, g2d, Bg2), (ones, onesd, Bones), (ident, identd, Bid),
                         (cm, cmd, Bcm), (cw, cwd, Bcw), (cb, cbd, Bcb)]:
        k.dma(k.sp, d_, s_, writes=[b_])
    epsb = k.sb([128, 1], F32); Beps = Buf()
    k.op(k.dve, lambda e: e.memset(epsb, EPS), writes=[Beps])
    a2 = k.sb([128, 2, KC], F32); Ba2 = Buf()
    for s in range(2):
        k.op(k.dve, lambda e, s=s: e.scalar_tensor_tensor(out=a2[:, s, :], in0=mods[:, s, 4, :], scalar=1.0, in1=g2s, op0=ALU.add, op1=ALU.mult),
             reads=[Bmods, Bg2], writes=[Ba2])
    psr = Ring([(k.ps([128, 512]), Buf()) for _ in range(8)])
    ss4 = k.sb([128, 4], F32); Bss4 = Buf()
    sd4 = k.sb([128, 4], F32); Bsd4 = Buf()
    rs4 = k.sb([128, 4], F32); Brs4 = Buf()
    a16r = Ring([(k.sb([128, T3], BF16), Buf()) for _ in range(2)])
    big_cm = nc.sbuf_tensor("bigmix", [128, KC * T3], BF16)
    big = big_cm.__enter__().ap()
    mixT = big.rearrange("p (k t) -> p k t", t=T3)
    for g in range(0, 24, 6):
        k.dma(k.sp, mixT[:, g:g + 6, :], mixd[g * 128:(g + 6) * 128, :].rearrange("(kc p) t -> p kc t", p=128), writes=Bmix[g:g + 6])
    with nc.sbuf_tensor("gA", [128, 8, 1024], F32) as gA_t:
        gA = gA_t.ap()
        gng = gA[:, 0, :]; Bgng = Buf()
        k.dma(k.sp, gng, gngd, writes=[Bgng])
        t_of, t_ob, t_r, t_o, t_sr, t_y, t_junk = (gA[:, i, :] for i in range(1, 8))
        Bof, Bob, Br_, Bo_, Bsr, By, Bj = (Buf() for _ in range(7))
        for s0 in range(0, T3, 128):
            mt = min(128, T3 - s0)
            k.dma(k.sp, t_of[0:mt], gofd[s0:s0 + mt, :], writes=[Bof])
            k.dma(k.act, t_ob[0:mt], gobd[s0:s0 + mt, :], writes=[Bob])
            k.dma(k.sp, t_r[0:mt], rd[s0:s0 + mt, :], writes=[Br_])
            k.op(k.pool, lambda e, mt=mt: e.tensor_tensor(out=t_o[0:mt], in0=t_of[0:mt], in1=t_ob[0:mt], op=ALU.add), reads=[Bof, Bob], writes=[Bo_])
            k.op(k.act, lambda e, mt=mt: e.activation(out=t_sr[0:mt], in_=t_r[0:mt], func=AF.Silu), reads=[Br_], writes=[Bsr])
            k.op(k.pool, lambda e, mt=mt: e.tensor_tensor(out=t_sr[0:mt], in0=t_sr[0:mt], in1=gng[0:mt], op=ALU.mult), reads=[Bsr, Bgng], writes=[Bsr])
            for hh in range(4):
                k.op(k.act, lambda e, mt=mt, hh=hh: e.activation(out=t_junk[0:mt, hh * 256:(hh + 1) * 256], in_=t_o[0:mt, hh * 256:(hh + 1) * 256],
                                                                 func=AF.Square, accum_out=ss4[0:mt, hh:hh + 1]), reads=[Bo_], writes=[Bj, Bss4])
            k.op(k.act, lambda e, mt=mt: e.activation(out=sd4[0:mt], in_=ss4[0:mt], func=AF.Sqrt, bias=epsb[0:mt], scale=1.0 / 256), reads=[Bss4, Beps], writes=[Bsd4])
            k.op(k.dve, lambda e, mt=mt: e.reciprocal(out=rs4[0:mt], in_=sd4[0:mt]), reads=[Bsd4], writes=[Brs4])
            for hh in range(4):
                k.op(k.dve, lambda e, mt=mt, hh=hh: e.scalar_tensor_tensor(
                    out=t_y[0:mt, hh * 256:(hh + 1) * 256], in0=t_o[0:mt, hh * 256:(hh + 1) * 256], scalar=rs4[0:mt, hh:hh + 1],
                    in1=t_sr[0:mt, hh * 256:(hh + 1) * 256], op0=ALU.mult, op1=ALU.mult), reads=[Bo_, Brs4, Bsr], writes=[By])
            for b in range(2):
                pb, Bp = psr.next()
                for f in range(4):
                    k.op(k.pe, lambda e, pb=pb, b=b, f=f, mt=mt: e.transpose(out=pb[:, f * 128:f * 128 + mt], in_=t_y[0:mt, (4 * b + f) * 128:(4 * b + f + 1) * 128],
                                                                           identity=ident[0:mt, 0:mt]), reads=[By, Bid], writes=[Bp])
                k.op(k.act if b == 0 else k.dve, (lambda e, pb=pb, b=b, s0=s0, mt=mt: e.activation(
                    out=mixT[:, 24 + 4 * b:28 + 4 * b, s0:s0 + mt], in_=pb.rearrange("p (f q) -> p f q", f=4)[:, :, 0:mt], func=AF.Copy)) if b == 0 else
                    (lambda e, pb=pb, b=b, s0=s0, mt=mt: e.tensor_copy(
                        out=mixT[:, 24 + 4 * b:28 + 4 * b, s0:s0 + mt], in_=pb.rearrange("p (f q) -> p f q", f=4)[:, :, 0:mt])),
                    reads=[Bp], writes=Bmix[24 + 4 * b:28 + 4 * b])
        k_barrier(k)

    plist = [(wod, c0) for c0 in range(0, D, 256)]
    for pp in range(NJ // 2):
        plist += [(wud, pp * 256), (wud, DFF + pp * 256)]
    ploaded = {}
    pnx = [0]

    def prefetch():
        if pnx[0] < len(plist):
            wsrc, c0 = plist[pnx[0]]
            wt, Bw = wr.next()
            k.dma(k.pool, wt, wsrc[:, c0:c0 + 256].rearrange("(kc p) n -> p kc n", p=128), writes=[Bw])
            ploaded[pnx[0]] = (wt, Bw)
            pnx[0] += 1
    pcur = [0]

    def load_panel(wsrc, c0, pw, nk):
        while pcur[0] >= pnx[0]:
            prefetch()
        res = ploaded.pop(pcur[0])
        pcur[0] += 1
        while pnx[0] < min(len(plist), pcur[0] + 2):
            prefetch()
        return res

    with nc.sbuf_tensor("tB", [128, 12, T3], F32) as tB_t:
        tB = tB_t.ap()
        xr = Ring([(tB[:, i, :], Buf()) for i in range(2)])
        xnr = Ring([(tB[:, 2 + i, :], Buf()) for i in range(2)])
        sqr = Ring([(tB[:, 4 + i, :], Buf()) for i in range(2)])
        acc = tB[:, 6, :]; Bacc = Buf()
        rstd = tB[:, 7, :]; Brstd = Buf()
        for c0 in range(0, D, 256):
            wt, Bw = load_panel(wod, c0, 256, KC)
            for mi in range(2):
                m = c0 // 128 + mi
                banks = [psr.next() for _ in BLK3]
                for kc in range(KC):
                    for bi, (s0, wd) in enumerate(BLK3):
                        k.op(k.pe, lambda e, bi=bi, s0=s0, wd=wd, kc=kc, mi=mi, wt=wt, banks=banks: e.matmul(
                            banks[bi][0][:, 0:wd], lhsT=wt[:, kc, mi * 128:(mi + 1) * 128], rhs=mixT[:, kc, s0:s0 + wd],
                            start=(kc == 0), stop=(kc == KC - 1)), reads=[Bw, Bmix[kc]], writes=[banks[bi][1]])
                xt, Bx = xr.next()
                k.dma(k.sp, xt, xTd[m * 128:(m + 1) * 128, :], writes=[Bx])
                xn, Bxn = xnr.next()
                for bi, blk in enumerate(BLK3):
                    for (a, b, s) in split_ranges(blk, RNG3):
                        k.op(k.dve, lambda e, a=a, b=b, s=s, m=m, xn=xn, xt=xt, pb=banks[bi][0], s0=blk[0]: e.scalar_tensor_tensor(
                            out=xn[:, a:b], in0=pb[:, a - s0:b - s0], scalar=mods[:, s, 2, m:m + 1], in1=xt[:, a:b], op0=ALU.mult, op1=ALU.add),
                            reads=[banks[bi][1], Bx, Bmods], writes=[Bxn])
                k.dma(k.sp, xnd[m * 128:(m + 1) * 128, :], xn, reads=[Bxn], writes=[Bxn_d[m]])
                if m == 0:
                    k.op(k.act, lambda e, xn=xn: e.activation(out=acc, in_=xn, func=AF.Square), reads=[Bxn], writes=[Bacc])
                else:
                    sq, Bsq = sqr.next()
                    k.op(k.act, lambda e, xn=xn, sq=sq: e.activation(out=sq, in_=xn, func=AF.Square), reads=[Bxn], writes=[Bsq])
                    k.op(k.dve, lambda e, sq=sq: e.tensor_tensor(out=acc, in0=acc, in1=sq, op=ALU.add), reads=[Bsq, Bacc], writes=[Bacc])
        banks = [psr.next() for _ in BLK3]
        for bi, (s0, wd) in enumerate(BLK3):
            k.op(k.pe, lambda e, bi=bi, s0=s0, wd=wd: e.matmul(banks[bi][0][:, 0:wd], lhsT=ones, rhs=acc[:, s0:s0 + wd], start=True, stop=True),
                 reads=[Bones, Bacc], writes=[banks[bi][1]])
            k.op(k.act, lambda e, bi=bi, s0=s0, wd=wd: e.activation(out=rstd[:, s0:s0 + wd], in_=banks[bi][0][:, 0:wd], func=AF.Sqrt, bias=epsb, scale=1.0 / D),
                 reads=[banks[bi][1], Beps], writes=[Brstd])
        rst2 = tB[:, 8, :]; Brst2 = Buf()
        k.op(k.dve, lambda e: e.reciprocal(out=rst2, in_=rstd), reads=[Brstd], writes=[Brst2])
        for m in range(KC):
            xn, Bxn = xnr.next()
            k.dma(k.sp, xn, xnd[m * 128:(m + 1) * 128, :], reads=[Bxn_d[m]], writes=[Bxn])
            tmp, Bt = sqr.next()
            k.op(k.dve, lambda e, tmp=tmp, xn=xn: e.tensor_tensor(out=tmp, in0=xn, in1=rst2, op=ALU.mult), reads=[Bxn, Brst2], writes=[Bt])
            for (a, b, s) in RNG3:
                k.op(k.dve, lambda e, a=a, b=b, s=s, m=m, tmp=tmp: e.tensor_scalar(
                    out=tmp[:, a:b], in0=tmp[:, a:b], scalar1=a2[:, s, m:m + 1], scalar2=mods[:, s, 3, m:m + 1], op0=ALU.mult, op1=ALU.add),
                    reads=[Bt, Ba2, Bmods], writes=[Bt])
            k.op(k.pool, lambda e, m=m, tmp=tmp: e.tensor_tensor(out=mixT[:, m, :], in0=tmp, in1=cm, op=ALU.mult), reads=[Bt, Bcm], writes=[Bmix[m]])
        k_barrier(k)
        ugr = Ring([(tB[:, i, :], Buf()) for i in (0, 1)])
        uvr = Ring([(tB[:, i, :], Buf()) for i in (2, 3)])
        cgr = Ring([(tB[:, i, :], Buf()) for i in (4, 5)])
        cvr = Ring([(tB[:, i, :], Buf()) for i in (6, 7)])
        sgr = Ring([(tB[:, i, :], Buf()) for i in (8, 9)])
        for rg_ in (cgr, cvr):
            for (c_, Bc) in rg_.items:
                k.op(k.pool, lambda e, c_=c_: e.memset(c_, 0.0), writes=[Bc])
        for pp in range(NJ // 2):
            wg, Bwg = load_panel(wud, pp * 256, 256, KC)
            wv, Bwv = load_panel(wud, DFF + pp * 256, 256, KC)
            for mi in range(2):
                j = 2 * pp + mi
                cs_ = []
                for (wt, Bw, ur, cr, ti) in [(wg, Bwg, ugr, cgr, j), (wv, Bwv, uvr, cvr, NJ + j)]:
                    banks = [psr.next() for _ in BLK3]
                    for kc in range(KC):
                        for bi, (s0, wd) in enumerate(BLK3):
                            k.op(k.pe, lambda e, bi=bi, s0=s0, wd=wd, kc=kc, mi=mi, wt=wt, banks=banks: e.matmul(
                                banks[bi][0][:, 0:wd], lhsT=wt[:, kc, mi * 128:(mi + 1) * 128], rhs=mixT[:, kc, s0:s0 + wd],
                                start=(kc == 0), stop=(kc == KC - 1)), reads=[Bw, Bmix[kc]], writes=[banks[bi][1]])
                    u, Bu = ur.next()
                    for bi, (s0, wd) in enumerate(BLK3):
                        k.op(k.act, lambda e, u=u, pb=banks[bi][0], s0=s0, wd=wd: e.activation(out=u[:, s0:s0 + wd], in_=pb[:, 0:wd], func=AF.Copy),
                             reads=[banks[bi][1]], writes=[Bu])
                    c_, Bc = cr.next()
                    for (a, b, s) in RNG3:
                        k.op(k.act, lambda e, u=u, c_=c_, a=a, b=b, ti=ti: e.activation(
                            out=c_[:, a + 1:b - 1], in_=u[:, a + 1:b - 1], func=AF.Identity, scale=cw[:, 1, ti:ti + 1], bias=cb[:, ti:ti + 1]),
                            reads=[Bu, Bcw, Bcb], writes=[Bc])
                        k.op(k.dve, lambda e, u=u, c_=c_, a=a, b=b, ti=ti: e.scalar_tensor_tensor(
                            out=c_[:, a + 1:b - 1], in0=u[:, a:b - 2], scalar=cw[:, 0, ti:ti + 1], in1=c_[:, a + 1:b - 1], op0=ALU.mult, op1=ALU.add),
                            reads=[Bu, Bcw, Bc], writes=[Bc])
                        k.op(k.dve, lambda e, u=u, c_=c_, a=a, b=b, ti=ti: e.scalar_tensor_tensor(
                            out=c_[:, a + 1:b - 1], in0=u[:, a + 2:b], scalar=cw[:, 2, ti:ti + 1], in1=c_[:, a + 1:b - 1], op0=ALU.mult, op1=ALU.add),
                            reads=[Bu, Bcw, Bc], writes=[Bc])
                    cs_.append((c_, Bc))
                (cg, Bcg), (cv, Bcv) = cs_
                sg, Bsg = sgr.next()
                k.op(k.act, lambda e, sg=sg, cg=cg: e.activation(out=sg, in_=cg, func=AF.Silu), reads=[Bcg], writes=[Bsg])
                a16, Ba16 = a16r.next()
                k.op(k.dve, lambda e, a16=a16, sg=sg, cv=cv: e.tensor_tensor(out=a16, in0=sg, in1=cv, op=ALU.mult), reads=[Bsg, Bcv], writes=[Ba16])
                k.dma(k.sp, aTd[j], a16, reads=[Ba16], writes=[BaT_d[j]])
        k_barrier(k)

    big_cm.__exit__(None, None, None)
    KH = NJ // 2
    aS = k.sb([128, KH, T3], BF16)
    BaS = Buf()
    wx = k.sb([128, KH, 256], BF16)
    wr2 = Ring([(wflat[:, i * KH * 256:(i + 1) * KH * 256].rearrange("p (k n) -> p k n", n=256), Buf()) for i in range(2)] + [(wx, Buf())])
    xpr = Ring([(k.sb([128, 354], F32), Buf()) for _ in range(3)])
    xor_ = Ring([(k.sb([128, 354], F32), Buf()) for _ in range(3)])
    Bxp_d = [Buf() for _ in range(KC)]
    for h in range(2):
        k0 = h * KH
        for g in range(0, KH, 15):
            ge = min(KH, g + 15)
            k.dma(k.sp, aS[:, g:ge, :], aTd[k0 + g:k0 + ge, :, :].rearrange("k p t -> p k t"), reads=BaT_d[k0 + g:k0 + ge], writes=[BaS])
        wl = {}

        def wload(mp, k0=k0, wl=wl):
            if mp < KC // 2 and mp not in wl:
                wt, Bw = wr2.next()
                k.dma(k.pool, wt, wdd[k0 * 128:(k0 + KH) * 128, mp * 256:(mp + 1) * 256].rearrange("(kc p) n -> p kc n", p=128), writes=[Bw])
                wl[mp] = (wt, Bw)
        wload(0)
        wload(1)
        for mp in range(KC // 2):
            wt, Bw = wl.pop(mp)
            wload(mp + 2)
            for mi in range(2):
                m = 2 * mp + mi
                banks = [psr.next() for _ in BLK3]
                for kc in range(KH):
                    for bi, (s0, wd) in enumerate(BLK3):
                        k.op(k.pe, lambda e, bi=bi, s0=s0, wd=wd, kc=kc, mi=mi, wt=wt, banks=banks: e.matmul(
                            banks[bi][0][:, 0:wd], lhsT=wt[:, kc, mi * 128:(mi + 1) * 128], rhs=aS[:, kc, s0:s0 + wd],
                            start=(kc == 0), stop=(kc == KH - 1)), reads=[Bw, BaS], writes=[banks[bi][1]])
                src_d, Bsrc = (xnd, Bxn_d[m]) if h == 0 else (xpd, Bxp_d[m])
                for bi, (s0, wd) in enumerate(BLK3):
                    xp, Bxp = xpr.next()
                    k.dma(k.sp if bi != 1 else k.act, xp[:, 0:wd], src_d[m * 128:(m + 1) * 128, s0:s0 + wd], reads=[Bsrc], writes=[Bxp])
                    xo, Bxo = xor_.next()
                    for (a, b, s) in split_ranges((s0, wd), RNG3):
                        k.op(k.dve, lambda e, a=a, b=b, s=s, m=m, xo=xo, xp=xp, pb=banks[bi][0], s0=s0: e.scalar_tensor_tensor(
                            out=xo[:, a - s0:b - s0], in0=pb[:, a - s0:b - s0], scalar=mods[:, s, 5, m:m + 1], in1=xp[:, a - s0:b - s0], op0=ALU.mult, op1=ALU.add),
                            reads=[banks[bi][1], Bxp, Bmods], writes=[Bxo])
                    if h == 0:
                        k.dma(k.sp, xpd[m * 128:(m + 1) * 128, s0:s0 + wd], xo[:, 0:wd], reads=[Bxo], writes=[Bxp_d[m]])
                    else:
                        k.dma(k.sp, xod[m * 128:(m + 1) * 128, s0:s0 + wd], xo[:, 0:wd], reads=[Bxo], is_output=True)
    k_barrier(k)
    k.finish()
    return nc


def idx1_of(i):
    return np.concatenate([np.arange(TC * i, TC * (i + 1)), CTX + np.arange(TL * i, TL * (i + 1))])


def idx3_of(i):
    c = np.arange(TC * i - 1, TC * (i + 1) + 1)
    t = np.arange(TL * i - 1, TL * (i + 1) + 1)
    valid = np.concatenate([(c >= 0) & (c < CTX), (t >= 0) & (t < SEQ)])
    idx = np.concatenate([np.clip(c, 0, CTX - 1), CTX + np.clip(t, 0, SEQ - 1)])
    return idx, valid


def run_p3(xT_glob, daT, ft_cores, gof, gob, p1, modsT, inp, l):
    ftT = to_global_fm(ft_cores)
    mix_glob = np.concatenate([daT, ftT], axis=0)
    r_glob = to_global_tm([np.asarray(r["o_r"]) for r in p1])
    nc = build_p3()
    ones = np.ones((128, 128), np.float32)
    ident = np.eye(128, dtype=np.float32)
    g2 = vec_layout(inp["norm2_g"][l])
    gng = np.ascontiguousarray(np.broadcast_to(np.tile(inp["gla_norm_g"][l], 4), (128, 1024)))
    cw = np.ascontiguousarray(inp["conv_w"][l].reshape(3, NFT, 128).transpose(2, 0, 1))
    cb = vec_layout(inp["conv_b"][l])
    w_out, w_up, w_down = inp["w_out"][l], inp["w_up"][l], inp["w_down"][l]
    in_maps = []
    for i in range(NCORES):
        idx, valid = idx3_of(i)
        vm = valid.astype(np.float32)
        xT = np.ascontiguousarray(xT_glob[:, idx] * vm[None, :])
        mixT = mix_glob[:, idx].copy()
        mixT[:, ~valid] = 0
        gf = gof[idx].copy(); gf[~valid] = 0
        gb = gob[idx].copy(); gb[~valid] = 0
        rr = r_glob[idx].copy(); rr[~valid] = 0
        in_maps.append({"xT": xT, "mixT": np.ascontiguousarray(mixT), "gof": gf, "gob": gb, "r": rr, "modsT": modsT, "g2": g2,
                        "gng": gng, "ones": ones, "ident": ident, "cm": np.ascontiguousarray(np.broadcast_to(vm, (128, T3))),
                        "cw": cw, "cb": cb, "w_out": w_out, "w_up": w_up, "w_down": w_down})
    res = run_bass_kernel_spmd(nc, in_maps, core_ids=list(range(NCORES)))
    outs = [np.asarray(r["xoT"]) for r in res.results]
    ctx_part = [o[:, 1:1 + TC] for o in outs]
    lat_part = [o[:, TC + 3:TC + 3 + TL] for o in outs]
    return np.concatenate(ctx_part + lat_part, axis=1)


def kernel(**inp):
    inp = {k_: np.asarray(v_) for k_, v_ in inp.items()}
    mods = run_p0(inp)
    xT_glob = np.ascontiguousarray(np.concatenate([inp["ctx"][0], inp["x"][0]], axis=0).T)
    for l in range(DEPTH):
        modsT = mods_layout(mods[l])
        xT_cores = [np.ascontiguousarray(xT_glob[:, idx1_of(i)]) for i in range(NCORES)]
        p1 = run_p1(xT_cores, inp["w_in"][l], modsT, inp["norm1_g"][l], inp["q_norm_g"][l], inp["k_norm_g"][l])
        p1 = [{k_: np.asarray(v_) for k_, v_ in r.items()} for r in p1]
        daT = run_da(p1, inp, l)
        ft = run_ft(p1)
        gof, gob = run_gla(p1, inp, l)
        xT_glob = run_p3(xT_glob, daT, ft, gof, gob, p1, modsT, inp, l)
    out = np.ascontiguousarray(xT_glob[:, CTX:].T)[None].astype(np.float32)
    return out
```

```python
import numpy as np
import ml_dtypes
import concourse.bass as bass
import concourse.mybir as mybir
from concourse.bass_utils import run_bass_kernel_spmd

F32 = mybir.dt.float32
BF16 = mybir.dt.bfloat16
ALU = mybir.AluOpType
AF = mybir.ActivationFunctionType

NCORES = 8
D = 4096
SEQ = 8192
CTX = 256
DEPTH = 2
KC = D // 128
TL = SEQ // NCORES
TC = CTX // NCORES
T1 = TC + TL
IN_W = 10272
DFF = 11008
EPS = 1e-6


class Buf:
    __slots__ = ("name", "w", "r")

    def __init__(self, name=""):
        self.name = name
        self.w = None
        self.r = {}


class Eng:
    def __init__(self, name):
        self.name = name
        self.prog = []
        self.count = 0
        self.seen = {}
        self.sem = None


class K:
    def __init__(self, nc, n_dma_sems=10):
        self.nc = nc
        self.pe = Eng("tensor")
        self.act = Eng("scalar")
        self.dve = Eng("vector")
        self.pool = Eng("gpsimd")
        self.sp = Eng("sync")
        self.engs = [self.pe, self.act, self.dve, self.pool, self.sp]
        for e in self.engs:
            e.sem = nc.alloc_semaphore(name="c_" + e.name)
        self.dma_pool = {}
        for e in (self.sp, self.pool, self.act):
            lst = []
            for i in range(n_dma_sems):
                lst.append([nc.alloc_semaphore(name=f"d_{e.name}{i}"), 0])
            self.dma_pool[e.name] = [lst, 0]
        self.out_events = []
        self._n = 0

    def sb(self, shape, dt, name=None):
        self._n += 1
        return self.nc.alloc_sbuf_tensor(name or f"sb{self._n}", list(shape), dt).ap()

    def ps(self, shape, dt=F32, name=None):
        self._n += 1
        return self.nc.alloc_psum_tensor(name or f"ps{self._n}", list(shape), dt).ap()

    def _wait(self, E, ev):
        if ev is None:
            return
        sem, val = ev
        k = id(sem)
        if E.seen.get(k, 0) >= val:
            return
        E.seen[k] = val
        E.prog.append(lambda eng, sem=sem, val=val: eng.wait_ge(sem, val))

    def _deps(self, E, reads, writes):
        for b in reads:
            self._wait(E, b.w)
        for b in writes:
            self._wait(E, b.w)
            for ev in list(b.r.values()):
                self._wait(E, ev)

    def _commit(self, ev, reads, writes):
        for b in reads:
            b.r[id(ev[0])] = ev
        for b in writes:
            b.w = ev
            b.r = {}

    def op(self, E, fn, reads=(), writes=()):
        self._deps(E, reads, writes)
        E.count += 1
        cnt = E.count
        sem = E.sem
        E.prog.append(lambda eng, fn=fn, sem=sem: fn(eng).then_inc(sem, 1))
        ev = (sem, cnt)
        if E is self.pe:
            E.seen[id(sem)] = cnt
        self._commit(ev, reads, writes)
        return ev

    def dma(self, E, out_ap, in_ap, reads=(), writes=(), is_output=False, **kw):
        self._deps(E, reads, writes)
        pool, idx = self.dma_pool[E.name]
        slot = pool[idx % len(pool)]
        self.dma_pool[E.name][1] = idx + 1
        sem, uses = slot
        if uses > 0:
            self._wait(E, (sem, 16 * uses))
        slot[1] = uses + 1
        val = 16 * (uses + 1)
        E.prog.append(lambda eng, o=out_ap, i=in_ap, sem=sem, kw=kw:
                      eng.dma_start(out=o, in_=i, **kw).then_inc(sem, 16))
        ev = (sem, val)
        self._commit(ev, reads, writes)
        if is_output:
            self.out_events.append(ev)
        return ev

    def finish(self):
        for ev in self.out_events:
            self._wait(self.sp, ev)
        with self.nc.Block() as block:
            for E in self.engs:
                if not E.prog:
                    continue

                def body(eng, E=E):
                    for c in E.prog:
                        c(eng)
                getattr(block, E.name)(body)


class Ring:
    def __init__(self, items):
        self.items = items
        self.i = 0

    def next(self):
        it = self.items[self.i % len(self.items)]
        self.i += 1
        return it


def blocks_of(total, nblk):
    base = -(-total // nblk)
    out, s = [], 0
    while s < total:
        w = min(base, total - s)
        out.append((s, w))
        s += w
    return out


def split_ranges(blk, ranges):
    s, w = blk
    res = []
    for (rs, re_, tag) in ranges:
        a, b = max(s, rs), min(s + w, re_)
        if a < b:
            res.append((a, b, tag))
    return res


MODW = 6 * D // NCORES


def build_p0():
    nc = bass.Bass("TRN2", target_bir_lowering=False)
    cs = nc.dram_tensor("cs", [128, KC, 2], F32, kind="ExternalInput").ap()
    wa = nc.dram_tensor("wa", [DEPTH, D, MODW], F32, kind="ExternalInput").ap()
    ba = nc.dram_tensor("ba", [DEPTH, MODW], F32, kind="ExternalInput").ap()
    out = nc.dram_tensor("mods", [DEPTH, 2, MODW], F32, kind="ExternalOutput").ap()
    k = K(nc)
    cs_sb = k.sb([128, KC, 2], F32)
    css = k.sb([128, KC, 2], F32)
    Bcs, Bcss = Buf(), Buf()
    k.dma(k.sp, cs_sb, cs, writes=[Bcs])
    k.op(k.act, lambda e: e.activation(out=css, in_=cs_sb, func=AF.Silu), reads=[Bcs], writes=[Bcss])
    wring = Ring([(k.sb([128, 4, 512], F32), Buf()) for _ in range(4)])
    pring = Ring([(k.ps([128, 512]), Buf()) for _ in range(2)])
    bring = Ring([(k.sb([2, 512], F32), Buf()) for _ in range(2)])
    oring = Ring([(k.sb([2, 512], F32), Buf()) for _ in range(2)])
    nq = 0
    for l in range(DEPTH):
        for cb in range(MODW // 512):
            c0 = cb * 512
            ps, Bp = pring.next()
            for kg in range(KC // 4):
                wt, Bw = wring.next()
                q = k.sp if nq % 2 == 0 else k.act
                nq += 1
                k.dma(q, wt, wa[l, kg * 512:(kg + 1) * 512, c0:c0 + 512].rearrange("(kc p) n -> p kc n", p=128),
                      writes=[Bw])
                for j in range(4):
                    kc = kg * 4 + j
                    k.op(k.pe, lambda e, ps=ps, wt=wt, j=j, kc=kc: e.matmul(
                        ps[0:2, :], lhsT=css[:, kc, :], rhs=wt[:, j, :], start=(kc == 0), stop=(kc == KC - 1)),
                        reads=[Bcss, Bw], writes=[Bp])
            bt, Bb = bring.next()
            for s in range(2):
                k.dma(k.sp, bt[s:s + 1, :], ba[l:l + 1, c0:c0 + 512], writes=[Bb])
            ot, Bo = oring.next()
            k.op(k.dve, lambda e, ot=ot, ps=ps, bt=bt: e.tensor_tensor(out=ot, in0=ps[0:2, :], in1=bt, op=ALU.add),
                 reads=[Bp, Bb], writes=[Bo])
            k.dma(k.sp, out[l, :, c0:c0 + 512], ot, reads=[Bo], is_output=True)
    k.finish()
    return nc


def run_p0(inp):
    cs = np.stack([inp["c"][0], inp["c_ctx"]], axis=-1)
    cs = np.ascontiguousarray(cs.reshape(KC, 128, 2).transpose(1, 0, 2))
    nc = build_p0()
    in_maps = []
    for i in range(NCORES):
        sl = slice(i * MODW, (i + 1) * MODW)
        in_maps.append({"cs": cs,
                        "wa": np.ascontiguousarray(inp["w_ada"][:, :, sl]),
                        "ba": np.ascontiguousarray(inp["b_ada"][:, sl])})
    res = run_bass_kernel_spmd(nc, in_maps, core_ids=list(range(NCORES)))
    mods = np.concatenate([r["mods"] for r in res.results], axis=-1)
    return mods


def mods_layout(mods_l):
    return np.ascontiguousarray(mods_l.reshape(2, 6, KC, 128).transpose(3, 0, 1, 2))


def vec_layout(v):
    return np.ascontiguousarray(v.reshape(-1, 128).T)


BLK1 = blocks_of(T1, 3)
RNG1 = [(0, TC, 1), (TC, T1, 0)]
FM_GROUPS = [(0, 4096, "qk"), (6144, 1024, "z"), (7168, 1024, "gqk"), (9216, 32, "gd")]
TM_GROUPS = [(4096, 2048, "v"), (7680, 512, "gk"), (8192, 1024, "gv"), (9248, 1024, "r")]


def build_p1():
    nc = bass.Bass("TRN2", target_bir_lowering=False)
    dt = nc.dram_tensor
    xT = dt("xT", [D, T1], F32, kind="ExternalInput").ap()
    w = dt("w", [D, IN_W], F32, kind="ExternalInput").ap()
    modsT = dt("modsT", [128, 2, 6, KC], F32, kind="ExternalInput").ap()
    g1 = dt("g1", [128, KC], F32, kind="ExternalInput").ap()
    qkg = dt("qkg", [128, 2], F32, kind="ExternalInput").ap()
    cosT = dt("cosT", [128, T1], F32, kind="ExternalInput").ap()
    sinT = dt("sinT", [128, T1], F32, kind="ExternalInput").ap()
    onesd = dt("ones", [128, 128], F32, kind="ExternalInput").ap()
    rotd = dt("rot", [128, 128], F32, kind="ExternalInput").ap()
    o_qk = dt("o_qk", [32, 128, T1], BF16, kind="ExternalOutput").ap()
    o_z = dt("o_z", [8, 128, T1], BF16, kind="ExternalOutput").ap()
    o_gqk = dt("o_gqk", [8, 128, T1], F32, kind="ExternalOutput").ap()
    o_gd = dt("o_gd", [32, T1], F32, kind="ExternalOutput").ap()
    o_v = dt("o_v", [T1, 2048], BF16, kind="ExternalOutput").ap()
    o_gk = dt("o_gk", [T1, 512], F32, kind="ExternalOutput").ap()
    o_gv = dt("o_gv", [T1, 1024], BF16, kind="ExternalOutput").ap()
    o_r = dt("o_r", [T1, 1024], F32, kind="ExternalOutput").ap()

    k = K(nc)
    mods = k.sb([128, 2, 6, KC], F32); Bmods = Buf()
    g1s = k.sb([128, KC], F32); Bg1 = Buf()
    qkgs = k.sb([128, 2], F32); Bqkg = Buf()
    cosS = k.sb([128, T1], F32); Bcos = Buf()
    sinS = k.sb([128, T1], F32); Bsin = Buf()
    ones = k.sb([128, 128], F32); Bones = Buf()
    rot = k.sb([128, 128], F32); Brot = Buf()
    for (d_, s_, b_) in [(mods, modsT, Bmods), (g1s, g1, Bg1), (qkgs, qkg, Bqkg), (cosS, cosT, Bcos),
                         (sinS, sinT, Bsin), (ones, onesd, Bones), (rot, rotd, Brot)]:
        k.dma(k.sp, d_, s_, writes=[b_])
    a1 = k.sb([128, 2, KC], F32); Ba1 = Buf()
    for s in range(2):
        k.op(k.dve, lambda e, s=s: e.scalar_tensor_tensor(out=a1[:, s, :], in0=mods[:, s, 1, :], scalar=1.0, in1=g1s,
                                                          op0=ALU.add, op1=ALU.mult),
             reads=[Bmods, Bg1], writes=[Ba1])

    psr = Ring([(k.ps([128, 512]), Buf()) for _ in range(6)])
    ps_ss = (k.ps([128, 512]), Buf())
    ps_rot = (k.ps([128, 512]), Buf())

    xr = Ring([(k.sb([128, T1], F32), Buf()) for _ in range(2)])
    sqr = Ring([(k.sb([128, T1], F32), Buf()) for _ in range(2)])
    ssb = [psr.next() for _ in range(3)]
    for t in range(KC):
        xt, Bx = xr.next()
        k.dma(k.sp, xt, xT[t * 128:(t + 1) * 128, :], writes=[Bx])
        sq, Bsq = sqr.next()
        k.op(k.act, lambda e, sq=sq, xt=xt: e.activation(out=sq, in_=xt, func=AF.Square), reads=[Bx], writes=[Bsq])
        for bi, (s0, wd) in enumerate(BLK1):
            k.op(k.pe, lambda e, bi=bi, s0=s0, wd=wd, sq=sq, t=t: e.matmul(
                ssb[bi][0][:, 0:wd], lhsT=ones, rhs=sq[:, s0:s0 + wd], start=(t == 0), stop=(t == KC - 1)),
                reads=[Bones, Bsq], writes=[ssb[bi][1]])
    rstd = k.sb([128, T1], F32); Brstd = Buf()
    epsb = k.sb([128, 1], F32); Beps = Buf()
    k.op(k.dve, lambda e: e.memset(epsb, EPS), writes=[Beps])
    for bi, (s0, wd) in enumerate(BLK1):
        k.op(k.act, lambda e, bi=bi, s0=s0, wd=wd: e.activation(out=rstd[:, s0:s0 + wd], in_=ssb[bi][0][:, 0:wd],
                                                                func=AF.Sqrt, bias=epsb, scale=1.0 / D),
             reads=[ssb[bi][1], Beps], writes=[Brstd])
    k.op(k.dve, lambda e: e.reciprocal(out=rstd, in_=rstd), reads=[Brstd], writes=[Brstd])

    hT = k.sb([128, KC, T1], BF16)
    Bh = [Buf() for _ in range(KC)]
    for t in range(KC):
        xt, Bx = xr.next()
        k.dma(k.sp, xt, xT[t * 128:(t + 1) * 128, :], writes=[Bx])
        tmp, Bt = sqr.next()
        k.op(k.dve, lambda e, tmp=tmp, xt=xt: e.tensor_tensor(out=tmp, in0=xt, in1=rstd, op=ALU.mult),
             reads=[Bx, Brstd], writes=[Bt])
        for (a, b, s) in RNG1:
            k.op(k.dve, lambda e, a=a, b=b, s=s, t=t, tmp=tmp: e.tensor_scalar(
                out=hT[:, t, a:b], in0=tmp[:, a:b], scalar1=a1[:, s, t:t + 1], scalar2=mods[:, s, 0, t:t + 1],
                op0=ALU.mult, op1=ALU.add), reads=[Bt, Ba1, Bmods], writes=[Bh[t]])

    wr = Ring([(k.sb([128, KC, 512], BF16), Buf()) for _ in range(2)])

    panel_list = []
    for (g0, gw, kind) in FM_GROUPS + TM_GROUPS:
        for c0 in range(g0, g0 + gw, 512):
            panel_list.append((c0, min(512, g0 + gw - c0)))
    loaded = {}
    nxt = [0]

    def prefetch():
        if nxt[0] < len(panel_list):
            c0, pw = panel_list[nxt[0]]
            nxt[0] += 1
            wt, Bw = wr.next()
            k.dma(k.pool, wt[:, :, 0:pw], w[:, c0:c0 + pw].rearrange("(kc p) n -> p kc n", p=128), writes=[Bw])
            loaded[c0] = (wt, Bw)

    def load_panel(c0, pw):
        if c0 not in loaded:
            prefetch()
        res = loaded.pop(c0)
        prefetch()
        return res

    t_sq = Ring([(k.sb([128, 352], F32), Buf()) for _ in range(3)])
    t_sd = Ring([(k.sb([128, 352], F32), Buf()) for _ in range(2)])
    t_qn = Ring([(k.sb([128, 352], F32), Buf()) for _ in range(3)])
    t_a = Ring([(k.sb([128, 352], F32), Buf()) for _ in range(2)])
    t_b = Ring([(k.sb([128, 352], F32), Buf()) for _ in range(2)])
    o16 = Ring([(k.sb([128, T1], BF16), Buf()) for _ in range(2)])
    o32 = Ring([(k.sb([128, T1], F32), Buf()) for _ in range(2)])

    tile_idx = {"qk": 0, "z": 0, "gqk": 0}
    pending = {}

    def add_hook(kc, fn):
        pending.setdefault(kc, []).append(fn)

    def qk_epilogue(ti, banks):
        gi = 0 if ti < 16 else 1
        ot, Bo = o16.next()
        st = {}
        for bi, (s0, wd) in enumerate(BLK1):
            pb, Bp = banks[bi]
            sq, Bsq = t_sq.next()
            k.op(k.act, lambda e, sq=sq, pb=pb, wd=wd: e.activation(out=sq[:, 0:wd], in_=pb[:, 0:wd], func=AF.Square),
                 reads=[Bp], writes=[Bsq])
            st[bi] = (sq, Bsq)

            def stage1(bi=bi, s0=s0, wd=wd, pb=pb, Bp=Bp):
                sq, Bsq = st[bi]
                k.op(k.pe, lambda e, sq=sq, wd=wd: e.matmul(ps_ss[0][:, 0:wd], lhsT=ones, rhs=sq[:, 0:wd], start=True, stop=True),
                     reads=[Bones, Bsq], writes=[ps_ss[1]])
                sd, Bsd = t_sd.next()
                k.op(k.act, lambda e, sd=sd, wd=wd: e.activation(out=sd[:, 0:wd], in_=ps_ss[0][:, 0:wd], func=AF.Sqrt,
                                                                 bias=epsb, scale=1.0 / 128),
                     reads=[ps_ss[1], Beps], writes=[Bsd])
                k.op(k.dve, lambda e, sd=sd, wd=wd: e.reciprocal(out=sd[:, 0:wd], in_=sd[:, 0:wd]), reads=[Bsd], writes=[Bsd])
                qn, Bqn = t_qn.next()
                k.op(k.dve, lambda e, qn=qn, pb=pb, sd=sd, wd=wd, gi=gi: e.scalar_tensor_tensor(
                    out=qn[:, 0:wd], in0=pb[:, 0:wd], scalar=qkgs[:, gi:gi + 1], in1=sd[:, 0:wd], op0=ALU.mult, op1=ALU.mult),
                    reads=[Bp, Bsd, Bqkg], writes=[Bqn])
                st[("qn", bi)] = (qn, Bqn)

            def stage2(bi=bi, s0=s0, wd=wd):
                qn, Bqn = st[("qn", bi)]
                k.op(k.pe, lambda e, qn=qn, wd=wd: e.matmul(ps_rot[0][:, 0:wd], lhsT=rot, rhs=qn[:, 0:wd], start=True, stop=True),
                     reads=[Brot, Bqn], writes=[ps_rot[1]])
                ta, Bta = t_a.next()
                k.op(k.pool, lambda e, ta=ta, qn=qn, s0=s0, wd=wd: e.tensor_tensor(out=ta[:, 0:wd], in0=qn[:, 0:wd], in1=cosS[:, s0:s0 + wd], op=ALU.mult),
                     reads=[Bqn, Bcos], writes=[Bta])
                tb, Btb = t_b.next()
                k.op(k.dve, lambda e, tb=tb, s0=s0, wd=wd: e.tensor_tensor(out=tb[:, 0:wd], in0=ps_rot[0][:, 0:wd], in1=sinS[:, s0:s0 + wd], op=ALU.mult),
                     reads=[ps_rot[1], Bsin], writes=[Btb])
                k.op(k.pool, lambda e, ta=ta, tb=tb, ot=ot, s0=s0, wd=wd: e.tensor_tensor(out=ot[:, s0:s0 + wd], in0=ta[:, 0:wd], in1=tb[:, 0:wd], op=ALU.add),
                     reads=[Bta, Btb], writes=[Bo])
                if bi == len(BLK1) - 1:
                    k.dma(k.sp, o_qk[ti], ot, reads=[Bo], is_output=True)
            add_hook(6 + 3 * bi, stage1)
            add_hook(16 + 4 * bi, stage2)

    def flush_hooks(hooks, upto=None):
        for kc in sorted(hooks):
            if upto is not None and kc != upto:
                continue
            for fn in hooks[kc]:
                fn()

    for (g0, gw, kind) in FM_GROUPS:
        for c0 in range(g0, g0 + gw, 512):
            pw = min(512, g0 + gw - c0)
            wt, Bw = load_panel(c0, pw)
            for m0 in range(0, pw, 128):
                mw = min(128, pw - m0)
                banks = [psr.next() for _ in BLK1]
                cur = pending
                pending = {}
                for kc in range(KC):
                    for bi, (s0, wd) in enumerate(BLK1):
                        k.op(k.pe, lambda e, bi=bi, s0=s0, wd=wd, kc=kc, m0=m0, mw=mw, wt=wt, banks=banks: e.matmul(
                            banks[bi][0][0:mw, 0:wd], lhsT=wt[:, kc, m0:m0 + mw], rhs=hT[:, kc, s0:s0 + wd],
                            start=(kc == 0), stop=(kc == KC - 1)), reads=[Bw, Bh[kc]], writes=[banks[bi][1]])
                    if kc in cur:
                        flush_hooks(cur, upto=kc)
                if kind == "qk":
                    ti = tile_idx["qk"]; tile_idx["qk"] += 1
                    qk_epilogue(ti, banks)
                else:
                    use16 = (kind == "z")
                    ot, Bo = (o16 if use16 else o32).next()
                    for bi, (s0, wd) in enumerate(BLK1):
                        pb, Bp = banks[bi]
                        eng = k.act if bi % 2 == 0 else k.dve
                        if eng is k.act:
                            k.op(eng, lambda e, ot=ot, pb=pb, s0=s0, wd=wd, mw=mw: e.activation(out=ot[0:mw, s0:s0 + wd], in_=pb[0:mw, 0:wd], func=AF.Copy),
                                 reads=[Bp], writes=[Bo])
                        else:
                            k.op(eng, lambda e, ot=ot, pb=pb, s0=s0, wd=wd, mw=mw: e.tensor_copy(out=ot[0:mw, s0:s0 + wd], in_=pb[0:mw, 0:wd]),
                                 reads=[Bp], writes=[Bo])
                    if kind == "gd":
                        k.dma(k.sp, o_gd, ot[0:32, :], reads=[Bo], is_output=True)
                    else:
                        ti = tile_idx[kind]; tile_idx[kind] += 1
                        k.dma(k.sp, (o_z if kind == "z" else o_gqk)[ti], ot, reads=[Bo], is_output=True)
    flush_hooks(pending)
    pending = {}

    e16 = Ring([(k.sb([128, 512], BF16), Buf()) for _ in range(3)])
    e32 = Ring([(k.sb([128, 512], F32), Buf()) for _ in range(3)])
    ttiles = [(s, min(128, T1 - s)) for s in range(0, T1, 128)]
    outs = {"v": (o_v, True), "gk": (o_gk, False), "gv": (o_gv, True), "r": (o_r, False)}
    ne = 0
    for (g0, gw, kind) in TM_GROUPS:
        od, is16 = outs[kind]
        for c0 in range(g0, g0 + gw, 512):
            wt, Bw = load_panel(c0, 512)
            for (s0, mt) in ttiles:
                pb, Bp = psr.next()
                for kc in range(KC):
                    k.op(k.pe, lambda e, pb=pb, kc=kc, s0=s0, mt=mt, wt=wt: e.matmul(
                        pb[0:mt, :], lhsT=hT[:, kc, s0:s0 + mt], rhs=wt[:, kc, :], start=(kc == 0), stop=(kc == KC - 1)),
                        reads=[Bw, Bh[kc]], writes=[Bp])
                et, Be = (e16 if is16 else e32).next()
                ne += 1
                if ne % 2 == 0:
                    k.op(k.act, lambda e, et=et, pb=pb, mt=mt: e.activation(out=et[0:mt, :], in_=pb[0:mt, :], func=AF.Copy), reads=[Bp], writes=[Be])
                else:
                    k.op(k.dve, lambda e, et=et, pb=pb, mt=mt: e.tensor_copy(out=et[0:mt, :], in_=pb[0:mt, :]), reads=[Bp], writes=[Be])
                k.dma(k.sp, od[s0:s0 + mt, c0 - g0:c0 - g0 + 512], et[0:mt, :], reads=[Be], is_output=True)
    k.finish()
    return nc


def rope_tables():
    half = 64
    freqs = (10000.0 ** (-np.arange(0, half, 2, dtype=np.float32) / half)).astype(np.float32)
    t = np.arange(SEQ)
    ar = (t // 64).astype(np.float32)[:, None] * freqs
    ac = (t % 64).astype(np.float32)[:, None] * freqs
    ang = np.concatenate([ar, ar, ac, ac], axis=-1)
    return np.cos(ang).astype(np.float32), np.sin(ang).astype(np.float32)


def rot_matrix():
    R = np.zeros((128, 128), np.float32)
    for j in range(32):
        R[32 + j, j] = -1.0
        R[j, 32 + j] = 1.0
        R[96 + j, 64 + j] = -1.0
        R[64 + j, 96 + j] = 1.0
    return R


def run_p1(xT_cores, w_in_l, modsT, g1n, qg, kg):
    cos, sin = rope_tables()
    nc = build_p1()
    ones = np.ones((128, 128), np.float32)
    rot = rot_matrix()
    g1 = vec_layout(g1n)
    qkg = np.ascontiguousarray(np.stack([qg, kg], axis=-1))
    in_maps = []
    for i in range(NCORES):
        cT = np.concatenate([np.ones((128, TC), np.float32), cos[i * TL:(i + 1) * TL].T], axis=1)
        sT = np.concatenate([np.zeros((128, TC), np.float32), sin[i * TL:(i + 1) * TL].T], axis=1)
        in_maps.append({"xT": xT_cores[i], "w": w_in_l, "modsT": modsT, "g1": g1, "qkg": qkg,
                        "cosT": np.ascontiguousarray(cT), "sinT": np.ascontiguousarray(sT), "ones": ones, "rot": rot})
    res = run_bass_kernel_spmd(nc, in_maps, core_ids=list(range(NCORES)))
    return res.results


NTOK = CTX + SEQ
NKT = NTOK // 128


def build_da():
    nc = bass.Bass("TRN2", target_bir_lowering=False)
    dt = nc.dram_tensor
    qTd = dt("qT", [2, 128, NTOK], BF16, kind="ExternalInput").ap()
    kTd = dt("kT", [2, 128, NTOK], BF16, kind="ExternalInput").ap()
    vd = dt("vaug", [NTOK, 257], BF16, kind="ExternalInput").ap()
    lamv = dt("lamv", [128, 4], F32, kind="ExternalInput").ap()
    lamc = dt("lamc", [128, 2], F32, kind="ExternalInput").ap()
    gsd = dt("gs", [128, 256], F32, kind="ExternalInput").ap()
    onesd = dt("ones", [128, 128], F32, kind="ExternalInput").ap()
    identd = dt("ident", [128, 128], F32, kind="ExternalInput").ap()
    outd = dt("daT", [2, 128, NTOK], BF16, kind="ExternalOutput").ap()
    k = K(nc)
    qT = k.sb([128, 2, NTOK], BF16); Bq = Buf()
    kT = k.sb([128, 2, NTOK], BF16); Bk = Buf()
    va = k.sb([128, NKT, 257], BF16); Bv = Buf()
    for m in range(2):
        k.dma(k.sp, qT[:, m, :], qTd[m], writes=[Bq])
        k.dma(k.sp, kT[:, m, :], kTd[m], writes=[Bk])
    for g in range(0, NKT, 11):
        k.dma(k.sp, va[:, g:g + 11, :], vd[g * 128:(g + 11) * 128, :].rearrange("(kt p) c -> p kt c", p=128), writes=[Bv])
    lv = k.sb([128, 4], F32); Blv = Buf()
    lc = k.sb([128, 2], F32); Blc = Buf()
    gs = k.sb([128, 256], F32); Bgs = Buf()
    ones = k.sb([128, 128], F32); Bones = Buf()
    ident = k.sb([128, 128], F32); Bid = Buf()
    for (d_, s_, b_) in [(lv, lamv, Blv), (lc, lamc, Blc), (gs, gsd, Bgs), (ones, onesd, Bones), (ident, identd, Bid)]:
        k.dma(k.sp, d_, s_, writes=[b_])
    epsb = k.sb([128, 1], F32); Beps = Buf()
    k.op(k.dve, lambda e: e.memset(epsb, EPS), writes=[Beps])
    pr = k.sb([128, 2], F32); Bpr = Buf()
    k.op(k.dve, lambda e: e.tensor_tensor(out=pr[:, 0:1], in0=lv[:, 0:1], in1=lv[:, 1:2], op=ALU.mult), reads=[Blv], writes=[Bpr])
    k.op(k.dve, lambda e: e.tensor_tensor(out=pr[:, 1:2], in0=lv[:, 2:3], in1=lv[:, 3:4], op=ALU.mult), reads=[Blv, Bpr], writes=[Bpr])
    ps_t = (k.ps([128, 512]), Buf())
    k.op(k.pe, lambda e: e.matmul(ps_t[0][:, 0:2], lhsT=ones, rhs=pr, start=True, stop=True), reads=[Bones, Bpr], writes=[ps_t[1]])
    ex = k.sb([128, 2], F32); Bex = Buf()
    k.op(k.act, lambda e: e.activation(out=ex, in_=ps_t[0][:, 0:2], func=AF.Exp), reads=[ps_t[1]], writes=[Bex])
    neglam = k.sb([128, 1], F32); Bnl = Buf()
    k.op(k.dve, lambda e: e.tensor_tensor(out=neglam, in0=ex[:, 1:2], in1=ex[:, 0:1], op=ALU.subtract), reads=[Bex], writes=[Bnl])
    k.op(k.dve, lambda e: e.tensor_tensor(out=neglam, in0=neglam, in1=lc[:, 0:1], op=ALU.subtract), reads=[Blc, Bnl], writes=[Bnl])
    gss = k.sb([128, 256], F32); Bgss = Buf()
    k.op(k.dve, lambda e: e.tensor_scalar(out=gss, in0=gs, scalar1=lc[:, 1:2], scalar2=None, op0=ALU.mult), reads=[Bgs, Blc], writes=[Bgss])

    ps_s = Ring([(k.ps([128, 512]), Buf()) for _ in range(3)])
    acc = [[(k.ps([128, 512]), Buf()) for _ in range(2)] for _ in range(2)]
    pT = Ring([(k.sb([128, 2, 256], BF16), Buf()) for _ in range(4)])
    sc = 128.0 ** -0.5
    r12 = Ring([(k.sb([128, 2], F32), Buf()) for _ in range(4)])
    o1r = Ring([(k.sb([128, 256], F32), Buf()) for _ in range(2)])
    orr = Ring([(k.sb([128, 256], F32), Buf()) for _ in range(2)])
    sqr = Ring([(k.sb([128, 256], F32), Buf()) for _ in range(2)])
    ssr = Ring([(k.sb([128, 1], F32), Buf()) for _ in range(4)])
    sdr = Ring([(k.sb([128, 2], F32), Buf()) for _ in range(4)])
    yr = Ring([(k.sb([128, 256], F32), Buf()) for _ in range(4)])
    otr = Ring([(k.sb([128, 2, 256], BF16), Buf()) for _ in range(2)])

    qblocks = [(0, list(range(2)))] + [(CTX + 256 * b, list(range(NKT))) for b in range(SEQ // 256)]
    steps = []
    for bi, (q0, kts) in enumerate(qblocks):
        for ki, kt in enumerate(kts):
            steps.append((bi, q0, ki, kt, len(kts)))
    LOOK = 2
    sbuf_of = {}

    def issue_S(si):
        bi, q0, ki, kt, n = steps[si]
        sb_, Bs = ps_s.next()
        pt, Bpt = pT.next()
        for m in range(2):
            k.op(k.pe, lambda e, sb_=sb_, m=m, kt=kt, q0=q0: e.matmul(
                sb_[:, m * 256:(m + 1) * 256], lhsT=kT[:, m, kt * 128:(kt + 1) * 128], rhs=qT[:, m, q0:q0 + 256],
                start=True, stop=True), reads=[Bk, Bq], writes=[Bs])
        k.op(k.act, lambda e, pt=pt, sb_=sb_: e.activation(out=pt.rearrange("p a b -> p (a b)"), in_=sb_, func=AF.Exp, scale=sc),
             reads=[Bs], writes=[Bpt])
        sbuf_of[si] = (pt, Bpt)

    def epilogue_front(q0):
        ys = []
        for qs in range(2):
            a1_, Ba1 = acc[0][qs]
            a2_, Ba2 = acc[1][qs]
            rr, Brr = r12.next()
            k.op(k.dve, lambda e, rr=rr, a1_=a1_: e.reciprocal(out=rr[:, 0:1], in_=a1_[:, 256:257]), reads=[Ba1], writes=[Brr])
            k.op(k.dve, lambda e, rr=rr, a2_=a2_: e.reciprocal(out=rr[:, 1:2], in_=a2_[:, 256:257]), reads=[Ba2, Brr], writes=[Brr])
            k.op(k.dve, lambda e, rr=rr: e.tensor_tensor(out=rr[:, 1:2], in0=rr[:, 1:2], in1=neglam, op=ALU.mult), reads=[Brr, Bnl], writes=[Brr])
            o1, Bo1 = o1r.next()
            k.op(k.act, lambda e, o1=o1, a1_=a1_, rr=rr: e.activation(out=o1, in_=a1_[:, 0:256], func=AF.Copy, scale=rr[:, 0:1]),
                 reads=[Ba1, Brr], writes=[Bo1])
            oo, Boo = orr.next()
            k.op(k.dve, lambda e, oo=oo, a2_=a2_, rr=rr, o1=o1: e.scalar_tensor_tensor(
                out=oo, in0=a2_[:, 0:256], scalar=rr[:, 1:2], in1=o1, op0=ALU.mult, op1=ALU.add), reads=[Ba2, Brr, Bo1], writes=[Boo])
            sq, Bsq = sqr.next()
            ss, Bss = ssr.next()
            k.op(k.act, lambda e, sq=sq, oo=oo, ss=ss: e.activation(out=sq, in_=oo, func=AF.Square, accum_out=ss), reads=[Boo], writes=[Bsq, Bss])
            sd, Bsd = sdr.next()
            k.op(k.act, lambda e, ss=ss, sd=sd: e.activation(out=sd[:, 0:1], in_=ss, func=AF.Sqrt, bias=epsb, scale=1.0 / 256), reads=[Bss, Beps], writes=[Bsd])
            k.op(k.dve, lambda e, sd=sd: e.reciprocal(out=sd[:, 1:2], in_=sd[:, 0:1]), reads=[Bsd], writes=[Bsd])
            y, By = yr.next()
            k.op(k.dve, lambda e, y=y, oo=oo, sd=sd: e.scalar_tensor_tensor(out=y, in0=oo, scalar=sd[:, 1:2], in1=gss, op0=ALU.mult, op1=ALU.mult),
                 reads=[Boo, Bsd, Bgss], writes=[By])
            ys.append((y, By))

        def back():
            ot, Bot = otr.next()
            for qs in range(2):
                y, By = ys[qs]
                for f in range(2):
                    k.op(k.pe, lambda e, y=y, f=f: e.transpose(out=ps_t[0][:, f * 128:(f + 1) * 128], in_=y[:, f * 128:(f + 1) * 128], identity=ident),
                         reads=[By, Bid], writes=[ps_t[1]])
                k.op(k.dve, lambda e, ot=ot, qs=qs: e.tensor_copy(out=ot[:, :, qs * 128:(qs + 1) * 128],
                                                               in_=ps_t[0][:, 0:256].rearrange("p (f q) -> p f q", f=2)),
                     reads=[ps_t[1]], writes=[Bot])
            for f in range(2):
                k.dma(k.sp, outd[f, :, q0:q0 + 256], ot[:, f, :], reads=[Bot], is_output=True)
        return back

    for si in range(min(LOOK, len(steps))):
        issue_S(si)
    pending = None
    for si, (bi, q0, ki, kt, n) in enumerate(steps):
        if si + LOOK < len(steps):
            issue_S(si + LOOK)
        pt, Bpt = sbuf_of.pop(si)
        for m in range(2):
            for qs in range(2):
                a_, Ba = acc[m][qs]
                k.op(k.pe, lambda e, a_=a_, pt=pt, m=m, qs=qs, kt=kt, ki=ki, n=n: e.matmul(
                    a_[:, 0:257], lhsT=pt[:, m, qs * 128:(qs + 1) * 128], rhs=va[:, kt, :],
                    start=(ki == 0), stop=(ki == n - 1)), reads=[Bpt, Bv], writes=[Ba])
        if pending is not None and (ki == 6 or ki == n - 1):
            pending()
            pending = None
        if ki == n - 1:
            pending = epilogue_front(q0)
    if pending is not None:
        pending()
    k.finish()
    return nc


def to_global_fm(arrs):
    return np.concatenate([a[..., :TC] for a in arrs] + [a[..., TC:] for a in arrs], axis=-1)


def to_global_tm(arrs):
    return np.concatenate([a[:TC] for a in arrs] + [a[TC:] for a in arrs], axis=0)


def lam_init_of(l):
    return float(0.8 - 0.6 * np.exp(-0.3 * l))


def run_da(p1, inp, l):
    qk = to_global_fm([np.asarray(r["o_qk"]) for r in p1])
    v = to_global_tm([np.asarray(r["o_v"]) for r in p1])
    nc = build_da()
    li = lam_init_of(l)
    lamv = np.ascontiguousarray(np.stack([inp["lambda_q1"][l], inp["lambda_k1"][l], inp["lambda_q2"][l], inp["lambda_k2"][l]], -1))
    lamc = np.ascontiguousarray(np.broadcast_to(np.array([li, 1.0 - li], np.float32), (128, 2)))
    gs = np.ascontiguousarray(np.broadcast_to(inp["da_subln_g"][l], (128, 256)))
    ones = np.ones((128, 128), np.float32)
    ident = np.eye(128, dtype=np.float32)
    onecol = np.ones((NTOK, 1), dtype=v.dtype)
    in_maps = []
    for h in range(NCORES):
        in_maps.append({"qT": np.ascontiguousarray(qk[2 * h:2 * h + 2]), "kT": np.ascontiguousarray(qk[16 + 2 * h:16 + 2 * h + 2]),
                        "vaug": np.ascontiguousarray(np.concatenate([v[:, 256 * h:256 * (h + 1)], onecol], axis=1)),
                        "lamv": lamv, "lamc": lamc, "gs": gs, "ones": ones, "ident": ident})
    res = run_bass_kernel_spmd(nc, in_maps, core_ids=list(range(NCORES)))
    return np.concatenate([np.asarray(r["daT"]).reshape(256, NTOK) for r in res.results], axis=0)


def build_ft():
    nc = bass.Bass("TRN2", target_bir_lowering=False)
    dt = nc.dram_tensor
    zTd = dt("zT", [8, 128, NTOK], BF16, kind="ExternalInput").ap()
    csd = dt("cs", [2, 128, 512], BF16, kind="ExternalInput").ap()
    tabd = dt("tab", [2, SEQ, TL], BF16, kind="ExternalInput").ap()
    ctabd = dt("ctab", [2, CTX, TC], BF16, kind="ExternalInput").ap()
    outd = dt("ftT", [8, 128, T1], BF16, kind="ExternalOutput").ap()
    k = K(nc)
    cs = k.sb([128, 2, 512], BF16); Bcs = Buf()
    for kc in range(2):
        k.dma(k.sp, cs[:, kc, :], csd[kc], writes=[Bcs])
    ctab = k.sb([128, 2, 2, TC], BF16); Bct = Buf()
    for c in range(2):
        k.dma(k.sp, ctab[:, c, :, :], ctabd[c].rearrange("(st p) t -> p st t", p=128), writes=[Bct])
    zr = Ring([(k.sb([128, 2, NTOK], BF16), Buf()) for _ in range(2)])
    AB = k.sb([128, NKT, 512], BF16); BAB = Buf()
    psa = Ring([(k.ps([128, 512]), Buf()) for _ in range(3)])
    acc = [[(k.ps([128, 512]), Buf()) for _ in range(2)] for _ in range(2)]
    psc = (k.ps([128, 512]), Buf())
    tr = Ring([(k.sb([128, 2, 2, TL], BF16), Buf()) for _ in range(3)])
    otr = Ring([(k.sb([128, T1], BF16), Buf()) for _ in range(3)])
    ne = 0
    for g in range(4):
        zt, Bz = zr.next()
        for kc in range(2):
            k.dma(k.sp, zt[:, kc, :], zTd[2 * g + kc], writes=[Bz])
        for tt in range(NKT):
            pa, Bpa = psa.next()
            for kc in range(2):
                k.op(k.pe, lambda e, pa=pa, zt=zt, kc=kc, tt=tt: e.matmul(pa, lhsT=zt[:, kc, tt * 128:(tt + 1) * 128], rhs=cs[:, kc, :],
                                                                          start=(kc == 0), stop=(kc == 1)), reads=[Bz, Bcs], writes=[Bpa])
            ne += 1
            if ne % 2:
                k.op(k.act, lambda e, pa=pa, tt=tt: e.activation(out=AB[:, tt, :], in_=pa, func=AF.Copy), reads=[Bpa], writes=[BAB])
            else:
                k.op(k.dve, lambda e, pa=pa, tt=tt: e.tensor_copy(out=AB[:, tt, :], in_=pa), reads=[Bpa], writes=[BAB])
        for st2 in range(0, SEQ // 128, 2):
            tb, Btb = tr.next()
            for c in range(2):
                k.dma(k.sp if c == 0 else k.act, tb[:, c, :, :],
                      tabd[c, st2 * 128:(st2 + 2) * 128, :].rearrange("(s p) t -> p s t", p=128), writes=[Btb])
            for sj in range(2):
                st = st2 + sj
                for c in range(2):
                    for ct in range(2):
                        for nb in range(2):
                            a_, Ba = acc[ct][nb]
                            k.op(k.pe, lambda e, a_=a_, st=st, sj=sj, c=c, ct=ct, nb=nb, tb=tb: e.matmul(
                                a_, lhsT=AB[:, 2 + st, c * 256 + ct * 128:c * 256 + (ct + 1) * 128], rhs=tb[:, c, sj, nb * 512:(nb + 1) * 512],
                                start=(st == 0 and c == 0), stop=(st == SEQ // 128 - 1 and c == 1)), reads=[BAB, Btb], writes=[Ba])
        for ct in range(2):
            ot, Bo = otr.next()
            n = 0
            for st in range(2):
                for c in range(2):
                    k.op(k.pe, lambda e, st=st, c=c, ct=ct, n=n: e.matmul(
                        psc[0][:, 0:TC], lhsT=AB[:, st, c * 256 + ct * 128:c * 256 + (ct + 1) * 128], rhs=ctab[:, c, st, :],
                        start=(n == 0), stop=(n == 3)), reads=[BAB, Bct], writes=[psc[1]])
                    n += 1
            k.op(k.dve, lambda e, ot=ot: e.tensor_copy(out=ot[:, 0:TC], in_=psc[0][:, 0:TC]), reads=[psc[1]], writes=[Bo])
            for nb in range(2):
                a_, Ba = acc[ct][nb]
                if nb == 0:
                    k.op(k.act, lambda e, ot=ot, a_=a_, nb=nb: e.activation(out=ot[:, TC + nb * 512:TC + (nb + 1) * 512], in_=a_, func=AF.Copy),
                         reads=[Ba], writes=[Bo])
                else:
                    k.op(k.dve, lambda e, ot=ot, a_=a_, nb=nb: e.tensor_copy(out=ot[:, TC + nb * 512:TC + (nb + 1) * 512], in_=a_),
                         reads=[Ba], writes=[Bo])
            k.dma(k.sp, outd[2 * g + ct], ot, reads=[Bo], is_output=True)
    k.finish()
    return nc


_FT_TABLES = {}


def ft_tables():
    if not _FT_TABLES:
        bf = ml_dtypes.bfloat16
        c = np.arange(256)
        ang = 2 * np.pi * ((c[:, None] * c[None, :]) % 256) / 256.0
        cs = np.concatenate([np.cos(ang) / 16.0, -np.sin(ang) / 16.0], axis=1)
        _FT_TABLES["cs"] = np.ascontiguousarray(cs.reshape(2, 128, 512)).astype(bf)
        _FT_TABLES["ctab"] = np.stack([np.cos(ang) / 16.0, np.sin(ang) / 16.0]).astype(np.float32)
        s = np.arange(SEQ, dtype=np.int64)
        tabs = []
        for i in range(NCORES):
            t = np.arange(i * TL, (i + 1) * TL, dtype=np.int64)
            a = (2 * np.pi / SEQ) * ((s[:, None] * t[None, :]) % SEQ).astype(np.float64)
            sc = 1.0 / np.sqrt(SEQ)
            tabs.append(np.stack([(np.cos(a) * sc).astype(np.float32).astype(bf), (np.sin(a) * sc).astype(np.float32).astype(bf)]))
        _FT_TABLES["tab"] = tabs
    return _FT_TABLES


def run_ft(p1):
    zT = to_global_fm([np.asarray(r["o_z"]) for r in p1])
    T = ft_tables()
    nc = build_ft()
    bf = ml_dtypes.bfloat16
    in_maps = []
    for i in range(NCORES):
        in_maps.append({"zT": zT, "cs": T["cs"], "tab": T["tab"][i],
                        "ctab": np.ascontiguousarray(T["ctab"][:, :, i * TC:(i + 1) * TC]).astype(bf)})
    res = run_bass_kernel_spmd(nc, in_maps, core_ids=list(range(NCORES)))
    return [np.asarray(r["ftT"]).reshape(1024, T1) for r in res.results]


def build_gla():
    nc = bass.Bass("TRN2", target_bir_lowering=False)
    dt = nc.dram_tensor
    qTd = dt("qT", [128, NTOK], F32, kind="ExternalInput").ap()
    kTd = dt("kT", [128, NTOK], F32, kind="ExternalInput").ap()
    ktd = dt("ktok", [NTOK, 128], F32, kind="ExternalInput").ap()
    vd = dt("v", [NTOK, 256], BF16, kind="ExternalInput").ap()
    gdd = dt("gda", [17, NTOK], F32, kind="ExternalInput").ap()
    w2d = dt("w2a", [17, 128], F32, kind="ExternalInput").ap()
    triId = dt("triI", [128, 128], F32, kind="ExternalInput").ap()
    triAd = dt("triA", [128, 128], F32, kind="ExternalInput").ap()
    maskd = dt("maskT", [128, 128], F32, kind="ExternalInput").ap()
    outd = dt("o", [NTOK, 256], F32, kind="ExternalOutput").ap()
    k = K(nc)
    qT = k.sb([128, NTOK], F32); Bq = Buf()
    kT = k.sb([128, NTOK], F32); Bk = Buf()
    kt = k.sb([128, NKT, 128], F32); Bkt = Buf()
    v = k.sb([128, NKT, 256], BF16); Bv = Buf()
    gda = k.sb([17, NTOK], F32); Bgd = Buf()
    w2a = k.sb([17, 128], F32); Bw2 = Buf()
    triI = k.sb([128, 128], F32); BtI = Buf()
    triA = k.sb([128, 128], F32); BtA = Buf()
    mask = k.sb([128, 128], F32); Bm = Buf()
    k.dma(k.sp, gda, gdd, writes=[Bgd])
    for (d_, s_, b_) in [(w2a, w2d, Bw2), (triI, triId, BtI), (triA, triAd, BtA), (mask, maskd, Bm)]:
        k.dma(k.sp, d_, s_, writes=[b_])
    k.dma(k.sp, qT, qTd, writes=[Bq])
    k.dma(k.act, kT, kTd, writes=[Bk])
    for g in range(0, NKT, 22):
        k.dma(k.sp, kt[:, g:g + 22, :], ktd[g * 128:(g + 22) * 128, :].rearrange("(c p) d -> p c d", p=128), writes=[Bkt])
        k.dma(k.act, v[:, g:g + 22, :], vd[g * 128:(g + 22) * 128, :].rearrange("(c p) d -> p c d", p=128), writes=[Bv])
    oneb = k.sb([128, 1], F32); B1 = Buf()
    k.op(k.dve, lambda e: e.memset(oneb, 1.0), writes=[B1])
    S = [(k.sb([128, 256], F32), Buf()) for _ in range(2)]
    Sb = [(k.sb([128, 256], BF16), Buf()) for _ in range(2)]
    k.op(k.dve, lambda e: e.memset(S[0][0], 0.0), writes=[S[0][1]])
    k.op(k.dve, lambda e: e.memset(Sb[0][0], 0.0), writes=[Sb[0][1]])
    p_lg = (k.ps([128, 512]), Buf()); p_cT = (k.ps([128, 512]), Buf()); p_rs = (k.ps([128, 512]), Buf())
    p_AT = (k.ps([128, 512]), Buf()); p_o = Ring([(k.ps([128, 512]), Buf()) for _ in range(2)]); p_kv = (k.ps([128, 512]), Buf())

    def R2(shape, dt_):
        return Ring([(k.sb(shape, dt_), Buf()) for _ in range(2)])
    r_e, r_la, r_E1, r_E2, r_E3 = (R2([128, 128], F32) for _ in range(5))
    r_qi, r_ki, r_ko, r_Am = (R2([128, 128], BF16) for _ in range(4))
    r_o = R2([128, 256], F32)
    sc = 128.0 ** -0.5
    for c in range(NKT):
        cs_ = slice(c * 128, (c + 1) * 128)
        k.op(k.pe, lambda e, cs_=cs_: e.matmul(p_lg[0][:, 0:128], lhsT=gda[:, cs_], rhs=w2a, start=True, stop=True),
             reads=[Bgd, Bw2], writes=[p_lg[1]])
        e_, Be = r_e.next()
        k.op(k.act, lambda e, e_=e_: e.activation(out=e_, in_=p_lg[0][:, 0:128], func=AF.Exp, scale=-1.0), reads=[p_lg[1]], writes=[Be])
        la, Bla = r_la.next()
        k.op(k.act, lambda e, e_=e_, la=la: e.activation(out=la, in_=e_, func=AF.Ln, bias=oneb, scale=1.0), reads=[Be, B1], writes=[Bla])
        k.op(k.pe, lambda e, la=la: e.matmul(p_cT[0][:, 0:128], lhsT=la, rhs=triI, start=True, stop=True), reads=[Bla, BtI], writes=[p_cT[1]])
        k.op(k.pe, lambda e, la=la: e.matmul(p_rs[0][:, 0:128], lhsT=triA, rhs=la, start=True, stop=True), reads=[Bla, BtA], writes=[p_rs[1]])
        E1, BE1 = r_E1.next(); E2, BE2 = r_E2.next(); E3, BE3 = r_E3.next()
        k.op(k.act, lambda e, E1=E1: e.activation(out=E1, in_=p_cT[0][:, 0:128], func=AF.Exp), reads=[p_cT[1]], writes=[BE1])
        k.op(k.act, lambda e, E2=E2: e.activation(out=E2, in_=p_cT[0][:, 0:128], func=AF.Exp, scale=-1.0), reads=[p_cT[1]], writes=[BE2])
        k.op(k.act, lambda e, E3=E3: e.activation(out=E3, in_=p_rs[0][:, 0:128], func=AF.Exp), reads=[p_rs[1]], writes=[BE3])
        qi, Bqi = r_qi.next(); ki, Bki = r_ki.next(); ko, Bko = r_ko.next()
        k.op(k.dve, lambda e, qi=qi, E1=E1, cs_=cs_: e.scalar_tensor_tensor(out=qi, in0=qT[:, cs_], scalar=sc, in1=E1, op0=ALU.mult, op1=ALU.mult),
             reads=[Bq, BE1], writes=[Bqi])
        k.op(k.pool, lambda e, ki=ki, E2=E2, cs_=cs_: e.tensor_tensor(out=ki, in0=kT[:, cs_], in1=E2, op=ALU.mult), reads=[Bk, BE2], writes=[Bki])
        k.op(k.pool, lambda e, ko=ko, E3=E3, c=c: e.tensor_tensor(out=ko, in0=kt[:, c, :], in1=E3, op=ALU.mult), reads=[Bkt, BE3], writes=[Bko])
        k.op(k.pe, lambda e, ki=ki, qi=qi: e.matmul(p_AT[0][:, 0:128], lhsT=ki, rhs=qi, start=True, stop=True), reads=[Bki, Bqi], writes=[p_AT[1]])
        Am, BAm = r_Am.next()
        k.op(k.dve, lambda e, Am=Am: e.tensor_tensor(out=Am, in0=p_AT[0][:, 0:128], in1=mask, op=ALU.mult), reads=[p_AT[1], Bm], writes=[BAm])
        po, Bpo = p_o.next()
        Scur, BScur = S[c % 2]; Snew, BSnew = S[(c + 1) % 2]
        Sbcur, BSbcur = Sb[c % 2]; Sbnew, BSbnew = Sb[(c + 1) % 2]
        k.op(k.pe, lambda e, po=po, Am=Am, c=c: e.matmul(po[:, 0:256], lhsT=Am, rhs=v[:, c, :], start=True, stop=False), reads=[BAm, Bv], writes=[Bpo])
        k.op(k.pe, lambda e, po=po, qi=qi, Sbcur=Sbcur: e.matmul(po[:, 0:256], lhsT=qi, rhs=Sbcur, start=False, stop=True), reads=[Bqi, BSbcur], writes=[Bpo])
        k.op(k.pe, lambda e, ko=ko, c=c: e.matmul(p_kv[0][:, 0:256], lhsT=ko, rhs=v[:, c, :], start=True, stop=True), reads=[Bko, Bv], writes=[p_kv[1]])
        k.op(k.dve, lambda e, Snew=Snew, Scur=Scur, E1=E1: e.scalar_tensor_tensor(out=Snew, in0=Scur, scalar=E1[:, 127:128], in1=p_kv[0][:, 0:256],
                                                                                op0=ALU.mult, op1=ALU.add), reads=[BScur, BE1, p_kv[1]], writes=[BSnew])
        k.op(k.act, lambda e, Sbnew=Sbnew, Snew=Snew: e.activation(out=Sbnew, in_=Snew, func=AF.Copy), reads=[BSnew], writes=[BSbnew])
        ot, Bot = r_o.next()
        k.op(k.act, lambda e, ot=ot, po=po: e.activation(out=ot, in_=po[:, 0:256], func=AF.Copy), reads=[Bpo], writes=[Bot])
        k.dma(k.sp, outd[cs_, :], ot, reads=[Bot], is_output=True)
    k.finish()
    return nc


def gla_perm(direction):
    if direction == 0:
        return np.arange(NTOK)
    return np.concatenate([np.arange(CTX)[::-1], CTX + np.arange(SEQ)[::-1]])


def run_gla(p1, inp, l):
    gqk = to_global_fm([np.asarray(r["o_gqk"]) for r in p1])
    gk = to_global_tm([np.asarray(r["o_gk"]) for r in p1])
    gv = to_global_tm([np.asarray(r["o_gv"]) for r in p1])
    gd = to_global_fm([np.asarray(r["o_gd"]) for r in p1])
    nc = build_gla()
    j = np.arange(128)
    triI = np.where(j[:, None] <= j[None, :], -1.0 / 16, 0.0).astype(np.float32)
    triA = np.where(j[:, None] > j[None, :], -1.0 / 16, 0.0).astype(np.float32)
    maskT = (j[:, None] <= j[None, :]).astype(np.float32)
    onesrow = np.ones((1, NTOK), np.float32)
    in_maps = []
    perms = []
    for i in range(NCORES):
        hh, d = i // 2, i % 2
        pm = gla_perm(d)
        perms.append(pm)
        w2a = np.concatenate([inp["gla_gate_w2"][l, d][:, hh * 128:(hh + 1) * 128], inp["gla_gate_b"][l, d][None, hh * 128:(hh + 1) * 128]], 0)
        in_maps.append({"qT": np.ascontiguousarray(gqk[hh][:, pm]), "kT": np.ascontiguousarray(gqk[4 + hh][:, pm]),
                        "ktok": np.ascontiguousarray(gk[pm, hh * 128:(hh + 1) * 128]),
                        "v": np.ascontiguousarray(gv[pm, hh * 256:(hh + 1) * 256]),
                        "gda": np.ascontiguousarray(np.concatenate([gd[16 * d:16 * (d + 1)][:, pm], onesrow], 0)),
                        "w2a": np.ascontiguousarray(w2a.astype(np.float32)), "triI": triI, "triA": triA, "maskT": maskT})
    res = run_bass_kernel_spmd(nc, in_maps, core_ids=list(range(NCORES)))
    of = np.zeros((NTOK, 1024), np.float32)
    ob = np.zeros((NTOK, 1024), np.float32)
    for i in range(NCORES):
        hh, d = i // 2, i % 2
        o = np.asarray(res.results[i]["o"])
        tgt = of if d == 0 else ob
        tgt[perms[i], hh * 256:(hh + 1) * 256] = o
    return of, ob


T3 = T1 + 4
BLK3 = blocks_of(T3, 3)
RNG3 = [(0, TC + 2, 1), (TC + 2, T3, 0)]
NFT = 2 * DFF // 128
NJ = DFF // 128


def k_barrier(k):
    evs = [(E.sem, E.count) for E in k.engs if E.count > 0]
    for name, (lst, _) in k.dma_pool.items():
        for sem, uses in lst:
            if uses > 0:
                evs.append((sem, 16 * uses))
    for E in k.engs:
        for ev in evs:
            if ev[0] is E.sem:
                continue
            k._wait(E, ev)


def build_p3():
    nc = bass.Bass("TRN2", target_bir_lowering=False)
    dt = nc.dram_tensor
    xTd = dt("xT", [D, T3], F32, kind="ExternalInput").ap()
    mixd = dt("mixT", [3072, T3], BF16, kind="ExternalInput").ap()
    gofd = dt("gof", [T3, 1024], F32, kind="ExternalInput").ap()
    gobd = dt("gob", [T3, 1024], F32, kind="ExternalInput").ap()
    rd = dt("r", [T3, 1024], F32, kind="ExternalInput").ap()
    modsd = dt("modsT", [128, 2, 6, KC], F32, kind="ExternalInput").ap()
    g2d = dt("g2", [128, KC], F32, kind="ExternalInput").ap()
    gngd = dt("gng", [128, 1024], F32, kind="ExternalInput").ap()
    onesd = dt("ones", [128, 128], F32, kind="ExternalInput").ap()
    identd = dt("ident", [128, 128], F32, kind="ExternalInput").ap()
    cmd = dt("cm", [128, T3], F32, kind="ExternalInput").ap()
    cwd = dt("cw", [128, 3, NFT], F32, kind="ExternalInput").ap()
    cbd = dt("cb", [128, NFT], F32, kind="ExternalInput").ap()
    wod = dt("w_out", [D, D], F32, kind="ExternalInput").ap()
    wud = dt("w_up", [D, 2 * DFF], F32, kind="ExternalInput").ap()
    wdd = dt("w_down", [DFF, D], F32, kind="ExternalInput").ap()
    xod = dt("xoT", [D, T3], F32, kind="ExternalOutput").ap()
    xnd = dt("xnT", [D, T3], F32, kind="Internal").ap()
    aTd = dt("aT", [NJ, 128, T3], BF16, kind="Internal").ap()
    xpd = dt("xpart", [D, T3], F32, kind="Internal").ap()
    Bxn_d = [Buf() for _ in range(KC)]
    BaT_d = [Buf() for _ in range(NJ)]
    k = K(nc)

    Bmix = [Buf() for _ in range(KC)]
    wflat = k.sb([128, 4 * KC * 256], BF16)
    wr = Ring([(wflat[:, i * KC * 256:(i + 1) * KC * 256].rearrange("p (k n) -> p k n", n=256), Buf()) for i in range(4)])
    mods = k.sb([128, 2, 6, KC], F32); Bmods = Buf()
    g2s = k.sb([128, KC], F32); Bg2 = Buf()
    ones = k.sb([128, 128], F32); Bones = Buf()
    ident = k.sb([128, 128], F32); Bid = Buf()
    cm = k.sb([128, T3], F32); Bcm = Buf()
    cw = k.sb([128, 3, NFT], F32); Bcw = Buf()
    cb = k.sb([128, NFT], F32); Bcb = Buf()
    for (d_, s_, b_) in [(mods, modsd, Bmods), (g2s, g2d, Bg2), (ones, onesd, Bones), (ident, identd, Bid),
                         (cm, cmd, Bcm), (cw, cwd, Bcw), (cb, cbd, Bcb)]:
        k.dma(k.sp, d_, s_, writes=[b_])
    epsb = k.sb([128, 1], F32); Beps = Buf()
    k.op(k.dve, lambda e: e.memset(epsb, EPS), writes=[Beps])
    a2 = k.sb([128, 2, KC], F32); Ba2 = Buf()
    for s in range(2):
        k.op(k.dve, lambda e, s=s: e.scalar_tensor_tensor(out=a2[:, s, :], in0=mods[:, s, 4, :], scalar=1.0, in1=g2s, op0=ALU.add, op1=ALU.mult),
             reads=[Bmods, Bg2], writes=[Ba2])
    psr = Ring([(k.ps([128, 512]), Buf()) for _ in range(8)])
    ss4 = k.sb([128, 4], F32); Bss4 = Buf()
    sd4 = k.sb([128, 4], F32); Bsd4 = Buf()
    rs4 = k.sb([128, 4], F32); Brs4 = Buf()
    a16r = Ring([(k.sb([128, T3], BF16), Buf()) for _ in range(2)])
    big_cm = nc.sbuf_tensor("bigmix", [128, KC * T3], BF16)
    big = big_cm.__enter__().ap()
    mixT = big.rearrange("p (k t) -> p k t", t=T3)
    for g in range(0, 24, 6):
        k.dma(k.sp, mixT[:, g:g + 6, :], mixd[g * 128:(g + 6) * 128, :].rearrange("(kc p) t -> p kc t", p=128), writes=Bmix[g:g + 6])
    with nc.sbuf_tensor("gA", [128, 8, 1024], F32) as gA_t:
        gA = gA_t.ap()
        gng = gA[:, 0, :]; Bgng = Buf()
        k.dma(k.sp, gng, gngd, writes=[Bgng])
        t_of, t_ob, t_r, t_o, t_sr, t_y, t_junk = (gA[:, i, :] for i in range(1, 8))
        Bof, Bob, Br_, Bo_, Bsr, By, Bj = (Buf() for _ in range(7))
        for s0 in range(0, T3, 128):
            mt = min(128, T3 - s0)
            k.dma(k.sp, t_of[0:mt], gofd[s0:s0 + mt, :], writes=[Bof])
            k.dma(k.act, t_ob[0:mt], gobd[s0:s0 + mt, :], writes=[Bob])
            k.dma(k.sp, t_r[0:mt], rd[s0:s0 + mt, :], writes=[Br_])
            k.op(k.pool, lambda e, mt=mt: e.tensor_tensor(out=t_o[0:mt], in0=t_of[0:mt], in1=t_ob[0:mt], op=ALU.add), reads=[Bof, Bob], writes=[Bo_])
            k.op(k.act, lambda e, mt=mt: e.activation(out=t_sr[0:mt], in_=t_r[0:mt], func=AF.Silu), reads=[Br_], writes=[Bsr])
            k.op(k.pool, lambda e, mt=mt: e.tensor_tensor(out=t_sr[0:mt], in0=t_sr[0:mt], in1=gng[0:mt], op=ALU.mult), reads=[Bsr, Bgng], writes=[Bsr])
            for hh in range(4):
                k.op(k.act, lambda e, mt=mt, hh=hh: e.activation(out=t_junk[0:mt, hh * 256:(hh + 1) * 256], in_=t_o[0:mt, hh * 256:(hh + 1) * 256],
                                                                 func=AF.Square, accum_out=ss4[0:mt, hh:hh + 1]), reads=[Bo_], writes=[Bj, Bss4])
            k.op(k.act, lambda e, mt=mt: e.activation(out=sd4[0:mt], in_=ss4[0:mt], func=AF.Sqrt, bias=epsb[0:mt], scale=1.0 / 256), reads=[Bss4, Beps], writes=[Bsd4])
            k.op(k.dve, lambda e, mt=mt: e.reciprocal(out=rs4[0:mt], in_=sd4[0:mt]), reads=[Bsd4], writes=[Brs4])
            for hh in range(4):
                k.op(k.dve, lambda e, mt=mt, hh=hh: e.scalar_tensor_tensor(
                    out=t_y[0:mt, hh * 256:(hh + 1) * 256], in0=t_o[0:mt, hh * 256:(hh + 1) * 256], scalar=rs4[0:mt, hh:hh + 1],
                    in1=t_sr[0:mt, hh * 256:(hh + 1) * 256], op0=ALU.mult, op1=ALU.mult), reads=[Bo_, Brs4, Bsr], writes=[By])
            for b in range(2):
                pb, Bp = psr.next()
                for f in range(4):
                    k.op(k.pe, lambda e, pb=pb, b=b, f=f, mt=mt: e.transpose(out=pb[:, f * 128:f * 128 + mt], in_=t_y[0:mt, (4 * b + f) * 128:(4 * b + f + 1) * 128],
                                                                           identity=ident[0:mt, 0:mt]), reads=[By, Bid], writes=[Bp])
                k.op(k.act if b == 0 else k.dve, (lambda e, pb=pb, b=b, s0=s0, mt=mt: e.activation(
                    out=mixT[:, 24 + 4 * b:28 + 4 * b, s0:s0 + mt], in_=pb.rearrange("p (f q) -> p f q", f=4)[:, :, 0:mt], func=AF.Copy)) if b == 0 else
                    (lambda e, pb=pb, b=b, s0=s0, mt=mt: e.tensor_copy(
                        out=mixT[:, 24 + 4 * b:28 + 4 * b, s0:s0 + mt], in_=pb.rearrange("p (f q) -> p f q", f=4)[:, :, 0:mt])),
                    reads=[Bp], writes=Bmix[24 + 4 * b:28 + 4 * b])
        k_barrier(k)

    plist = [(wod, c0) for c0 in range(0, D, 256)]
    for pp in range(NJ // 2):
        plist += [(wud, pp * 256), (wud, DFF + pp * 256)]
    ploaded = {}
    pnx = [0]

    def prefetch():
        if pnx[0] < len(plist):
            wsrc, c0 = plist[pnx[0]]
            wt, Bw = wr.next()
            k.dma(k.pool, wt, wsrc[:, c0:c0 + 256].rearrange("(kc p) n -> p kc n", p=128), writes=[Bw])
            ploaded[pnx[0]] = (wt, Bw)
            pnx[0] += 1
    pcur = [0]

    def load_panel(wsrc, c0, pw, nk):
        while pcur[0] >= pnx[0]:
            prefetch()
        res = ploaded.pop(pcur[0])
        pcur[0] += 1
        while pnx[0] < min(len(plist), pcur[0] + 2):
            prefetch()
        return res

    with nc.sbuf_tensor("tB", [128, 12, T3], F32) as tB_t:
        tB = tB_t.ap()
        xr = Ring([(tB[:, i, :], Buf()) for i in range(2)])
        xnr = Ring([(tB[:, 2 + i, :], Buf()) for i in range(2)])
        sqr = Ring([(tB[:, 4 + i, :], Buf()) for i in range(2)])
        acc = tB[:, 6, :]; Bacc = Buf()
        rstd = tB[:, 7, :]; Brstd = Buf()
        for c0 in range(0, D, 256):
            wt, Bw = load_panel(wod, c0, 256, KC)
            for mi in range(2):
                m = c0 // 128 + mi
                banks = [psr.next() for _ in BLK3]
                for kc in range(KC):
                    for bi, (s0, wd) in enumerate(BLK3):
                        k.op(k.pe, lambda e, bi=bi, s0=s0, wd=wd, kc=kc, mi=mi, wt=wt, banks=banks: e.matmul(
                            banks[bi][0][:, 0:wd], lhsT=wt[:, kc, mi * 128:(mi + 1) * 128], rhs=mixT[:, kc, s0:s0 + wd],
                            start=(kc == 0), stop=(kc == KC - 1)), reads=[Bw, Bmix[kc]], writes=[banks[bi][1]])
                xt, Bx = xr.next()
                k.dma(k.sp, xt, xTd[m * 128:(m + 1) * 128, :], writes=[Bx])
                xn, Bxn = xnr.next()
                for bi, blk in enumerate(BLK3):
                    for (a, b, s) in split_ranges(blk, RNG3):
                        k.op(k.dve, lambda e, a=a, b=b, s=s, m=m, xn=xn, xt=xt, pb=banks[bi][0], s0=blk[0]: e.scalar_tensor_tensor(
                            out=xn[:, a:b], in0=pb[:, a - s0:b - s0], scalar=mods[:, s, 2, m:m + 1], in1=xt[:, a:b], op0=ALU.mult, op1=ALU.add),
                            reads=[banks[bi][1], Bx, Bmods], writes=[Bxn])
                k.dma(k.sp, xnd[m * 128:(m + 1) * 128, :], xn, reads=[Bxn], writes=[Bxn_d[m]])
                if m == 0:
                    k.op(k.act, lambda e, xn=xn: e.activation(out=acc, in_=xn, func=AF.Square), reads=[Bxn], writes=[Bacc])
                else:
                    sq, Bsq = sqr.next()
                    k.op(k.act, lambda e, xn=xn, sq=sq: e.activation(out=sq, in_=xn, func=AF.Square), reads=[Bxn], writes=[Bsq])
                    k.op(k.dve, lambda e, sq=sq: e.tensor_tensor(out=acc, in0=acc, in1=sq, op=ALU.add), reads=[Bsq, Bacc], writes=[Bacc])
        banks = [psr.next() for _ in BLK3]
        for bi, (s0, wd) in enumerate(BLK3):
            k.op(k.pe, lambda e, bi=bi, s0=s0, wd=wd: e.matmul(banks[bi][0][:, 0:wd], lhsT=ones, rhs=acc[:, s0:s0 + wd], start=True, stop=True),
                 reads=[Bones, Bacc], writes=[banks[bi][1]])
            k.op(k.act, lambda e, bi=bi, s0=s0, wd=wd: e.activation(out=rstd[:, s0:s0 + wd], in_=banks[bi][0][:, 0:wd], func=AF.Sqrt, bias=epsb, scale=1.0 / D),
                 reads=[banks[bi][1], Beps], writes=[Brstd])
        rst2 = tB[:, 8, :]; Brst2 = Buf()
        k.op(k.dve, lambda e: e.reciprocal(out=rst2, in_=rstd), reads=[Brstd], writes=[Brst2])
        for m in range(KC):
            xn, Bxn = xnr.next()
            k.dma(k.sp, xn, xnd[m * 128:(m + 1) * 128, :], reads=[Bxn_d[m]], writes=[Bxn])
            tmp, Bt = sqr.next()
            k.op(k.dve, lambda e, tmp=tmp, xn=xn: e.tensor_tensor(out=tmp, in0=xn, in1=rst2, op=ALU.mult), reads=[Bxn, Brst2], writes=[Bt])
            for (a, b, s) in RNG3:
                k.op(k.dve, lambda e, a=a, b=b, s=s, m=m, tmp=tmp: e.tensor_scalar(
                    out=tmp[:, a:b], in0=tmp[:, a:b], scalar1=a2[:, s, m:m + 1], scalar2=mods[:, s, 3, m:m + 1], op0=ALU.mult, op1=ALU.add),
                    reads=[Bt, Ba2, Bmods], writes=[Bt])
            k.op(k.pool, lambda e, m=m, tmp=tmp: e.tensor_tensor(out=mixT[:, m, :], in0=tmp, in1=cm, op=ALU.mult), reads=[Bt, Bcm], writes=[Bmix[m]])
        k_barrier(k)
        ugr = Ring([(tB[:, i, :], Buf()) for i in (0, 1)])
        uvr = Ring([(tB[:, i, :], Buf()) for i in (2, 3)])
        cgr = Ring([(tB[:, i, :], Buf()) for i in (4, 5)])
        cvr = Ring([(tB[:, i, :], Buf()) for i in (6, 7)])
        sgr = Ring([(tB[:, i, :], Buf()) for i in (8, 9)])
        for rg_ in (cgr, cvr):
            for (c_, Bc) in rg_.items:
                k.op(k.pool, lambda e, c_=c_: e.memset(c_, 0.0), writes=[Bc])
        for pp in range(NJ // 2):
            wg, Bwg = load_panel(wud, pp * 256, 256, KC)
            wv, Bwv = load_panel(wud, DFF + pp * 256, 256, KC)
            for mi in range(2):
                j = 2 * pp + mi
                cs_ = []
                for (wt, Bw, ur, cr, ti) in [(wg, Bwg, ugr, cgr, j), (wv, Bwv, uvr, cvr, NJ + j)]:
                    banks = [psr.next() for _ in BLK3]
                    for kc in range(KC):
                        for bi, (s0, wd) in enumerate(BLK3):
                            k.op(k.pe, lambda e, bi=bi, s0=s0, wd=wd, kc=kc, mi=mi, wt=wt, banks=banks: e.matmul(
                                banks[bi][0][:, 0:wd], lhsT=wt[:, kc, mi * 128:(mi + 1) * 128], rhs=mixT[:, kc, s0:s0 + wd],
                                start=(kc == 0), stop=(kc == KC - 1)), reads=[Bw, Bmix[kc]], writes=[banks[bi][1]])
                    u, Bu = ur.next()
                    for bi, (s0, wd) in enumerate(BLK3):
                        k.op(k.act, lambda e, u=u, pb=banks[bi][0], s0=s0, wd=wd: e.activation(out=u[:, s0:s0 + wd], in_=pb[:, 0:wd], func=AF.Copy),
                             reads=[banks[bi][1]], writes=[Bu])
                    c_, Bc = cr.next()
                    for (a, b, s) in RNG3:
                        k.op(k.act, lambda e, u=u, c_=c_, a=a, b=b, ti=ti: e.activation(
                            out=c_[:, a + 1:b - 1], in_=u[:, a + 1:b - 1], func=AF.Identity, scale=cw[:, 1, ti:ti + 1], bias=cb[:, ti:ti + 1]),
                            reads=[Bu, Bcw, Bcb], writes=[Bc])
                        k.op(k.dve, lambda e, u=u, c_=c_, a=a, b=b, ti=ti: e.scalar_tensor_tensor(
                            out=c_[:, a + 1:b - 1], in0=u[:, a:b - 2], scalar=cw[:, 0, ti:ti + 1], in1=c_[:, a + 1:b - 1], op0=ALU.mult, op1=ALU.add),
                            reads=[Bu, Bcw, Bc], writes=[Bc])
                        k.op(k.dve, lambda e, u=u, c_=c_, a=a, b=b, ti=ti: e.scalar_tensor_tensor(
                            out=c_[:, a + 1:b - 1], in0=u[:, a + 2:b], scalar=cw[:, 2, ti:ti + 1], in1=c_[:, a + 1:b - 1], op0=ALU.mult, op1=ALU.add),
                            reads=[Bu, Bcw, Bc], writes=[Bc])
                    cs_.append((c_, Bc))
                (cg, Bcg), (cv, Bcv) = cs_
                sg, Bsg = sgr.next()
                k.op(k.act, lambda e, sg=sg, cg=cg: e.activation(out=sg, in_=cg, func=AF.Silu), reads=[Bcg], writes=[Bsg])
                a16, Ba16 = a16r.next()
                k.op(k.dve, lambda e, a16=a16, sg=sg, cv=cv: e.tensor_tensor(out=a16, in0=sg, in1=cv, op=ALU.mult), reads=[Bsg, Bcv], writes=[Ba16])
                k.dma(k.sp, aTd[j], a16, reads=[Ba16], writes=[BaT_d[j]])
        k_barrier(k)

    big_cm.__exit__(None, None, None)
    KH = NJ // 2
    aS = k.sb([128, KH, T3], BF16)
    BaS = Buf()
    wx = k.sb([128, KH, 256], BF16)
    wr2 = Ring([(wflat[:, i * KH * 256:(i + 1) * KH * 256].rearrange("p (k n) -> p k n", n=256), Buf()) for i in range(2)] + [(wx, Buf())])
    xpr = Ring([(k.sb([128, 354], F32), Buf()) for _ in range(3)])
    xor_ = Ring([(k.sb([128, 354], F32), Buf()) for _ in range(3)])
    Bxp_d = [Buf() for _ in range(KC)]
    for h in range(2):
        k0 = h * KH
        for g in range(0, KH, 15):
            ge = min(KH, g + 15)
            k.dma(k.sp, aS[:, g:ge, :], aTd[k0 + g:k0 + ge, :, :].rearrange("k p t -> p k t"), reads=BaT_d[k0 + g:k0 + ge], writes=[BaS])
        wl = {}

        def wload(mp, k0=k0, wl=wl):
            if mp < KC // 2 and mp not in wl:
                wt, Bw = wr2.next()
                k.dma(k.pool, wt, wdd[k0 * 128:(k0 + KH) * 128, mp * 256:(mp + 1) * 256].rearrange("(kc p) n -> p kc n", p=128), writes=[Bw])
                wl[mp] = (wt, Bw)
        wload(0)
        wload(1)
        for mp in range(KC // 2):
            wt, Bw = wl.pop(mp)
            wload(mp + 2)
            for mi in range(2):
                m = 2 * mp + mi
                banks = [psr.next() for _ in BLK3]
                for kc in range(KH):
                    for bi, (s0, wd) in enumerate(BLK3):
                        k.op(k.pe, lambda e, bi=bi, s0=s0, wd=wd, kc=kc, mi=mi, wt=wt, banks=banks: e.matmul(
                            banks[bi][0][:, 0:wd], lhsT=wt[:, kc, mi * 128:(mi + 1) * 128], rhs=aS[:, kc, s0:s0 + wd],
                            start=(kc == 0), stop=(kc == KH - 1)), reads=[Bw, BaS], writes=[banks[bi][1]])
                src_d, Bsrc = (xnd, Bxn_d[m]) if h == 0 else (xpd, Bxp_d[m])
                for bi, (s0, wd) in enumerate(BLK3):
                    xp, Bxp = xpr.next()
                    k.dma(k.sp if bi != 1 else k.act, xp[:, 0:wd], src_d[m * 128:(m + 1) * 128, s0:s0 + wd], reads=[Bsrc], writes=[Bxp])
                    xo, Bxo = xor_.next()
                    for (a, b, s) in split_ranges((s0, wd), RNG3):
                        k.op(k.dve, lambda e, a=a, b=b, s=s, m=m, xo=xo, xp=xp, pb=banks[bi][0], s0=s0: e.scalar_tensor_tensor(
                            out=xo[:, a - s0:b - s0], in0=pb[:, a - s0:b - s0], scalar=mods[:, s, 5, m:m + 1], in1=xp[:, a - s0:b - s0], op0=ALU.mult, op1=ALU.add),
                            reads=[banks[bi][1], Bxp, Bmods], writes=[Bxo])
                    if h == 0:
                        k.dma(k.sp, xpd[m * 128:(m + 1) * 128, s0:s0 + wd], xo[:, 0:wd], reads=[Bxo], writes=[Bxp_d[m]])
                    else:
                        k.dma(k.sp, xod[m * 128:(m + 1) * 128, s0:s0 + wd], xo[:, 0:wd], reads=[Bxo], is_output=True)
    k_barrier(k)
    k.finish()
    return nc


def idx1_of(i):
    return np.concatenate([np.arange(TC * i, TC * (i + 1)), CTX + np.arange(TL * i, TL * (i + 1))])


def idx3_of(i):
    c = np.arange(TC * i - 1, TC * (i + 1) + 1)
    t = np.arange(TL * i - 1, TL * (i + 1) + 1)
    valid = np.concatenate([(c >= 0) & (c < CTX), (t >= 0) & (t < SEQ)])
    idx = np.concatenate([np.clip(c, 0, CTX - 1), CTX + np.clip(t, 0, SEQ - 1)])
    return idx, valid


def run_p3(xT_glob, daT, ft_cores, gof, gob, p1, modsT, inp, l):
    ftT = to_global_fm(ft_cores)
    mix_glob = np.concatenate([daT, ftT], axis=0)
    r_glob = to_global_tm([np.asarray(r["o_r"]) for r in p1])
    nc = build_p3()
    ones = np.ones((128, 128), np.float32)
    ident = np.eye(128, dtype=np.float32)
    g2 = vec_layout(inp["norm2_g"][l])
    gng = np.ascontiguousarray(np.broadcast_to(np.tile(inp["gla_norm_g"][l], 4), (128, 1024)))
    cw = np.ascontiguousarray(inp["conv_w"][l].reshape(3, NFT, 128).transpose(2, 0, 1))
    cb = vec_layout(inp["conv_b"][l])
    w_out, w_up, w_down = inp["w_out"][l], inp["w_up"][l], inp["w_down"][l]
    in_maps = []
    for i in range(NCORES):
        idx, valid = idx3_of(i)
        vm = valid.astype(np.float32)
        xT = np.ascontiguousarray(xT_glob[:, idx] * vm[None, :])
        mixT = mix_glob[:, idx].copy()
        mixT[:, ~valid] = 0
        gf = gof[idx].copy(); gf[~valid] = 0
        gb = gob[idx].copy(); gb[~valid] = 0
        rr = r_glob[idx].copy(); rr[~valid] = 0
        in_maps.append({"xT": xT, "mixT": np.ascontiguousarray(mixT), "gof": gf, "gob": gb, "r": rr, "modsT": modsT, "g2": g2,
                        "gng": gng, "ones": ones, "ident": ident, "cm": np.ascontiguousarray(np.broadcast_to(vm, (128, T3))),
                        "cw": cw, "cb": cb, "w_out": w_out, "w_up": w_up, "w_down": w_down})
    res = run_bass_kernel_spmd(nc, in_maps, core_ids=list(range(NCORES)))
    outs = [np.asarray(r["xoT"]) for r in res.results]
    ctx_part = [o[:, 1:1 + TC] for o in outs]
    lat_part = [o[:, TC + 3:TC + 3 + TL] for o in outs]
    return np.concatenate(ctx_part + lat_part, axis=1)


def kernel(**inp):
    inp = {k_: np.asarray(v_) for k_, v_ in inp.items()}
    mods = run_p0(inp)
    xT_glob = np.ascontiguousarray(np.concatenate([inp["ctx"][0], inp["x"][0]], axis=0).T)
    for l in range(DEPTH):
        modsT = mods_layout(mods[l])
        xT_cores = [np.ascontiguousarray(xT_glob[:, idx1_of(i)]) for i in range(NCORES)]
        p1 = run_p1(xT_cores, inp["w_in"][l], modsT, inp["norm1_g"][l], inp["q_norm_g"][l], inp["k_norm_g"][l])
        p1 = [{k_: np.asarray(v_) for k_, v_ in r.items()} for r in p1]
        daT = run_da(p1, inp, l)
        ft = run_ft(p1)
        gof, gob = run_gla(p1, inp, l)
        xT_glob = run_p3(xT_glob, daT, ft, gof, gob, p1, modsT, inp, l)
    out = np.ascontiguousarray(xT_glob[:, CTX:].T)[None].astype(np.float32)
    return out
```
